# Optimizing a Trainium2 kernel written in Bass

```python
import jax, jax.numpy as jnp
from jax import lax
import numpy as np

D_MODEL = 1024
BATCH = 32
SEQ = 256
DEPTH = 2
DEC_BATCH = 8
DEC_SEQ = 4096
PAST_LEN = 256

GRID_W = 64
N_EVEN = (DEPTH + 1) // 2
N_ODD = DEPTH // 2
N_SUB = 3
HEAD_DIM = 64
N_RET = 8
RET_DK = 64
RET_DV = 64
RET_CHUNK = 128
RET_GN_EPS = 1e-5
N_Q = 8
N_KV = 2
Q_GROUP = N_Q // N_KV
Q_BLOCK = 128
ROPE_THETA = 10000.0
N_NA = 16
NA_KR_MAX = 8
NA_KC = 16
D_FF = 2816
EPS = 1e-6
RET_W = N_RET * RET_DK
RET_VW = N_RET * RET_DV
GQA_W = N_Q * HEAD_DIM
KV_W = N_KV * HEAD_DIM
AB_IN = 2 * RET_W + 2 * RET_VW + GQA_W + 2 * KV_W
AB_SPLITS = (RET_W, 2 * RET_W, 2 * RET_W + RET_VW, 2 * RET_W + 2 * RET_VW,
             2 * RET_W + 2 * RET_VW + GQA_W, 2 * RET_W + 2 * RET_VW + GQA_W + KV_W)
AB_OUT = RET_VW + GQA_W
NA_W = N_NA * HEAD_DIM

kernel_name = 'hybrid_flow_retention_gqa_natten'


def rmsnorm(x, g):
    xf = x.astype(jnp.float32)
    y = xf * lax.rsqrt(jnp.mean(xf * xf, axis=-1, keepdims=True) + EPS)
    return (y * g.astype(jnp.float32)).astype(x.dtype)


def softmax_f32(s, dtype):
    return jax.nn.softmax(s.astype(jnp.float32), axis=-1).astype(dtype)


def swiglu(h, w_in, w_out):
    gate, up = jnp.split(h @ w_in, 2, axis=-1)
    return (jax.nn.silu(gate) * up) @ w_out


def adaln_sublayer(x, mod, s, g_pre, g_post, fn, weight):
    shift = mod[:, 3 * s, None]
    scale = mod[:, 3 * s + 1, None]
    gate = mod[:, 3 * s + 2, None]
    h = rmsnorm(x, g_pre) * (1 + scale) + shift
    y, aux = fn(h)
    return x + weight * gate * rmsnorm(y, g_post), aux


def axial_rope(x):
    n = x.shape[1]
    t = jnp.arange(n)
    rows, cols = t // GRID_W, t % GRID_W
    half = HEAD_DIM // 2
    quarter = half // 2
    inv = ROPE_THETA ** (-jnp.arange(quarter, dtype=jnp.float32) / quarter)

    def rot(xp, pos):
        ang = pos.astype(jnp.float32)[:, None] * inv[None, :]
        cos = jnp.cos(ang)[None, :, None, :].astype(x.dtype)
        sin = jnp.sin(ang)[None, :, None, :].astype(x.dtype)
        x1, x2 = xp[..., :quarter], xp[..., quarter:]
        return jnp.concatenate([x1 * cos - x2 * sin, x2 * cos + x1 * sin], axis=-1)

    return jnp.concatenate([rot(x[..., :half], rows), rot(x[..., half:], cols)], axis=-1)


def attend(q, k, v):
    s = jnp.einsum('bqhgd,bkhd->bhgqk', q, k) * (HEAD_DIM ** -0.5)
    p = softmax_f32(s, v.dtype)
    return jnp.einsum('bhgqk,bkhd->bqhgd', p, v)


def attend_query_blocks(q, k, v):
    b, n = q.shape[0], q.shape[1]
    nb = n // Q_BLOCK
    qb = q.reshape((b, nb, Q_BLOCK) + q.shape[2:]).swapaxes(0, 1)
    ob = lax.map(lambda qq: attend(qq, k, v), qb)
    return ob.swapaxes(0, 1).reshape(q.shape)


def retention_chunkwise(q, k, v, log_gamma, init_state, strict):
    b, nh, seq_len, dk = q.shape
    dv = v.shape[-1]
    c = RET_CHUNK
    nc = seq_len // c
    qc = q.reshape(b, nh, nc, c, dk)
    kc = k.reshape(b, nh, nc, c, dk)
    vc = v.reshape(b, nh, nc, c, dv)
    idx = jnp.arange(c, dtype=jnp.float32)
    lg = log_gamma.astype(jnp.float32)[:, None]
    diff = idx[:, None] - idx[None, :]
    mask = (diff > 0) if strict else (diff >= 0)
    dmat = jnp.where(mask[None], jnp.exp(lg[:, :, None] * jnp.maximum(diff, 0.0)[None]), 0.0).astype(q.dtype)
    q_dec = jnp.exp(lg * (idx + 1.0)).astype(q.dtype)
    k_dec = jnp.exp(lg * (c - 1.0 - idx)).astype(q.dtype)
    c_dec = jnp.exp(lg[:, 0] * c).astype(q.dtype)[None, :, None, None]
    s = jnp.einsum('bhnid,bhnjd->bhnij', qc, kc) * dmat[None, :, None]
    intra = jnp.einsum('bhnij,bhnje->bhnie', s, vc)
    kv = jnp.einsum('bhncd,bhnce,hc->nbhde', kc, vc, k_dec)

    def step(state, kv_n):
        return c_dec * state + kv_n, state

    final_state, prev_states = lax.scan(step, init_state.astype(q.dtype), kv)
    cross = jnp.einsum('bhncd,nbhde,hc->bhnce', qc, prev_states, q_dec)
    return (intra + cross).reshape(b, nh, seq_len, dv), final_state


def bidirectional_retention(q, k, v, lg_f, lg_b, s_f, s_b):
    o_f, fin_f = retention_chunkwise(q, k, v, lg_f, s_f, False)
    o_b, fin_b = retention_chunkwise(jnp.flip(q, 2), jnp.flip(k, 2), jnp.flip(v, 2), lg_b, s_b, True)
    return o_f + jnp.flip(o_b, 2), fin_f, fin_b


def head_groupnorm(o, g):
    of = o.astype(jnp.float32)
    mu = jnp.mean(of, axis=-1, keepdims=True)
    var = jnp.mean(jnp.square(of - mu), axis=-1, keepdims=True)
    y = (of - mu) * lax.rsqrt(var + RET_GN_EPS)
    b, nh, seq_len, dv = o.shape
    y = y.transpose(0, 2, 1, 3).reshape(b, seq_len, nh * dv)
    return (y * g.astype(jnp.float32)).astype(o.dtype)


def mixer_ab(h, w_in, w_out, dec_f, dec_b, gn, qn, kn, ctx):
    b, n, _ = h.shape
    rq, rk, rv, rg, gq, gk, gv = jnp.split(h @ w_in, AB_SPLITS, axis=-1)
    to_heads = lambda a, nh: a.reshape(b, n, nh, -1)
    rq_h = to_heads(rq, N_RET).transpose(0, 2, 1, 3)
    rk_h = to_heads(rk, N_RET).transpose(0, 2, 1, 3) * (RET_DK ** -0.5)
    rv_h = to_heads(rv, N_RET).transpose(0, 2, 1, 3)
    lg_f = jax.nn.log_sigmoid(dec_f.astype(jnp.float32))
    lg_b = jax.nn.log_sigmoid(dec_b.astype(jnp.float32))
    if ctx is None:
        s_f = jnp.zeros((b, N_RET, RET_DK, RET_DV), h.dtype)
        s_b = jnp.zeros((b, N_RET, RET_DK, RET_DV), h.dtype)
    else:
        s_f, s_b = ctx[2], ctx[3]
    o_r, fin_f, fin_b = bidirectional_retention(rq_h, rk_h, rv_h, lg_f, lg_b, s_f, s_b)
    y_ret = head_groupnorm(o_r, gn) * jax.nn.silu(rg)
    q = rmsnorm(to_heads(gq, N_Q), qn)
    k = rmsnorm(to_heads(gk, N_KV), kn)
    v = to_heads(gv, N_KV)
    if ctx is None:
        y_att = attend(q.reshape(b, n, N_KV, Q_GROUP, HEAD_DIM), k, v)
        aux = (k, v, fin_f, fin_b)
    else:
        q = axial_rope(q)
        k_all = jnp.concatenate([axial_rope(k), ctx[0]], axis=1)
        v_all = jnp.concatenate([v, ctx[1]], axis=1)
        y_att = attend_query_blocks(q.reshape(b, n, N_KV, Q_GROUP, HEAD_DIM), k_all, v_all)
        aux = None
    y = jnp.concatenate([y_ret, y_att.reshape(b, n, GQA_W)], axis=-1) @ w_out
    return y, aux


def neighbourhood_attention(q, k, v, k_ctx, v_ctx, rpb):
    b, n, nh, d = q.shape
    rows = n // GRID_W
    kr = min(NA_KR_MAX, rows)
    qg = q.reshape(b, rows, GRID_W, nh, d)
    kg = k.reshape(b, rows, GRID_W, nh, d)
    vg = v.reshape(b, rows, GRID_W, nh, d)
    cols = jnp.arange(GRID_W)
    c0 = jnp.clip(cols - NA_KC // 2, 0, GRID_W - NA_KC)
    col_idx = c0[:, None] + jnp.arange(NA_KC)[None, :]
    dc = col_idx - cols[:, None] + (NA_KC - 1)
    scale = HEAD_DIM ** -0.5
    rpb = rpb.astype(q.dtype)

    def row_fn(args):
        r, q_row = args
        r0 = jnp.clip(r - kr // 2, 0, rows - kr)
        k_win = lax.dynamic_slice_in_dim(kg, r0, kr, axis=1)[:, :, col_idx]
        v_win = lax.dynamic_slice_in_dim(vg, r0, kr, axis=1)[:, :, col_idx]
        dr = r0 + jnp.arange(kr) - r + (NA_KR_MAX - 1)
        bias = rpb[:, dr[None, :, None], dc[:, None, :]]
        s_win = jnp.einsum('bqhd,baqchd->bhqac', q_row, k_win) * scale + bias
        s_ctx = jnp.einsum('bqhd,bkhd->bhqk', q_row, k_ctx) * scale
        logits = jnp.concatenate([s_win.reshape(b, nh, GRID_W, kr * NA_KC), s_ctx], axis=-1)
        p = softmax_f32(logits, v.dtype)
        p_win = p[..., :kr * NA_KC].reshape(b, nh, GRID_W, kr, NA_KC)
        p_ctx = p[..., kr * NA_KC:]
        return (jnp.einsum('bhqac,baqchd->bqhd', p_win, v_win)
                + jnp.einsum('bhqk,bkhd->bqhd', p_ctx, v_ctx))

    out = lax.map(row_fn, (jnp.arange(rows), qg.transpose(1, 0, 2, 3, 4)))
    return out.transpose(1, 0, 2, 3, 4).reshape(b, n, nh, d)


def mixer_na(h, w_qkv, w_out, rpb, ctx):
    b, n, _ = h.shape
    q, k, v = [a.reshape(b, n, N_NA, HEAD_DIM) for a in jnp.split(h @ w_qkv, 3, axis=-1)]
    if ctx is None:
        y = attend(q[:, :, :, None, :], k, v)
        aux = (k, v)
    else:
        y = neighbourhood_attention(q, k, v, ctx[0], ctx[1], rpb)
        aux = None
    return y.reshape(b, n, NA_W) @ w_out, aux


def setup_inputs(seed: int = 0) -> dict:
    key = jax.random.key(seed)
    ks = jax.random.split(key, 26)

    def nrm(i, shape, scale):
        return jax.random.normal(ks[i], shape, jnp.float32) * scale

    D = D_MODEL
    decay_logit = jnp.log(2.0 ** (5.0 + jnp.arange(N_RET, dtype=jnp.float32)) - 1.0)
    return {
        'x_prompt': nrm(0, (BATCH, SEQ, D), 1.0),
        'x_sample': nrm(1, (DEC_BATCH, DEC_SEQ, D), 1.0),
        'cache_gqa_k': nrm(2, (DEC_BATCH, N_EVEN, PAST_LEN, N_KV, HEAD_DIM), 1.0),
        'cache_gqa_v': nrm(3, (DEC_BATCH, N_EVEN, PAST_LEN, N_KV, HEAD_DIM), 1.0),
        'state_ret_fwd': nrm(4, (DEC_BATCH, N_EVEN, N_RET, RET_DK, RET_DV), 0.5),
        'state_ret_bwd': nrm(5, (DEC_BATCH, N_EVEN, N_RET, RET_DK, RET_DV), 0.5),
        'cache_na_k': nrm(6, (DEC_BATCH, N_ODD, PAST_LEN, N_NA, HEAD_DIM), 1.0),
        'cache_na_v': nrm(7, (DEC_BATCH, N_ODD, PAST_LEN, N_NA, HEAD_DIM), 1.0),
        'c': nrm(8, (DEC_BATCH, D), 1.0),
        'c_ctx': nrm(9, (D,), 1.0),
        'mod_w': nrm(10, (DEPTH, D, 3 * N_SUB * D), 0.5 * D ** -0.5),
        'mod_b': nrm(11, (DEPTH, 3 * N_SUB * D), 0.02),
        'norm_pre': 1.0 + nrm(12, (DEPTH, N_SUB, D), 0.02),
        'norm_post': 1.0 + nrm(13, (DEPTH, N_SUB, D), 0.02),
        'ffn_w_in': nrm(14, (DEPTH, 2, D, 2 * D_FF), D ** -0.5),
        'ffn_w_out': nrm(15, (DEPTH, 2, D_FF, D), D_FF ** -0.5),
        'ab_w_in': nrm(16, (N_EVEN, D, AB_IN), D ** -0.5),
        'ab_w_out': nrm(17, (N_EVEN, AB_OUT, D), AB_OUT ** -0.5),
        'ret_decay_fwd': decay_logit[None, :] + nrm(18, (N_EVEN, N_RET), 0.1),
        'ret_decay_bwd': decay_logit[None, :] + nrm(19, (N_EVEN, N_RET), 0.1),
        'ret_gn': 1.0 + nrm(20, (N_EVEN, RET_VW), 0.02),
        'gqa_q_norm': 1.0 + nrm(21, (N_EVEN, HEAD_DIM), 0.02),
        'gqa_k_norm': 1.0 + nrm(22, (N_EVEN, HEAD_DIM), 0.02),
        'na_w_qkv': nrm(23, (N_ODD, D, 3 * NA_W), D ** -0.5),
        'na_w_out': nrm(24, (N_ODD, NA_W, D), NA_W ** -0.5),
        'na_rpb': nrm(25, (N_ODD, N_NA, 2 * NA_KR_MAX - 1, 2 * NA_KC - 1), 0.1),
    }


def reference(x_prompt, x_sample, cache_gqa_k, cache_gqa_v, state_ret_fwd, state_ret_bwd, cache_na_k,
              cache_na_v, c, c_ctx, mod_w, mod_b, norm_pre, norm_post, ffn_w_in, ffn_w_out, ab_w_in,
              ab_w_out, ret_decay_fwd, ret_decay_bwd, ret_gn, gqa_q_norm, gqa_k_norm, na_w_qkv, na_w_out,
              na_rpb):

    def trunk(x, cond, ctx_layers):
        aux_out = []
        for l in range(DEPTH):
            mod = (jax.nn.silu(cond) @ mod_w[l] + mod_b[l]).reshape(cond.shape[0], 3 * N_SUB, D_MODEL)
            ctx = None if ctx_layers is None else ctx_layers[l]
            i = l // 2
            ffn1 = lambda h, l=l: (swiglu(h, ffn_w_in[l, 0], ffn_w_out[l, 0]), None)
            ffn2 = lambda h, l=l: (swiglu(h, ffn_w_in[l, 1], ffn_w_out[l, 1]), None)
            if l % 2 == 0:
                mix = lambda h, i=i, ctx=ctx: mixer_ab(h, ab_w_in[i], ab_w_out[i], ret_decay_fwd[i],
                                                       ret_decay_bwd[i], ret_gn[i], gqa_q_norm[i],
                                                       gqa_k_norm[i], ctx)
            else:
                mix = lambda h, i=i, ctx=ctx: mixer_na(h, na_w_qkv[i], na_w_out[i], na_rpb[i], ctx)
            x, _ = adaln_sublayer(x, mod, 0, norm_pre[l, 0], norm_post[l, 0], ffn1, 0.5)
            x, aux = adaln_sublayer(x, mod, 1, norm_pre[l, 1], norm_post[l, 1], mix, 1.0)
            x, _ = adaln_sublayer(x, mod, 2, norm_pre[l, 2], norm_post[l, 2], ffn2, 0.5)
            aux_out.append(aux)
        return x, aux_out

    y_prompt, ctx_aux = trunk(x_prompt, c_ctx[None, :], None)

    ctx_layers = []
    for l in range(DEPTH):
        i = l // 2
        if l % 2 == 0:
            ctx_layers.append((cache_gqa_k[:, i], cache_gqa_v[:, i], state_ret_fwd[:, i], state_ret_bwd[:, i]))
        else:
            ctx_layers.append((cache_na_k[:, i], cache_na_v[:, i]))
    y_sample, _ = trunk(x_sample, c, ctx_layers)

    new_gqa_k = jnp.stack([ctx_aux[l][0] for l in range(0, DEPTH, 2)], axis=1)
    new_gqa_v = jnp.stack([ctx_aux[l][1] for l in range(0, DEPTH, 2)], axis=1)
    new_ret_fwd = jnp.stack([ctx_aux[l][2] for l in range(0, DEPTH, 2)], axis=1)
    new_ret_bwd = jnp.stack([ctx_aux[l][3] for l in range(0, DEPTH, 2)], axis=1)
    new_na_k = jnp.stack([ctx_aux[l][0] for l in range(1, DEPTH, 2)], axis=1)
    new_na_v = jnp.stack([ctx_aux[l][1] for l in range(1, DEPTH, 2)], axis=1)
    return (y_prompt, y_sample, new_gqa_k, new_gqa_v, new_ret_fwd, new_ret_bwd, new_na_k, new_na_v)
```

```python
import numpy as np
from contextlib import ExitStack
import concourse.bass as bass
import concourse.mybir as mybir
from concourse.bass_utils import run_bass_kernel_spmd
import ml_dtypes

F32, BF16 = mybir.dt.float32, mybir.dt.bfloat16
AF = mybir.ActivationFunctionType
ALU = mybir.AluOpType
AX = mybir.AxisListType

D = 1024
DFF = 2816
NJ = DFF // 128
TS = 1024
SUB = 512
NS = TS // SUB
NT = 5
NTOK = NT * TS
EPS = 1e-6


class P:
    def __init__(self, nc, es):
        self.nc = nc
        self.E = {'pe': nc.tensor, 'act': nc.scalar, 'dve': nc.vector, 'pool': nc.gpsimd, 'sp': nc.sync}
        self.sems = {}
        self.cnt = {}
        for k in self.E:
            self.sems[k] = es.enter_context(nc.semaphore("s_" + k))
            self.cnt[k] = 0
        self.NDS = 12
        for q in ('pool', 'sp'):
            for i in range(self.NDS):
                key = ('d', q, i)
                self.sems[key] = es.enter_context(nc.semaphore(f"d_{q}{i}"))
                self.cnt[key] = 0
        self.dq = {'pool': 0, 'sp': 0}
        self.waited = {}
        self.W = {}
        self.R = {}
        self.bank_i = 0
        self.banks = []

    def _deps(self, e, r, w, is_dma):
        deps = {}

        def add(sk, v, raw):
            if sk == e and not is_dma and e == 'pe':
                return
            if deps.get(sk, 0) < v:
                deps[sk] = v
        for k in r:
            for sk, v in self.W.get(k, {}).items():
                add(sk, v, True)
        for k in w:
            for sk, v in self.W.get(k, {}).items():
                add(sk, v, False)
            for sk, v in self.R.get(k, {}).items():
                add(sk, v, False)
        return deps

    def _wait(self, e, deps):
        for sk, v in deps.items():
            if self.waited.get((e, sk), 0) >= v:
                continue
            self.E[e].wait_ge(self.sems[sk], v)
            self.waited[(e, sk)] = v

    def _reg(self, ev, r, w):
        sk, v = ev
        for k in r:
            self.R.setdefault(k, {})[sk] = v
        for k in w:
            self.W[k] = {sk: v}
            self.R[k] = {}

    def op(self, e, fn, r=(), w=()):
        self._wait(e, self._deps(e, r, w, False))
        ins = fn(self.E[e])
        ins.then_inc(self.sems[e], 1)
        self.cnt[e] += 1
        self._reg((e, self.cnt[e]), r, w)

    def mmg(self, fns, r=(), w=()):
        self._wait('pe', self._deps('pe', r, w, False))
        pe = self.E['pe']
        for f in fns[:-1]:
            f(pe)
        ins = fns[-1](pe)
        ins.then_inc(self.sems['pe'], 1)
        self.cnt['pe'] += 1
        self._reg(('pe', self.cnt['pe']), r, w)

    def dma(self, q, out, in_, r=(), w=(), **kw):
        self._wait(q, self._deps(q, r, w, True))
        i = self.dq[q] % self.NDS
        self.dq[q] += 1
        key = ('d', q, i)
        self.E[q].dma_start(out=out, in_=in_, **kw).then_inc(self.sems[key], 16)
        self.cnt[key] += 16
        self._reg((key, self.cnt[key]), r, w)

    def barrier(self):
        for e in self.E:
            for sk, v in self.cnt.items():
                if sk == e or v == 0:
                    continue
                if self.waited.get((e, sk), 0) >= v:
                    continue
                self.E[e].wait_ge(self.sems[sk], v)
                self.waited[(e, sk)] = v
        self.W = {}
        self.R = {}

    def bank(self):
        i = self.bank_i % 8
        self.bank_i += 1
        return self.banks[i], ('ps', i)


def build_program():
    nc = bass.Bass("TRN2", target_bir_lowering=False)
    dt = nc.dram_tensor
    x_in = dt("x", [NTOK, D], F32, kind="ExternalInput").ap()
    condT = dt("condT", [D, 2], F32, kind="ExternalInput").ap()
    mod_w = dt("mod_w", [2, D, 9 * D], F32, kind="ExternalInput").ap()
    mod_bT = dt("mod_bT", [128, 2, 72], F32, kind="ExternalInput").ap()
    vecs = dt("vecs", [128, 12, 8], F32, kind="ExternalInput").ap()
    ffn_w_in = dt("ffn_w_in", [2, 2, D, 2 * DFF], F32, kind="ExternalInput").ap()
    ffn_w_out = dt("ffn_w_out", [2, 2, DFF, D], F32, kind="ExternalInput").ap()
    identf = dt("identf", [128, 128], F32, kind="ExternalInput").ap()
    y_out = dt("y", [NTOK, D], F32, kind="ExternalOutput").ap()
    ab_w_in = dt("ab_w_in", [D, 2816], F32, kind="ExternalInput").ap()
    hvec = dt("hvec", [128, 8], F32, kind="ExternalInput").ap()
    ropeT = dt("ropeT", [128, 2, 4096], F32, kind="ExternalInput").ap()
    pmat = dt("pmat", [128, 128], F32, kind="ExternalInput").ap()
    rc = dt("rc", [128, 6 * 128 + 66], F32, kind="ExternalInput").ap()
    dec_in = dt("dec_in", [2, 8], F32, kind="ExternalInput").ap()
    st_f_in = dt("st_f", [8, 64, 64], F32, kind="ExternalInput").ap()
    st_b_in = dt("st_b", [8, 64, 64], F32, kind="ExternalInput").ap()
    o_gk = dt("o_gk", [1024, 128], F32, kind="ExternalOutput").ap()
    o_gv = dt("o_gv", [1024, 128], F32, kind="ExternalOutput").ap()
    o_rf = dt("o_rf", [4, 8, 64, 64], F32, kind="ExternalOutput").ap()
    o_rb = dt("o_rb", [4, 8, 64, 64], F32, kind="ExternalOutput").ap()
    ab_w_out = dt("ab_w_out", [D, D], F32, kind="ExternalInput").ap()
    na_w_qkv = dt("na_w_qkv", [D, 3 * D], F32, kind="ExternalInput").ap()
    na_w_out = dt("na_w_out", [D, D], F32, kind="ExternalInput").ap()
    cgk = dt("cgk", [256, 128], F32, kind="ExternalInput").ap()
    cgv = dt("cgv", [256, 128], F32, kind="ExternalInput").ap()
    cnk = dt("cnk", [256, 1024], F32, kind="ExternalInput").ap()
    cnv = dt("cnv", [256, 1024], F32, kind="ExternalInput").ap()
    o_nk = dt("o_nk", [1024, 1024], F32, kind="ExternalOutput").ap()
    o_nv = dt("o_nv", [1024, 1024], F32, kind="ExternalOutput").ap()
    rpb_pad = dt("rpb_pad", [64 + 16 * 15 * 31 + 64], F32, kind="ExternalInput").ap()
    nmask = dt("nmask", [128, 2, 64], F32, kind="ExternalInput").ap()
    identb = dt("identb", [128, 128], F32, kind="ExternalInput").ap()
    FM1 = dt("FM1", [16 * 128, NTOK + 256], BF16).ap()
    TM1 = dt("TM1", [NTOK, 16, 128], BF16).ap()
    XS = dt("XS", [D, NTOK], F32).ap()
    FM0 = dt("FM0", [17 * 128, NTOK + 256], BF16).ap()
    TM0 = dt("TM0", [NTOK, 1152], BF16).ap()
    YC = dt("YC", [D, NTOK], BF16).ap()

    with ExitStack() as es:
        p = P(nc, es)
        sb = lambda name, shape, dtype: es.enter_context(nc.sbuf_tensor(name, shape, dtype))
        psum_all = es.enter_context(nc.psum_tensor("psum_all", [128, 8, SUB], F32))
        for i in range(8):
            p.banks.append(psum_all[:, i, :])
        ident_f = sb("ident_f", [128, 128], F32)
        onesD = sb("onesD", [128, 128], BF16)
        eps_t = sb("eps_t", [128, 1], F32)
        scal = sb("scal", [128, 2 * 3 * 2 * 3, 8], F32)
        vec_sb = sb("vec_sb", [128, 12, 8], F32)
        p.dma('sp', ident_f[:], identf, w=['ident_f'])
        p.dma('sp', vec_sb[:], vecs, w=['vec_sb'])
        p.op('dve', lambda v: v.memset(onesD[:], 1.0 / D), w=['onesD'])
        p.op('dve', lambda v: v.memset(eps_t[:], EPS), w=['eps_t'])

        bones = sb("bones", [128, 128], BF16)
        pm_b = sb("pm_b", [128, 128], BF16)
        hv = sb("hv", [128, 8], F32)
        p.op('dve', lambda v: v.memset(bones[:], 0.0), w=['bones'])
        p.op('dve', lambda v: v.memset(bones[0:64, 0:64], 1.0 / 64), w=['bones'])
        p.op('dve', lambda v: v.memset(bones[64:128, 64:128], 1.0 / 64), w=['bones'])
        p.dma('pool', pm_b[:], pmat, w=['pm_b'])
        p.dma('sp', hv[:], hvec, w=['hv'])

        def sc(l, s, c, which):
            return scal[:, ((l * 3 + s) * 2 + c) * 3 + which, :]

        import os
        DBG = int(os.environ.get("KDEBUG", "9"))
        cond_f = sb("cond_f", [128, 8, 2], F32)
        cond_b = sb("cond_b", [128, 8, 2], BF16)
        modb = sb("modb", [128, 2, 72], F32)
        modsb = sb("modsb", [128, 72, 2], F32)
        p.dma('sp', cond_f[:], condT.rearrange("(k p) c -> p k c", p=128), w=['cond_f'])
        p.dma('sp', modb[:], mod_bT, w=['modb'])
        p.op('act', lambda a: a.activation(out=cond_b[:], in_=cond_f[:], func=AF.Silu), r=['cond_f'], w=['cond_b'])

        def mod_layer(l, mwbufs, ncol, bk, bkey, tag):
            jper = ncol // 128
            for g in range(72 // jper):
                buf = mwbufs[g % 2]
                p.dma('pool', buf[:, :, 0:ncol], mod_w[l, :, g * ncol:(g + 1) * ncol].rearrange("(k p) n -> p k n", p=128),
                      w=[(tag, g % 2)])
                for jj in range(jper):
                    j = g * jper + jj
                    p.mmg([(lambda t, k=k: t.matmul(bk[:, 2 * j:2 * j + 2], lhsT=buf[:, k, jj * 128:(jj + 1) * 128],
                                                    rhs=cond_b[:, k, :], start=(k == 0), stop=(k == 7))) for k in range(8)],
                          r=[(tag, g % 2), 'cond_b'], w=[bkey])
                yield
            for c in range(2):
                p.op('dve', lambda v, c=c: v.tensor_tensor(
                    out=modsb[:, :, c], in0=bk[:, 0:144].rearrange("p (j c) -> p j c", c=2)[:, :, c],
                    in1=modb[:, l, :], op=ALU.add), r=[bkey, 'modb'], w=[('modsb', c)])
            for s in range(3):
                wgt = 1.0 if s == 1 else 0.5
                for c in range(2):
                    shift = modsb[:, (3 * s) * 8:(3 * s) * 8 + 8, c]
                    scale = modsb[:, (3 * s + 1) * 8:(3 * s + 1) * 8 + 8, c]
                    gate = modsb[:, (3 * s + 2) * 8:(3 * s + 2) * 8 + 8, c]
                    gpre = vec_sb[:, l * 3 + s, :]
                    gpost = vec_sb[:, 6 + l * 3 + s, :]
                    p.op('dve', lambda v: v.scalar_tensor_tensor(
                        out=sc(l, s, c, 0), in0=scale, scalar=1.0, in1=gpre, op0=ALU.add, op1=ALU.mult),
                        r=[('modsb', c), 'vec_sb'], w=['scal'])
                    p.op('dve', lambda v: v.tensor_copy(out=sc(l, s, c, 1), in_=shift), r=[('modsb', c)], w=['scal'])
                    p.op('dve', lambda v: v.scalar_tensor_tensor(
                        out=sc(l, s, c, 2), in0=gate, scalar=wgt, in1=gpost, op0=ALU.mult, op1=ALU.mult),
                        r=[('modsb', c), 'vec_sb'], w=['scal'])
            yield

        with ExitStack() as es0:
            mw = [es0.enter_context(nc.sbuf_tensor(f"mw{i}", [128, 8, 1152], BF16)) for i in range(2)]
            bk0, bk0k = p.bank()
            for _ in mod_layer(0, mw, 1152, bk0, bk0k, 'mw'):
                pass
            p.barrier()

        def tile_scope(which):
            with ExitStack() as esT:
                sbT = lambda name, shape, dtype: esT.enter_context(nc.sbuf_tensor(name + '_' + which, shape, dtype))
                xT = sbT("xT", [128, 8, TS], F32)
                hy = sbT("hy", [128, 8, TS], F32)
                def hTv(k, u):
                    return hy[:, k, u * SUB:u * SUB + SUB // 2].bitcast(BF16)
                aT = sbT("aT", [128, NJ, TS], BF16)
                sq = sbT("sq", [128, 8, SUB], BF16)
                rstd = [sbT(f"rstd{i}", [128, SUB], F32) for i in range(2)]
                tmpf = [sbT(f"tmpf{i}", [128, SUB], F32) for i in range(2)]
                sil = [sbT(f"sil{i}", [128, SUB], F32) for i in range(2)]
                NWB = 3
                wbuf = [sbT(f"wbuf{i}", [128, 8, 2, 256], BF16) for i in range(NWB)]
                wobuf = [sbT(f"wobuf{i}", [128, NJ, 256], BF16) for i in range(2)]
                cnts = {'wb': 0, 'wo': 0, 'rs': 0, 'tf': 0, 'sl': 0}

                def rot(name, n):
                    i = cnts[name] % n
                    cnts[name] += 1
                    return i

                def rstd_of(src_fn, src_keys, nchunks, lhsT_ones, lkey):
                    for k in range(nchunks):
                        p.op('act', lambda a, k=k: a.activation(out=sq[:, k, :], in_=src_fn(k), func=AF.Square),
                             r=[src_keys[k]], w=[('sq', k)])
                    bk, bkey = p.bank()
                    p.mmg([(lambda t, k=k: t.matmul(bk[:], lhsT=lhsT_ones, rhs=sq[:, k, :], start=(k == 0),
                                                    stop=(k == nchunks - 1))) for k in range(nchunks)],
                          r=[('sq', k) for k in range(nchunks)] + [lkey], w=[bkey])
                    ri = rot('rs', 2)
                    p.op('act', lambda a: a.activation(out=rstd[ri][:], in_=bk[:], func=AF.Ln, bias=eps_t[:], scale=1.0),
                         r=[bkey, 'eps_t'], w=[('rstd', ri)])
                    p.op('act', lambda a: a.activation(out=rstd[ri][:], in_=rstd[ri][:], func=AF.Exp, scale=-0.5),
                         r=[('rstd', ri)], w=[('rstd', ri)])
                    return ri

                HPEND = {}

                def ensure_h(u):
                    if u in HPEND:
                        HPEND.pop(u)()

                PRE = {'done': None, 'next': None}

                def prenorm(l, s, c):
                    if PRE['done'] == (l, s, c):
                        PRE['done'] = None
                        return
                    for u in range(NS):
                        ensure_h(u)
                        HPEND[u] = (lambda u=u: prenorm_u(l, s, c, u))

                def early_pre0():
                    if PRE['next'] is not None:
                        l, s, c = PRE['next']
                        PRE['next'] = None
                        prenorm(l, s, c)
                        PRE['done'] = (l, s, c)
                        ensure_h(0)

                def prenorm_u(l, s, c, u):
                    if True:
                        cols = slice(u * SUB, (u + 1) * SUB)
                        ri = rstd_of(lambda k: xT[:, k, cols], [('xT', k, u) for k in range(8)], 8, onesD[:], 'onesD')
                        for k in range(8):
                            ti = rot('tf', 2)
                            p.op('dve', lambda v, k=k, ti=ti: v.scalar_tensor_tensor(
                                out=tmpf[ti][:], in0=xT[:, k, cols], scalar=sc(l, s, c, 0)[:, k:k + 1], in1=rstd[ri][:],
                                op0=ALU.mult, op1=ALU.mult), r=[('xT', k, u), ('rstd', ri), 'scal'], w=[('tmpf', ti)])
                            p.op('act', lambda a, k=k, ti=ti: a.activation(
                                out=hTv(k, u), in_=tmpf[ti][:], func=AF.Identity, bias=sc(l, s, c, 1)[:, k:k + 1], scale=1.0),
                                r=[('tmpf', ti), 'scal'], w=[('hy', u)])

                def postres(l, s, c, u):
                    cols = slice(u * SUB, (u + 1) * SUB)
                    bk, bkey = p.bank()
                    p.mmg([(lambda t, k=k: t.matmul(bk[:], lhsT=onesD[:], rhs=sq[:, k, :], start=(k == 0), stop=(k == 7)))
                           for k in range(8)], r=[('sq', k) for k in range(8)] + ['onesD'], w=[bkey])
                    ri = rot('rs', 2)
                    p.op('act', lambda a: a.activation(out=rstd[ri][:], in_=bk[:], func=AF.Ln, bias=eps_t[:], scale=1.0),
                         r=[bkey, 'eps_t'], w=[('rstd', ri)])
                    p.op('act', lambda a: a.activation(out=rstd[ri][:], in_=rstd[ri][:], func=AF.Exp, scale=-0.5),
                         r=[('rstd', ri)], w=[('rstd', ri)])
                    for k in range(8):
                        ti = rot('tf', 2)
                        p.op('dve', lambda v, k=k, ti=ti: v.tensor_tensor(out=tmpf[ti][:], in0=hy[:, k, cols], in1=rstd[ri][:],
                                                                          op=ALU.mult),
                             r=[('hy', u), ('rstd', ri)], w=[('tmpf', ti)])
                        p.op('dve', lambda v, k=k, ti=ti: v.scalar_tensor_tensor(
                            out=xT[:, k, cols], in0=tmpf[ti][:], scalar=sc(l, s, c, 2)[:, k:k + 1], in1=xT[:, k, cols],
                            op0=ALU.mult, op1=ALU.add), r=[('tmpf', ti), ('xT', k, u), 'scal'], w=[('xT', k, u)])

                def evac_y(bk, bkey, m, u):
                    cols = slice(u * SUB, (u + 1) * SUB)
                    p.op('act', lambda a: a.activation(out=sq[:, m, :], in_=bk[:], func=AF.Square), r=[bkey], w=[('sq', m)])
                    p.op('dve', lambda v: v.tensor_copy(out=hy[:, m, cols], in_=bk[:]), r=[bkey], w=[('hy', u)])

                def ffn(l, i, s, c, after_mm1=None):
                    prenorm(l, s, c)
                    w_in = ffn_w_in[l, i]
                    w_out = ffn_w_out[l, i]
                    wmap = {}

                    def load_in(jp):
                        wi = rot('wb', NWB)
                        wb = wbuf[wi]
                        for half in range(2):
                            p.dma('pool', wb[:, :, half, :],
                                  w_in[:, half * DFF + jp * 256: half * DFF + (jp + 1) * 256].rearrange("(k p) n -> p k n", p=128),
                                  w=[('wbuf', wi)])
                        wmap[jp] = wi

                    def mm1(jp, u):
                        wi = wmap[jp]
                        wb = wbuf[wi]
                        ensure_h(u)
                        for jj in range(2):
                            j = jp * 2 + jj
                            cols = slice(u * SUB, (u + 1) * SUB)
                            bg, bgk = p.bank()
                            bu, buk = p.bank()
                            p.mmg([(lambda t, k=k: t.matmul(bg[:], lhsT=wb[:, k, 0, jj * 128:(jj + 1) * 128], rhs=hTv(k, u),
                                                            start=(k == 0), stop=(k == 7))) for k in range(8)],
                                  r=[('wbuf', wi), ('hy', u)], w=[bgk])
                            p.mmg([(lambda t, k=k: t.matmul(bu[:], lhsT=wb[:, k, 1, jj * 128:(jj + 1) * 128], rhs=hTv(k, u),
                                                            start=(k == 0), stop=(k == 7))) for k in range(8)],
                                  r=[('wbuf', wi), ('hy', u)], w=[buk])
                            si = rot('sl', 2)
                            p.op('act', lambda a, si=si, bg=bg: a.activation(out=sil[si][:], in_=bg[:], func=AF.Silu),
                                 r=[bgk], w=[('sil', si)])
                            p.op('dve', lambda v, si=si, bu=bu, j=j, cols=cols: v.tensor_tensor(
                                out=aT[:, j, cols], in0=bu[:], in1=sil[si][:], op=ALU.mult),
                                r=[buk, ('sil', si)], w=[('aT', j, u)])
                    load_in(0)
                    load_in(1)
                    mm1(0, 0)
                    mm1(1, 0)
                    load_in(2)
                    mm1(0, 1)
                    mm1(1, 1)
                    for jp in range(2, NJ // 2):
                        if jp + 1 < NJ // 2:
                            load_in(jp + 1)
                        mm1(jp, 0)
                        mm1(jp, 1)
                    omap = {}

                    def load_out(mp):
                        wi = rot('wo', 2)
                        p.dma('pool', wobuf[wi][:], w_out[:, mp * 256:(mp + 1) * 256].rearrange("(j p) n -> p j n", p=128),
                              w=[('wobuf', wi)])
                        omap[mp] = wi

                    def mm2(mp, u):
                        wi = omap[mp]
                        wo = wobuf[wi]
                        for mm in range(2):
                            m = mp * 2 + mm
                            cols = slice(u * SUB, (u + 1) * SUB)
                            bk, bkey = p.bank()
                            p.mmg([(lambda t, j=j: t.matmul(bk[:], lhsT=wo[:, j, mm * 128:(mm + 1) * 128], rhs=aT[:, j, cols],
                                                            start=(j == 0), stop=(j == NJ - 1))) for j in range(NJ)],
                                  r=[('wobuf', wi)] + [('aT', j, u) for j in range(NJ)], w=[bkey])
                            evac2(bk, bkey, m, u)
                    load_out(0)
                    load_out(1)
                    mm2(0, 0)
                    mm2(0, 1)
                    load_out(2)
                    mm2(1, 0)
                    mm2(1, 1)
                    load_out(3)
                    mm2(2, 0)
                    mm2(3, 0)
                    finish(l, s, c, 0)
                    mm2(2, 1)
                    early_pre0()
                    mm2(3, 1)
                    if after_mm1 is not None:
                        after_mm1()
                    finish(l, s, c, 1)

                sq2 = [sq, sbT("sq_b", [128, 8, SUB], BF16)]

                def evac2(bk, bkey, m, u):
                    cols = slice(u * SUB, (u + 1) * SUB)
                    p.op('dve', lambda v: v.tensor_copy(out=hy[:, m, cols], in_=bk[:]), r=[bkey], w=[('hy', u), ('yT', u), 'hy2'])
                    p.op('act', lambda a: a.activation(out=sq2[u][:, m, :], in_=hy[:, m, cols], func=AF.Square),
                         r=[('hy', u)], w=[('sq', u, m)])

                def finish(l, s, c, u):
                    cols = slice(u * SUB, (u + 1) * SUB)
                    bk, bkey = p.bank()
                    p.mmg([(lambda t, k=k: t.matmul(bk[:], lhsT=onesD[:], rhs=sq2[u][:, k, :], start=(k == 0), stop=(k == 7)))
                           for k in range(8)], r=[('sq', u, k) for k in range(8)] + ['onesD'], w=[bkey])
                    ri = rot('rs', 2)
                    p.op('act', lambda a: a.activation(out=rstd[ri][:], in_=bk[:], func=AF.Ln, bias=eps_t[:], scale=1.0),
                         r=[bkey, 'eps_t'], w=[('rstd', ri)])
                    p.op('act', lambda a: a.activation(out=rstd[ri][:], in_=rstd[ri][:], func=AF.Exp, scale=-0.5),
                         r=[('rstd', ri)], w=[('rstd', ri)])
                    for k in range(8):
                        ti = rot('tf', 2)
                        p.op('dve', lambda v, k=k, ti=ti: v.tensor_tensor(out=tmpf[ti][:], in0=hy[:, k, cols], in1=rstd[ri][:],
                                                                          op=ALU.mult),
                             r=[('hy', u), ('yT', u), ('rstd', ri)], w=[('tmpf', ti)])
                        p.op('dve', lambda v, k=k, ti=ti: v.scalar_tensor_tensor(
                            out=xT[:, k, cols], in0=tmpf[ti][:], scalar=sc(l, s, c, 2)[:, k:k + 1], in1=xT[:, k, cols],
                            op0=ALU.mult, op1=ALU.add), r=[('tmpf', ti), ('xT', k, u), 'scal'], w=[('xT', k, u)])

                def load_x_tokmajor(t):
                    p.dma('sp', hy[:], x_in[t * TS:(t + 1) * TS, :].rearrange("(g p) d -> p g d", p=128),
                          w=[('hy', 0), ('hy', 1), 'hy2'])
                    for k in range(8):
                        for u in range(NS):
                            bk, bkey = p.bank()
                            p.mmg([(lambda tt, gi=gi: tt.transpose(bk[:, gi * 128:(gi + 1) * 128],
                                                                   hy[:, u * 4 + gi, k * 128:(k + 1) * 128], ident_f[:]))
                                   for gi in range(4)], r=[('hy', 0), ('hy', 1), 'ident_f'], w=[bkey])
                            eng = 'act' if (k + u) % 2 == 0 else 'dve'
                            if eng == 'act':
                                p.op('act', lambda a, k=k, u=u: a.copy(out=xT[:, k, u * SUB:(u + 1) * SUB], in_=bk[:]),
                                     r=[bkey], w=[('xT', k, u)])
                            else:
                                p.op('dve', lambda v, k=k, u=u: v.tensor_copy(out=xT[:, k, u * SUB:(u + 1) * SUB], in_=bk[:]),
                                     r=[bkey], w=[('xT', k, u)])

                def store_y_tokmajor(t):
                    for g in range(8):
                        u = g // 4
                        for half in range(2):
                            bk, bkey = p.bank()
                            p.mmg([(lambda tt, kk=kk: tt.transpose(bk[:, kk * 128:(kk + 1) * 128],
                                                                   xT[:, half * 4 + kk, g * 128:(g + 1) * 128], ident_f[:]))
                                   for kk in range(4)], r=[('xT', half * 4 + kk, u) for kk in range(4)] + ['ident_f'], w=[bkey])
                            eng = 'act' if half == 0 else 'dve'
                            if eng == 'act':
                                p.op('act', lambda a, g=g, half=half: a.copy(out=hy[:, g, half * 512:(half + 1) * 512], in_=bk[:]),
                                     r=[bkey], w=[('hy', u), 'hy2'])
                            else:
                                p.op('dve', lambda v, g=g, half=half: v.tensor_copy(out=hy[:, g, half * 512:(half + 1) * 512],
                                                                                    in_=bk[:]), r=[bkey], w=[('hy', u), 'hy2'])
                    if t + 1 < NT:
                        load_x_scratch(t + 1)
                    p.dma('sp', y_out[t * TS:(t + 1) * TS, :].rearrange("(g p) d -> p g d", p=128), hy[:],
                          r=[('hy', 0), ('hy', 1)])

                stgb = [sbT(f"stgb{i}", [128, TS], BF16) for i in range(2)]
                sqn = [sbT(f"sqn{i}", [128, SUB], BF16) for i in range(2)]
                qnb = [sbT(f"qnb{i}", [128, SUB], BF16) for i in range(2)]
                rope_sb = [sbT(f"rope_sb{i}", [128, 2, SUB], F32) for i in range(2)]
                ostg = sq2[1][:, 0:4, :].bitcast(F32).rearrange("p a (b c) -> p (a b) c", b=2)
                ostg2 = sq2[1][:, 4:8, :].bitcast(F32).rearrange("p a (b c) -> p (a b) c", b=2)
                OSTG_KEYS = [('sq', 1, m) for m in range(8)]
                tstg = aT[:, 0:9, :].rearrange("p a b -> p (a b)").rearrange("p (g n) -> p g n", g=8)
                TSTG_KEYS = [('aT', j, u) for j in range(9) for u in range(NS)]
                cnts.update({'sg': 0, 'sn': 0, 'qb': 0})

                def wflat(wb):
                    return wb[:].rearrange("p k a b -> p k (a b)")

                def proj_fm(W, col0, nch, t, evac):
                    wi = rot('wb', NWB)
                    wb = wflat(wbuf[wi])
                    p.dma('pool', wb[:, :, 0:nch * 128], W[:, col0:col0 + nch * 128].rearrange("(k p) n -> p k n", p=128),
                          w=[('wbuf', wi)])
                    if 1 in HPEND:
                        ensure_h(0)
                    for ci in range(nch):
                        for u in range(NS):
                            ensure_h(u)
                            cols = slice(u * SUB, (u + 1) * SUB)
                            bk, bkey = p.bank()
                            p.mmg([(lambda tt, k=k: tt.matmul(bk[:], lhsT=wb[:, k, ci * 128:(ci + 1) * 128], rhs=hTv(k, u),
                                                              start=(k == 0), stop=(k == 7))) for k in range(8)],
                                  r=[('wbuf', wi), ('hy', u)], w=[bkey])
                            evac(ci, u, bk, bkey)

                def proj_tm(W, col0, ncol, evac):
                    wi = rot('wb', NWB)
                    wb = wflat(wbuf[wi])
                    p.dma('pool', wb[:, :, 0:ncol], W[:, col0:col0 + ncol].rearrange("(k p) n -> p k n", p=128),
                          w=[('wbuf', wi)])
                    for g in range(8):
                        u = g // 4
                        ensure_h(u)
                        bk, bkey = p.bank()
                        p.mmg([(lambda tt, k=k: tt.matmul(bk[:, 0:ncol], lhsT=hTv(k, u)[:, (g % 4) * 128:(g % 4 + 1) * 128], rhs=wb[:, k, 0:ncol],
                                                          start=(k == 0), stop=(k == 7))) for k in range(8)],
                              r=[('wbuf', wi), ('hy', u)], w=[bkey])
                        evac(g, bk, bkey)

                def fm_store(FM, chunk, t, si):
                    p.dma('sp', FM[chunk * 128:(chunk + 1) * 128, t * TS:(t + 1) * TS], stgb[si][:], r=[('stgb', si)])

                def simple_evac(FM, chunk0, t, kind):
                    st = {}

                    def ev(ci, u, bk, bkey):
                        if u == 0:
                            st['si'] = rot('sg', 2)
                        si = st['si']
                        cols = slice(u * SUB, (u + 1) * SUB)
                        if kind == 'copy':
                            p.op('act', lambda a: a.copy(out=stgb[si][:, cols], in_=bk[:]), r=[bkey], w=[('stgb', si)])
                        elif kind == 'scale':
                            p.op('act', lambda a: a.mul(out=stgb[si][:, cols], in_=bk[:], mul=0.125), r=[bkey], w=[('stgb', si)])
                        elif kind == 'silu':
                            p.op('act', lambda a: a.activation(out=stgb[si][:, cols], in_=bk[:], func=AF.Silu), r=[bkey],
                                 w=[('stgb', si)])
                        if u == NS - 1:
                            fm_store(FM, chunk0 + ci, t, si)
                    return ev

                def normrope_evac(FM, chunk0, t, c, wcol, outk):
                    st = {}

                    def ev(ci, u, bk, bkey):
                        if u == 0:
                            st['si'] = rot('sg', 2)
                        si = st['si']
                        cols = slice(u * SUB, (u + 1) * SUB)
                        t0 = rot('tf', 2)
                        p.op('act', lambda a: a.copy(out=tmpf[t0][:], in_=bk[:]), r=[bkey], w=[('tmpf', t0)])
                        sn = rot('sn', 2)
                        p.op('act', lambda a: a.activation(out=sqn[sn][:], in_=tmpf[t0][:], func=AF.Square),
                             r=[('tmpf', t0)], w=[('sqn', sn)])
                        b2, b2k = p.bank()
                        p.mmg([lambda tt: tt.matmul(b2[:], lhsT=bones[:], rhs=sqn[sn][:], start=True, stop=True)],
                              r=[('sqn', sn), 'bones'], w=[b2k])
                        ri = rot('rs', 2)
                        p.op('act', lambda a: a.activation(out=rstd[ri][:], in_=b2[:], func=AF.Ln, bias=eps_t[:], scale=1.0),
                             r=[b2k, 'eps_t'], w=[('rstd', ri)])
                        p.op('act', lambda a: a.activation(out=rstd[ri][:], in_=rstd[ri][:], func=AF.Exp, scale=-0.5),
                             r=[('rstd', ri)], w=[('rstd', ri)])
                        p.op('dve', lambda v: v.scalar_tensor_tensor(out=tmpf[t0][:], in0=tmpf[t0][:], scalar=hv[:, wcol:wcol + 1],
                                                                     in1=rstd[ri][:], op0=ALU.mult, op1=ALU.mult),
                             r=[('tmpf', t0), ('rstd', ri), 'hv'], w=[('tmpf', t0)])
                        if outk and c == 1:
                            b3, b3k = p.bank()
                            p.mmg([(lambda tt, g=g: tt.transpose(b3[:, g * 128:(g + 1) * 128], tmpf[t0][:, g * 128:(g + 1) * 128],
                                                                 ident_f[:])) for g in range(4)],
                                  r=[('tmpf', t0), 'ident_f'], w=[b3k])
                            p.op('dve', lambda v: v.tensor_copy(out=ostg[:, u * 4:(u + 1) * 4, :],
                                                                in_=b3[:].rearrange("p (g n) -> p g n", g=4)),
                                 r=[b3k], w=OSTG_KEYS)
                            if u == NS - 1:
                                p.dma('sp', o_gk.rearrange("(g p) n -> p g n", p=128), ostg, r=OSTG_KEYS)
                        if c == 0:
                            qi = rot('qb', 2)
                            p.op('act', lambda a: a.copy(out=qnb[qi][:], in_=tmpf[t0][:]), r=[('tmpf', t0)], w=[('qnb', qi)])
                            b3, b3k = p.bank()
                            p.mmg([lambda tt: tt.matmul(b3[:], lhsT=pm_b[:], rhs=qnb[qi][:], start=True, stop=True)],
                                  r=[('qnb', qi), 'pm_b'], w=[b3k])
                            t1 = rot('tf', 2)
                            p.op('dve', lambda v: v.tensor_tensor(out=tmpf[t1][:], in0=b3[:], in1=rope_sb[u][:, 1, :], op=ALU.mult),
                                 r=[b3k, ('rope', u)], w=[('tmpf', t1)])
                            p.op('dve', lambda v: v.tensor_tensor(out=tmpf[t0][:], in0=tmpf[t0][:], in1=rope_sb[u][:, 0, :],
                                                                  op=ALU.mult), r=[('tmpf', t0), ('rope', u)], w=[('tmpf', t0)])
                            p.op('dve', lambda v: v.tensor_tensor(out=stgb[si][:, cols], in0=tmpf[t0][:], in1=tmpf[t1][:],
                                                                  op=ALU.add), r=[('tmpf', t0), ('tmpf', t1)], w=[('stgb', si)])
                        else:
                            p.op('act', lambda a: a.copy(out=stgb[si][:, cols], in_=tmpf[t0][:]), r=[('tmpf', t0)],
                                 w=[('stgb', si)])
                        if u == NS - 1:
                            fm_store(FM, chunk0 + ci, t, si)
                    return ev

                def inproj_ab(t, c):
                    W = ab_w_in
                    if c == 0:
                        for u in range(NS):
                            p.dma('sp', rope_sb[u][:], ropeT[:, :, t * TS + u * SUB:t * TS + (u + 1) * SUB], w=[('rope', u)])
                    proj_fm(W, 0, 4, t, simple_evac(FM0, 0, t, 'copy'))
                    proj_fm(W, 512, 4, t, simple_evac(FM0, 4, t, 'scale'))
                    proj_fm(W, 1536, 4, t, simple_evac(FM0, 8, t, 'silu'))
                    proj_fm(W, 2048, 4, t, normrope_evac(FM0, 12, t, c, 0, False))
                    proj_fm(W, 2560, 1, t, normrope_evac(FM0, 16, t, c, 1, True))

                    def ev_k(g, bk, bkey):
                        p.op('act', lambda a: a.mul(out=tstg[:, g, 0:512], in_=bk[:], mul=0.125), r=[bkey], w=TSTG_KEYS)

                    def ev_v(g, bk, bkey):
                        p.op('dve', lambda v: v.tensor_copy(out=tstg[:, g, 512:1024], in_=bk[:]), r=[bkey], w=TSTG_KEYS)

                    def ev_gv(g, bk, bkey):
                        p.op('dve', lambda v: v.tensor_copy(out=tstg[:, g, 1024:1152], in_=bk[:, 0:128]), r=[bkey], w=TSTG_KEYS)
                        if c == 1:
                            p.op('dve', lambda v: v.tensor_copy(out=ostg2[:, g, :], in_=bk[:, 0:128]), r=[bkey], w=OSTG_KEYS)
                    proj_tm(W, 512, 512, ev_k)
                    proj_tm(W, 1024, 512, ev_v)
                    proj_tm(W, 2688, 128, ev_gv)
                    if c == 1:
                        p.dma('sp', o_gv.rearrange("(g p) n -> p g n", p=128), ostg2, r=OSTG_KEYS)
                    p.dma('sp', TM0[t * TS:(t + 1) * TS, :].rearrange("(g p) n -> p g n", p=128), tstg, r=TSTG_KEYS)

                if which == 'A':
                    for t in range(NT):
                        c = 0 if t < 4 else 1
                        load_x_tokmajor(t)
                        PRE['next'] = (0, 1, c)
                        ffn(0, 0, 0, c)
                        prenorm(0, 1, c)
                        inproj_ab(t, c)
                        p.dma('sp', XS.rearrange("(k p) n -> p k n", p=128)[:, :, t * TS:(t + 1) * TS], xT[:],
                              r=[('xT', k, u) for k in range(8) for u in range(NS)])
                    p.barrier()
                def load_x_scratch(t):
                    p.dma('sp', xT[:], XS.rearrange("(k p) n -> p k n", p=128)[:, :, t * TS:(t + 1) * TS],
                          w=[('xT', k, u) for k in range(8) for u in range(NS)])

                def load_ycat(t):
                    p.dma('sp', aT[:, 0:8, :], YC.rearrange("(k p) n -> p k n", p=128)[:, :, t * TS:(t + 1) * TS],
                          w=[('aT', j, u) for j in range(8) for u in range(NS)])

                def outproj(Wo, l, c):
                    omap = {}

                    def load_out(mp):
                        wi = rot('wo', 2)
                        p.dma('pool', wobuf[wi][:, 0:8, :], Wo[:, mp * 256:(mp + 1) * 256].rearrange("(j p) n -> p j n", p=128),
                              w=[('wobuf', wi)])
                        omap[mp] = wi

                    def mmo(mp, u):
                        wi = omap[mp]
                        wo = wobuf[wi]
                        for mm in range(2):
                            m = mp * 2 + mm
                            cols = slice(u * SUB, (u + 1) * SUB)
                            bk, bkey = p.bank()
                            p.mmg([(lambda t_, j=j: t_.matmul(bk[:], lhsT=wo[:, j, mm * 128:(mm + 1) * 128], rhs=aT[:, j, cols],
                                                              start=(j == 0), stop=(j == 7))) for j in range(8)],
                                  r=[('wobuf', wi)] + [('aT', j, u) for j in range(8)], w=[bkey])
                            evac2(bk, bkey, m, u)
                    load_out(0)
                    load_out(1)
                    mmo(0, 0)
                    mmo(0, 1)
                    load_out(2)
                    mmo(1, 0)
                    mmo(1, 1)
                    load_out(3)
                    mmo(2, 0)
                    mmo(3, 0)
                    finish(l, 1, c, 0)
                    mmo(2, 1)
                    early_pre0()
                    mmo(3, 1)
                    finish(l, 1, c, 1)

                tstg1 = aT[:, 0:16, :].rearrange("p a b -> p (a b)").rearrange("p (g h e) -> p g h e", g=8, h=16)
                TSTG1_KEYS = [('aT', j, u) for j in range(16) for u in range(NS)]
                def ostgN(gq, cb):
                    return hy[:, gq * 2 + cb, :].rearrange("p (u h w) -> p u h w", u=2, h=2)[:, :, 1, :]
                ostgN_all = bass.AP(hy, 256, [[8 * TS, 128], [512, 16], [1, 256]])

                def inproj_na(t, c):
                    W = na_w_qkv
                    proj_fm(W, 0, 4, t, simple_evac(FM1, 0, t, 'copy'))
                    proj_fm(W, 512, 4, t, simple_evac(FM1, 4, t, 'copy'))
                    proj_fm(W, 1024, 4, t, simple_evac(FM1, 8, t, 'copy'))
                    proj_fm(W, 1536, 4, t, simple_evac(FM1, 12, t, 'copy'))
                    for g in range(8):
                        p.op('dve', lambda v, g=g: v.memset(tstg1[:, g, :, 64:128], 1.0), w=TSTG1_KEYS)
                    for cb in range(2):
                        def ev_v(g, bk, bkey, cb=cb):
                            p.op('dve', lambda v: v.tensor_copy(out=tstg1[:, g, cb * 8:(cb + 1) * 8, 0:64],
                                                                in_=bk[:].rearrange("p (h e) -> p h e", h=8)), r=[bkey],
                                 w=TSTG1_KEYS)
                        proj_tm(W, 2048 + cb * 512, 512, ev_v)
                    p.dma('sp', TM1[t * TS:(t + 1) * TS, :, :].rearrange("(g p) h e -> p g (h e)", p=128),
                          tstg1.rearrange("p g h e -> p g (h e)"), r=TSTG1_KEYS)
                    if c == 1:
                        for which, o_d in ((1, o_nk), (2, o_nv)):
                            for gh in range(2):
                                for cb in range(2):
                                    def ev_o(g, bk, bkey, cb=cb, gh=gh):
                                        if g // 4 == gh:
                                            p.op('act', lambda a: a.copy(out=ostgN(g % 4, cb), in_=bk[:].rearrange("p (u w) -> p u w", u=2)),
                                                 r=[bkey], w=['hy2'])
                                    proj_tm(W, which * 1024 + cb * 512, 512, ev_o)
                                for gq in range(4):
                                    p.dma('sp', o_d[gh * 512 + gq * 128:gh * 512 + (gq + 1) * 128, :].rearrange("p (q w) -> p q w", w=256),
                                          bass.AP(hy, 256 + gq * 2048, [[8 * TS, 128], [512, 4], [1, 256]]), r=['hy2'])

                if which == 'C':
                    load_x_scratch(0)
                    for t in range(NT):
                        c = 0 if t < 4 else 1
                        load_ycat(t)
                        PRE['next'] = (0, 2, c)
                        outproj(ab_w_out, 0, c)
                        PRE['next'] = (1, 0, c)
                        ffn(0, 1, 2, c)
                        PRE['next'] = (1, 1, c)
                        ffn(1, 0, 0, c)
                        prenorm(1, 1, c)
                        ensure_h(0)
                        ensure_h(1)
                        p.dma('sp', XS.rearrange("(k p) n -> p k n", p=128)[:, :, t * TS:(t + 1) * TS], xT[:],
                              r=[('xT', k, u) for k in range(8) for u in range(NS)])
                        if t + 1 < NT:
                            load_x_scratch(t + 1)
                        inproj_na(t, c)
                    p.barrier()
                if which == 'E':
                    load_x_scratch(0)
                    load_ycat(0)
                    for t in range(NT):
                        c = 0 if t < 4 else 1
                        PRE['next'] = (1, 2, c)
                        outproj(na_w_out, 1, c)
                        ffn(1, 1, 2, c, after_mm1=(lambda t=t: load_ycat(t + 1)) if t + 1 < NT else None)
                        store_y_tokmajor(t)
                    p.barrier()
        tile_scope('A')
        with ExitStack() as esB:
            sbB = lambda name, shape, dtype: esB.enter_context(nc.sbuf_tensor(name, shape, dtype))
            rc_sb = sbB("rc_sb", [128, 6 * 128 + 66], F32)
            n1, cn, dpos, dneg = rc_sb[:, 0:128], rc_sb[:, 128:256], rc_sb[:, 256:384], rc_sb[:, 384:512]
            mge, mlt = rc_sb[:, 512:640], rc_sb[:, 640:768]
            posf, posb, c128 = rc_sb[:, 768:769], rc_sb[:, 769:770], rc_sb[:, 770:834]
            lg = sbB("lg", [128, 2, 8], F32)
            DM = sbB("DM", [128, 8, 128], F32)
            tmpd = sbB("tmpd", [128, 128], F32)
            QF = sbB("QF", [128, 4, 128], F32)
            QB = sbB("QB", [128, 4, 128], F32)
            KF = sbB("KF", [128, 8], F32)
            KB = sbB("KB", [128, 8], F32)
            CDF = sbB("CDF", [128, 4, 64], F32)
            CDB = sbB("CDB", [128, 4, 64], F32)
            eps5 = sbB("eps5", [128, 1], F32)
            p.op('dve', lambda v: v.memset(eps5[:], 1e-5), w=['eps5'])
            p.dma('sp', rc_sb[:], rc, w=['rc'])
            p.dma('sp', lg[:].rearrange("p a b -> p (a b)"), bass.AP(dec_in.tensor, 0, [[0, 128], [1, 16]]), w=['lg'])
            lgf = lg[:].rearrange("p a b -> p (a b)")
            p.op('act', lambda a: a.activation(out=lgf, in_=lgf, func=AF.Exp, scale=-1.0), r=['lg'], w=['lg'])
            p.op('dve', lambda v: v.tensor_scalar_add(out=lgf, in0=lgf, scalar1=1.0), r=['lg'], w=['lg'])
            p.op('act', lambda a: a.activation(out=lgf, in_=lgf, func=AF.Ln), r=['lg'], w=['lg'])
            p.op('dve', lambda v: v.tensor_scalar_mul(out=lgf, in0=lgf, scalar1=-1.0), r=['lg'], w=['lg'])
            for h in range(8):
                par, pr = h % 2, h // 2
                rows = slice(par * 64, par * 64 + 64)
                p.op('act', lambda a: a.activation(out=QF[rows, pr, :], in_=n1[rows, :], func=AF.Exp, scale=lg[rows, 0, h:h + 1]),
                     r=['lg', 'rc'], w=['tab'])
                p.op('act', lambda a: a.activation(out=QB[rows, pr, :], in_=cn[rows, :], func=AF.Exp, scale=lg[rows, 1, h:h + 1]),
                     r=['lg', 'rc'], w=['tab'])
                p.op('act', lambda a: a.activation(out=CDF[rows, pr, :], in_=c128[rows, :], func=AF.Exp, scale=lg[rows, 0, h:h + 1]),
                     r=['lg', 'rc'], w=['tab'])
                p.op('act', lambda a: a.activation(out=CDB[rows, pr, :], in_=c128[rows, :], func=AF.Exp, scale=lg[rows, 1, h:h + 1]),
                     r=['lg', 'rc'], w=['tab'])
                p.op('act', lambda a: a.activation(out=DM[:, h, :], in_=dpos, func=AF.Exp, scale=lg[:, 0, h:h + 1]),
                     r=['lg', 'rc'], w=[('DM', h)])
                p.op('dve', lambda v: v.tensor_tensor(out=DM[:, h, :], in0=DM[:, h, :], in1=mge, op=ALU.mult),
                     r=[('DM', h), 'rc'], w=[('DM', h)])
                p.op('act', lambda a: a.activation(out=tmpd[:], in_=dneg, func=AF.Exp, scale=lg[:, 1, h:h + 1]),
                     r=['lg', 'rc'], w=['tmpd'])
                p.op('dve', lambda v: v.tensor_tensor(out=tmpd[:], in0=tmpd[:], in1=mlt, op=ALU.mult),
                     r=['tmpd', 'rc'], w=['tmpd'])
                p.op('dve', lambda v: v.tensor_tensor(out=DM[:, h, :], in0=DM[:, h, :], in1=tmpd[:], op=ALU.add),
                     r=[('DM', h), 'tmpd'], w=[('DM', h)])
            p.op('act', lambda a: a.activation(out=KF[:], in_=lg[:, 0, :], func=AF.Exp, scale=posf), r=['lg', 'rc'], w=['tab'])
            p.op('act', lambda a: a.activation(out=KB[:], in_=lg[:, 1, :], func=AF.Exp, scale=posb), r=['lg', 'rc'], w=['tab'])

            PS = [sbB("PSf", [128, 32, 4, 64], BF16), sbB("PSb", [128, 32, 4, 64], BF16)]
            stt = [sbB("stf", [128, 4, 64], F32), sbB("stb", [128, 4, 64], F32)]
            kvin = [sbB(f"kvin{i}", [128, 1024], BF16) for i in range(3)]
            kd = [sbB(f"kd{i}", [128, 512], BF16) for i in range(2)]
            qk_sb = [sbB(f"qk_sb{i}", [128, 8, 512], BF16) for i in range(2)]
            sg_sb = [sbB(f"sg_sb{i}", [128, 4, 512], BF16) for i in range(2)]
            v_sb = [sbB(f"v_sb{i}", [128, 4, 512], BF16) for i in range(2)]
            qfd = [sbB(f"qfd{i}", [128, 4, 128], BF16) for i in range(2)]
            qbd = [sbB(f"qbd{i}", [128, 4, 128], BF16) for i in range(2)]
            qm = [sbB(f"qm{i}", [128, 2, 4, 128], BF16) for i in range(2)]
            ptb = [sbB(f"ptb{i}", [128, 512], BF16) for i in range(4)]
            gnf = [sbB(f"gnf{i}", [128, 512], F32) for i in range(2)]
            gnm = [sbB(f"gnm{i}", [128, 512], F32) for i in range(2)]
            gnb = [sbB(f"gnb{i}", [128, 512], BF16) for i in range(2)]
            gns = [sbB(f"gns{i}", [128, 512], BF16) for i in range(2)]
            gnr = [sbB(f"gnr{i}", [128, 512], F32) for i in range(2)]
            ystg = [sbB(f"ystg{i}", [128, 4, 512], BF16) for i in range(2)]
            cnts = {'kv': 0, 'kd': 0, 'blk': 0, 'qd': 0, 'pt': 0, 'gn': 0, 'sbk': 0, 'rb': 0}

            def rot(name, n):
                i = cnts[name] % n
                cnts[name] += 1
                return i
            st_in = [st_f_in, st_b_in]
            seqs = [(0, 32, None)] + [(4096 + 256 * b, 2, b) for b in range(4)]
            DBGB = int(os.environ.get("KDEBUGB", "9"))
            o_st = [o_rf, o_rb]
            KT = [KF, KB]
            CD = [CDF, CDB]

            def sbank():
                i = cnts['sbk'] % 4
                cnts['sbk'] += 1
                return p.banks[i], ('ps', i)

            def ret_states(T0, NCH, bidx, banks=None):
                for d in range(2):
                    st = stt[d]
                    if bidx is None:
                        for par in range(2):
                            p.dma('sp', st[par * 64:(par + 1) * 64, :, :],
                                  st_in[d].rearrange("(pr par) dk dv -> par dk pr dv", par=2)[par], w=[('st', d)])
                    else:
                        p.op('dve', lambda v: v.memset(st[:], 0.0), w=[('st', d)])
                    order = range(NCH) if d == 0 else range(NCH - 1, -1, -1)
                    for cc in order:
                        ki = rot('kv', 3)
                        p.dma('sp', kvin[ki][:], TM0[T0 + cc * 128:T0 + (cc + 1) * 128, 0:1024], w=[('kvin', ki)])
                        p.op('act', lambda a: a.copy(out=PS[d][:, cc, :, :], in_=st[:]), r=[('st', d)], w=[('PS', d, cc)])
                        di = rot('kd', 2)
                        p.op('dve', lambda v: v.tensor_tensor(
                            out=kd[di][:].rearrange("p (h e) -> p h e", h=8),
                            in0=kvin[ki][:, 0:512].rearrange("p (h e) -> p h e", h=8),
                            in1=KT[d][:].unsqueeze(2).to_broadcast([128, 8, 64]), op=ALU.mult),
                            r=[('kvin', ki), 'tab'], w=[('kd', di)])
                        if banks is None:
                            bk, bkey = sbank()
                        else:
                            bi_ = banks[rot('rb', 2) % len(banks)]
                            bk, bkey = p.banks[bi_], ('ps', bi_)
                        p.mmg([(lambda tt, h=h: tt.matmul(
                            bk[(h % 2) * 64:(h % 2) * 64 + 64, (h // 2) * 64:(h // 2) * 64 + 64],
                            lhsT=kd[di][:, h * 64:(h + 1) * 64], rhs=kvin[ki][:, 512 + h * 64:512 + (h + 1) * 64],
                            start=True, stop=True)) for h in range(8)], r=[('kd', di), ('kvin', ki)], w=[bkey])
                        p.op('dve', lambda v: v.tensor_tensor(out=st[:], in0=st[:], in1=CD[d][:], op=ALU.mult),
                             r=[('st', d), 'tab'], w=[('st', d)])
                        p.op('dve', lambda v: v.tensor_tensor(
                            out=st[:].rearrange("p a b -> p (a b)"), in0=bk[:, 0:256],
                            in1=st[:].rearrange("p a b -> p (a b)"), op=ALU.add), r=[bkey, ('st', d)], w=[('st', d)])
                        yield
                    if bidx is not None:
                        for par in range(2):
                            p.dma('sp', o_st[d][bidx].rearrange("(pr par) dk dv -> par dk pr dv", par=2)[par],
                                  st[par * 64:(par + 1) * 64, :, :], r=[('st', d)])

            def ret_out(T0, NCH):
                CPB = min(4, NCH)
                rfifo = []
                for blk in range(NCH // CPB):
                    N = CPB * 128
                    bi = rot('blk', 2)
                    c0 = T0 + blk * N
                    p.dma('sp', qk_sb[bi][:, :, 0:N], FM0[0:1024, c0:c0 + N].rearrange("(k p) n -> p k n", p=128),
                          w=[('qk', bi)])
                    p.dma('sp', sg_sb[bi][:, :, 0:N], FM0[1024:1536, c0:c0 + N].rearrange("(k p) n -> p k n", p=128),
                          w=[('sgs', bi)])
                    p.dma('sp', v_sb[bi][:, 0:CPB, :], TM0[c0:c0 + N, 512:1024].rearrange("(ci p) n -> p ci n", p=128),
                          w=[('vs', bi)])
                    OB = [(p.banks[4 + pr], ('ps', 4 + pr)) for pr in range(4)]
                    for ci in range(CPB):
                        cc = blk * CPB + ci
                        ccols = slice(ci * 128, (ci + 1) * 128)
                        qi = rot('qd', 2)
                        p.op('dve', lambda v: v.tensor_tensor(out=qfd[qi][:], in0=qk_sb[bi][:, 0:4, ccols], in1=QF[:], op=ALU.mult),
                             r=[('qk', bi), 'tab'], w=[('qfd', qi)])
                        p.op('dve', lambda v: v.tensor_tensor(out=qbd[qi][:], in0=qk_sb[bi][:, 0:4, ccols], in1=QB[:], op=ALU.mult),
                             r=[('qk', bi), 'tab'], w=[('qbd', qi)])
                        for par in range(2):
                            p.op('dve', lambda v, par=par: v.tensor_scalar_mul(
                                out=qm[qi][:, par, :, :], in0=qk_sb[bi][:, 0:4, ccols], scalar1=hv[:, 6 + par:7 + par]),
                                r=[('qk', bi), 'hv'], w=[('qm', qi)])
                        pts = []
                        for hb in range(2):
                            bk, bkey = sbank()
                            p.mmg([(lambda tt, hh=hh: tt.matmul(
                                bk[:, hh * 128:(hh + 1) * 128],
                                lhsT=qk_sb[bi][:, 4 + (hb * 4 + hh) // 2, ccols],
                                rhs=qm[qi][:, (hb * 4 + hh) % 2, (hb * 4 + hh) // 2, :],
                                start=True, stop=True)) for hh in range(4)], r=[('qk', bi), ('qm', qi)], w=[bkey])
                            pi = rot('pt', 4)
                            p.op('dve', lambda v, pi=pi, bk=bk: v.tensor_tensor(
                                out=ptb[pi][:], in0=bk[:], in1=DM[:, hb * 4:(hb + 1) * 4, :].rearrange("p a b -> p (a b)"),
                                op=ALU.mult), r=[bkey] + [('DM', hb * 4 + i) for i in range(4)], w=[('ptb', pi)])
                            pts.append(pi)
                        def stB(bi=bi, ci=ci, cc=cc, ccols=ccols, qi=qi, pts=pts, OB=OB):
                            for h in range(8 if DBGB >= 4 else 0):
                                par, pr = h % 2, h // 2
                                rows = slice(par * 64, par * 64 + 64)
                                ob, obk = OB[pr]
                                pi = pts[h // 4]
                                hh = h % 4
                                p.mmg([
                                    lambda tt: tt.matmul(ob[rows, ccols], lhsT=v_sb[bi][:, ci, h * 64:(h + 1) * 64],
                                                         rhs=ptb[pi][:, hh * 128:(hh + 1) * 128], start=True, stop=False),
                                    lambda tt: tt.matmul(ob[rows, ccols], lhsT=PS[0][rows, cc, pr, :], rhs=qfd[qi][rows, pr, :],
                                                         start=False, stop=False),
                                    lambda tt: tt.matmul(ob[rows, ccols], lhsT=PS[1][rows, cc, pr, :], rhs=qbd[qi][rows, pr, :],
                                                         start=False, stop=True)],
                                    r=[('vs', bi), ('ptb', pi), ('PS', 0, cc), ('PS', 1, cc), ('qfd', qi), ('qbd', qi)], w=[obk])
                        rfifo.append(stB)
                        if ci == CPB - 1:
                            def stGN(bi=bi, N=N, c0=c0, OB=OB):
                                yi = rot('gn', 2)
                                for pr in range(4 if DBGB >= 5 else 0):
                                    ob, obk = OB[pr]
                                    gi = pr % 2
                                    p.op('act', lambda a: a.copy(out=gnf[gi][:, 0:N], in_=ob[:, 0:N]), r=[obk], w=[('gnf', gi)])
                                    p.op('act', lambda a: a.copy(out=gnb[gi][:, 0:N], in_=gnf[gi][:, 0:N]), r=[('gnf', gi)], w=[('gnb', gi)])
                                    p.op('act', lambda a: a.activation(out=gns[gi][:, 0:N], in_=gnf[gi][:, 0:N], func=AF.Square),
                                         r=[('gnf', gi)], w=[('gns', gi)])
                                    bm, bmk = sbank()
                                    p.mmg([lambda tt: tt.matmul(bm[:, 0:N], lhsT=bones[:], rhs=gnb[gi][:, 0:N], start=True, stop=True)],
                                          r=[('gnb', gi), 'bones'], w=[bmk])
                                    bq, bqk = sbank()
                                    p.mmg([lambda tt: tt.matmul(bq[:, 0:N], lhsT=bones[:], rhs=gns[gi][:, 0:N], start=True, stop=True)],
                                          r=[('gns', gi), 'bones'], w=[bqk])
                                    p.op('act', lambda a: a.copy(out=gnm[gi][:, 0:N], in_=bm[:, 0:N]), r=[bmk], w=[('gnm', gi)])
                                    p.op('dve', lambda v: v.tensor_tensor(out=gnr[gi][:, 0:N], in0=gnm[gi][:, 0:N], in1=gnm[gi][:, 0:N],
                                                                          op=ALU.mult), r=[('gnm', gi)], w=[('gnr', gi)])
                                    p.op('dve', lambda v: v.tensor_tensor(out=gnr[gi][:, 0:N], in0=bq[:, 0:N], in1=gnr[gi][:, 0:N],
                                                                          op=ALU.subtract), r=[bqk, ('gnr', gi)], w=[('gnr', gi)])
                                    p.op('act', lambda a: a.activation(out=gnr[gi][:, 0:N], in_=gnr[gi][:, 0:N], func=AF.Ln, bias=eps5[:],
                                                                       scale=1.0), r=[('gnr', gi), 'eps5'], w=[('gnr', gi)])
                                    p.op('act', lambda a: a.activation(out=gnr[gi][:, 0:N], in_=gnr[gi][:, 0:N], func=AF.Exp, scale=-0.5),
                                         r=[('gnr', gi)], w=[('gnr', gi)])
                                    p.op('dve', lambda v: v.tensor_tensor(out=gnf[gi][:, 0:N], in0=gnf[gi][:, 0:N], in1=gnm[gi][:, 0:N],
                                                                          op=ALU.subtract), r=[('gnf', gi), ('gnm', gi)], w=[('gnf', gi)])
                                    p.op('dve', lambda v: v.tensor_tensor(out=gnf[gi][:, 0:N], in0=gnf[gi][:, 0:N], in1=gnr[gi][:, 0:N],
                                                                          op=ALU.mult), r=[('gnf', gi), ('gnr', gi)], w=[('gnf', gi)])
                                    p.op('dve', lambda v: v.scalar_tensor_tensor(
                                        out=ystg[yi][:, pr, 0:N], in0=gnf[gi][:, 0:N], scalar=hv[:, 2 + pr:3 + pr], in1=sg_sb[bi][:, pr, 0:N],
                                        op0=ALU.mult, op1=ALU.mult), r=[('gnf', gi), ('sgs', bi), 'hv'], w=[('ystg', yi)])
                                if DBGB >= 5:
                                    p.dma('sp', YC[0:512, c0:c0 + N].rearrange("(k p) n -> p k n", p=128), ystg[yi][:, :, 0:N],
                                          r=[('ystg', yi)])
                            rfifo.append(stGN)
                        while len(rfifo) > (2 if ci == CPB - 1 else 1):
                            rfifo.pop(0)()
                while rfifo:
                    rfifo.pop(0)()

            ptq = [sbB(f"ptq{i}", [128, 2, 512], BF16) for i in range(4)]
            q_sb = [sbB(f"q_sb{i}", [128, 4, 512], BF16) for i in range(2)]
            qmk = [sbB(f"qmk{i}", [128, 2, 4, 512], BF16) for i in range(1)]
            rec = [sbB(f"rec{i}", [128, 512], F32) for i in range(2)]
            yst = [sbB(f"yst{i}", [128, 4, 512], BF16) for i in range(2)]
            cnts.update({'aq': 0, 'ap': 0, 'ar': 0, 'ay': 0, 'as': 0, 'ao': 0})

            def attention(FMq, qchunk0, nheads, kT_fn, vaug_fn, kv_keys, NKT, T0, L, QN, ychunk0, npairs=3, bg=None):
                fifo = []
                LAG = 2

                def drain(n):
                    while len(fifo) > n:
                        fifo.pop(0)()
                for qt in range(L // QN):
                    c0 = T0 + qt * QN
                    for hg in range(nheads // 8):
                        qi = rot('aq', 2)
                        p.dma('sp', q_sb[qi][:, :, 0:QN],
                              FMq[(qchunk0 + hg * 4) * 128:(qchunk0 + hg * 4 + 4) * 128, c0:c0 + QN].rearrange("(k p) n -> p k n", p=128),
                              w=[('q_sb', qi)])
                        for par in range(2):
                            p.op('dve', lambda v, par=par: v.tensor_scalar_mul(
                                out=qmk[qi % len(qmk)][:, par, :, 0:QN], in0=q_sb[qi][:, :, 0:QN], scalar1=hv[:, 6 + par:7 + par]),
                                r=[('q_sb', qi), 'hv'], w=[('qmk', qi % len(qmk))])
                        yi = rot('ay', 2)
                        for hh in range(8):
                            h = hg * 8 + hh
                            if bg is not None:
                                next(bg, None)
                            ai = 6 + rot('ao', 2)
                            acc, acck = p.banks[ai], ('ps', ai)
                            for kp in range(NKT // 2):
                                b = rot('as', npairs)
                                bkeys = [('ps', 2 * b), ('ps', 2 * b + 1)]
                                p.mmg([(lambda tt, e=e: tt.matmul(p.banks[2 * b + e][:, 0:QN], lhsT=kT_fn(h, 2 * kp + e),
                                                                  rhs=qmk[qi % len(qmk)][:, h % 2, hh // 2, 0:QN], start=True, stop=True))
                                       for e in range(2)], r=[('qmk', qi % len(qmk))] + kv_keys, w=bkeys)
                                pi = rot('ap', 4)
                                p.op('act', lambda a: a.activation(out=ptq[pi][:, :, 0:QN], in_=psum_all[:, 2 * b:2 * b + 2, 0:QN],
                                                                   func=AF.Exp, scale=0.125), r=bkeys, w=[('ptq', pi)])

                                def pv(h=h, kp=kp, pi=pi, acc=acc, acck=acck):
                                    p.mmg([(lambda tt, e=e: tt.matmul(acc[:, 0:QN], lhsT=vaug_fn(h, 2 * kp + e), rhs=ptq[pi][:, e, 0:QN],
                                                                      start=(kp == 0 and e == 0), stop=(kp == NKT // 2 - 1 and e == 1)))
                                           for e in range(2)], r=[('ptq', pi)] + kv_keys, w=[acck])
                                fifo.append(pv)
                                drain(LAG)

                            def norm(h=h, hh=hh, acc=acc, acck=acck, yi=yi):
                                ri = rot('ar', 2)
                                p.op('dve', lambda v: v.reciprocal(out=rec[ri][0:64, 0:QN], in_=acc[64:128, 0:QN]), r=[acck],
                                     w=[('rec', ri)])
                                p.op('dve', lambda v: v.tensor_tensor(
                                    out=yst[yi][(h % 2) * 64:(h % 2) * 64 + 64, hh // 2, 0:QN], in0=acc[0:64, 0:QN],
                                    in1=rec[ri][0:64, 0:QN], op=ALU.mult), r=[acck, ('rec', ri)], w=[('yst', yi)])
                            fifo.append(norm)

                        def store(hg=hg, c0=c0, yi=yi):
                            p.dma('sp', YC[(ychunk0 + hg * 4) * 128:(ychunk0 + hg * 4 + 4) * 128, c0:c0 + QN].rearrange(
                                "(k p) n -> p k n", p=128), yst[yi][:, :, 0:QN], r=[('yst', yi)])
                        fifo.append(store)
                drain(0)

            kT2 = sbB("kT2", [128, 2, 4352], BF16)
            vaug = sbB("vaug", [128, 34, 2, 128], BF16)
            ck_f = sbB("ck_f", [128, 2, 128], F32)
            ck_b = sbB("ck_b", [128, 256], BF16)
            p.dma('sp', ck_f[:], cgk.rearrange("(g p) n -> p g n", p=128), w=['ck_f'])
            bk, bkey = sbank()
            p.mmg([(lambda tt, g=g: tt.transpose(bk[:, g * 128:(g + 1) * 128], ck_f[:, g, :], ident_f[:])) for g in range(2)],
                  r=['ck_f', 'ident_f'], w=[bkey])
            p.op('act', lambda a: a.copy(out=ck_b[:], in_=bk[:, 0:256]), r=[bkey], w=['ck_b'])
            p.dma('sp', FM0[16 * 128:17 * 128, NTOK:NTOK + 256], ck_b[:], r=['ck_b'], w=['FM0c'])

            def gqa(T0, L, sample, bg):
                NK = L + (256 if sample else 0)
                NKT = NK // 128
                for kvh in range(2):
                    for half in range(2):
                        src = FM0[16 * 128 + kvh * 64:16 * 128 + kvh * 64 + 64, :]
                        p.dma('sp', kT2[half * 64:half * 64 + 64, kvh, 0:L], src[:, T0:T0 + L], w=['kT2'])
                        if sample:
                            p.dma('sp', kT2[half * 64:half * 64 + 64, kvh, L:L + 256], src[:, NTOK:NTOK + 256],
                                  r=['FM0c'], w=['kT2'])
                p.op('dve', lambda v: v.memset(vaug[:], 1.0), w=['vaug'])
                for kvh in range(2):
                    p.dma('sp', vaug[:, 0:L // 128, kvh, 0:64],
                          TM0[T0:T0 + L, 1024 + kvh * 64:1088 + kvh * 64].rearrange("(kt p) e -> p kt e", p=128), w=['vaug'])
                    if sample:
                        p.dma('pool', vaug[:, L // 128:L // 128 + 2, kvh, 0:64],
                              cgv[:, kvh * 64:(kvh + 1) * 64].rearrange("(kt p) e -> p kt e", p=128), w=['vaug'])
                attention(FM0, 12, 8, lambda h, kt: kT2[:, h // 4, kt * 128:(kt + 1) * 128],
                          lambda h, kt: vaug[:, kt, h // 4, :], ['kT2', 'vaug'], NKT, T0, L, 512 if sample else 256, 4,
                          npairs=(2 if bg is not None else 3), bg=bg)

            mwB = [sbB(f"mwB{i}", [128, 8, 256], BF16) for i in range(2)]

            def background():
                g1 = ret_states(0, 32, None, banks=[4])
                g2 = mod_layer(1, mwB, 256, p.banks[5], ('ps', 5), 'mwB')
                d1 = d2 = False
                while not (d1 and d2):
                    if not d1:
                        try:
                            next(g1)
                        except StopIteration:
                            d1 = True
                    if not d2:
                        try:
                            next(g2)
                        except StopIteration:
                            d2 = True
                    yield
            bg = background()
            gqa(0, 4096, True, bg)
            for _ in bg:
                pass
            for (T0, NCH, bidx) in seqs[1:]:
                gqa(T0, NCH * 128, False, None)
            ret_out(0, 32)
            for (T0, NCH, bidx) in seqs[1:]:
                for _ in ret_states(T0, NCH, bidx):
                    pass
                ret_out(T0, NCH)
            p.barrier()
        tile_scope('C')
        with ExitStack() as esD:
            sbD = lambda name, shape, dtype: esD.enter_context(nc.sbuf_tensor(name + '_D', shape, dtype))
            cnts = {'aq': 0, 'ap': 0, 'ar': 0, 'ay': 0, 'as': 0, 'ao': 0}

            def rot(name, n):
                i = cnts[name] % n
                cnts[name] += 1
                return i
            ptq = [sbD(f"ptq{i}", [128, 2, 512], BF16) for i in range(4)]
            q_sb = [sbD(f"q_sb{i}", [128, 4, 512], BF16) for i in range(2)]
            qmk = [sbD(f"qmk{i}", [128, 2, 4, 512], BF16) for i in range(2)]
            rec = [sbD(f"rec{i}", [128, 512], F32) for i in range(2)]
            yst = [sbD(f"yst{i}", [128, 4, 512], BF16) for i in range(2)]
            knT = sbD("knT", [128, 8, 256], BF16)
            vaugN = sbD("vaugN", [128, 2, 16, 128], BF16)
            def attention(FMq, qchunk0, nheads, kT_fn, vaug_fn, kv_keys, NKT, T0, L, QN, ychunk0, npairs=3, bg=None):
                fifo = []
                LAG = 2

                def drain(n):
                    while len(fifo) > n:
                        fifo.pop(0)()
                for qt in range(L // QN):
                    c0 = T0 + qt * QN
                    for hg in range(nheads // 8):
                        qi = rot('aq', 2)
                        p.dma('sp', q_sb[qi][:, :, 0:QN],
                              FMq[(qchunk0 + hg * 4) * 128:(qchunk0 + hg * 4 + 4) * 128, c0:c0 + QN].rearrange("(k p) n -> p k n", p=128),
                              w=[('q_sb', qi)])
                        for par in range(2):
                            p.op('dve', lambda v, par=par: v.tensor_scalar_mul(
                                out=qmk[qi % len(qmk)][:, par, :, 0:QN], in0=q_sb[qi][:, :, 0:QN], scalar1=hv[:, 6 + par:7 + par]),
                                r=[('q_sb', qi), 'hv'], w=[('qmk', qi % len(qmk))])
                        yi = rot('ay', 2)
                        for hh in range(8):
                            h = hg * 8 + hh
                            if bg is not None:
                                next(bg, None)
                            ai = 6 + rot('ao', 2)
                            acc, acck = p.banks[ai], ('ps', ai)
                            for kp in range(NKT // 2):
                                b = rot('as', npairs)
                                bkeys = [('ps', 2 * b), ('ps', 2 * b + 1)]
                                p.mmg([(lambda tt, e=e: tt.matmul(p.banks[2 * b + e][:, 0:QN], lhsT=kT_fn(h, 2 * kp + e),
                                                                  rhs=qmk[qi % len(qmk)][:, h % 2, hh // 2, 0:QN], start=True, stop=True))
                                       for e in range(2)], r=[('qmk', qi % len(qmk))] + kv_keys, w=bkeys)
                                pi = rot('ap', 4)
                                p.op('act', lambda a: a.activation(out=ptq[pi][:, :, 0:QN], in_=psum_all[:, 2 * b:2 * b + 2, 0:QN],
                                                                   func=AF.Exp, scale=0.125), r=bkeys, w=[('ptq', pi)])

                                def pv(h=h, kp=kp, pi=pi, acc=acc, acck=acck):
                                    p.mmg([(lambda tt, e=e: tt.matmul(acc[:, 0:QN], lhsT=vaug_fn(h, 2 * kp + e), rhs=ptq[pi][:, e, 0:QN],
                                                                      start=(kp == 0 and e == 0), stop=(kp == NKT // 2 - 1 and e == 1)))
                                           for e in range(2)], r=[('ptq', pi)] + kv_keys, w=[acck])
                                fifo.append(pv)
                                drain(LAG)

                            def norm(h=h, hh=hh, acc=acc, acck=acck, yi=yi):
                                ri = rot('ar', 2)
                                p.op('dve', lambda v: v.reciprocal(out=rec[ri][0:64, 0:QN], in_=acc[64:128, 0:QN]), r=[acck],
                                     w=[('rec', ri)])
                                p.op('dve', lambda v: v.tensor_tensor(
                                    out=yst[yi][(h % 2) * 64:(h % 2) * 64 + 64, hh // 2, 0:QN], in0=acc[0:64, 0:QN],
                                    in1=rec[ri][0:64, 0:QN], op=ALU.mult), r=[acck, ('rec', ri)], w=[('yst', yi)])
                            fifo.append(norm)

                        def store(hg=hg, c0=c0, yi=yi):
                            p.dma('sp', YC[(ychunk0 + hg * 4) * 128:(ychunk0 + hg * 4 + 4) * 128, c0:c0 + QN].rearrange(
                                "(k p) n -> p k n", p=128), yst[yi][:, :, 0:QN], r=[('yst', yi)])
                        fifo.append(store)
                drain(0)


            def na_prompt(T0):
                p.dma('sp', knT[:], FM1[8 * 128:16 * 128, T0:T0 + 256].rearrange("(k p) n -> p k n", p=128), w=['knT'])
                p.dma('sp', vaugN[:].rearrange("p kt h e -> p kt (h e)"),
                      TM1[T0:T0 + 256, :, :].rearrange("(kt p) h e -> p kt (h e)", p=128), w=['vaugN'])
                attention(FM1, 0, 16, lambda h, kt: knT[:, h // 2, kt * 128:(kt + 1) * 128],
                          lambda h, kt: vaugN[:, kt, h, :], ['knT', 'vaugN'], 2, T0, 256, 256, 0)
            ident_b = sbD("ident_b", [128, 128], BF16)
            p.dma('pool', ident_b[:], identb, w=['ident_b'])
            nm = sbD("nm", [128, 2, 64], F32)
            p.dma('sp', nm[:], nmask, w=['nm'])
            BT = sbD("BT", [128, 16, 14, 64], BF16)
            graw = [sbD(f"graw{i}", [128, 14, 64], F32) for i in range(2)]
            gtmp = [sbD(f"gtmp{i}", [128, 14, 64], F32) for i in range(2)]
            def build_bt(h):
                gi = h % 2
                base = 64 + h * 465 - 48
                for half in range(2):
                    p.dma('sp', graw[gi][half * 64:(half + 1) * 64, :, :],
                          bass.AP(rpb_pad.tensor, base + half * 31, [[1, 64], [31, 14], [1, 64]]), w=[('graw', gi)])
                grev = bass.AP(graw[gi], 63, [[14 * 64, 128], [64, 14], [-1, 64]])
                p.op('dve', lambda v: v.tensor_tensor(out=gtmp[gi][:], in0=grev,
                                                      in1=nm[:, 0:1, :].to_broadcast([128, 14, 64]), op=ALU.mult),
                     r=[('graw', gi), 'nm'], w=[('gtmp', gi)])
                p.op('dve', lambda v: v.scalar_tensor_tensor(out=BT[:, h, :, :], in0=gtmp[gi][:], scalar=8.0,
                                                             in1=nm[:, 1:2, :].to_broadcast([128, 14, 64]),
                                                             op0=ALU.mult, op1=ALU.add),
                     r=[('gtmp', gi), 'nm'], w=['BT'])
            ctxK = sbD("ctxK", [128, 8, 256], BF16)
            ctxV = sbD("ctxV", [128, 2, 16, 128], BF16)
            ckf = sbD("ckf", [128, 2, 1024], F32)
            p.dma('sp', ckf[:], cnk.rearrange("(g p) n -> p g n", p=128), w=['ckf'])
            for k in range(8):
                si = rot('as', 4)
                bk, bkey = p.banks[si], ('ps', si)
                p.mmg([(lambda tt, g=g: tt.transpose(bk[:, g * 128:(g + 1) * 128], ckf[:, g, k * 128:(k + 1) * 128], ident_f[:]))
                       for g in range(2)], r=['ckf', 'ident_f'], w=[bkey])
                p.op('act', lambda a: a.copy(out=ctxK[:, k, :], in_=bk[:, 0:256]), r=[bkey], w=['ctxK'])
            p.op('dve', lambda v: v.memset(ctxV[:], 1.0), w=['ctxV'])
            for kt in range(2):
                p.dma('pool', ctxV[:, kt, :, 0:64], cnv[kt * 128:(kt + 1) * 128, :].rearrange("p (h e) -> p h e", h=16),
                      w=['ctxV'])
            for b in range(4):
                na_prompt(4096 + 256 * b)
                for h in range(4 * b, 4 * b + 4):
                    build_bt(h)
            kwin = [sbD(f"kwin{i}", [128, 8, 512], BF16) for i in range(2)]
            vwin = [sbD(f"vwin{i}", [128, 4, 16, 128], BF16) for i in range(2)]
            qrow = [sbD(f"qrow{i}", [128, 8, 64], BF16) for i in range(2)]
            qrm = [sbD(f"qrm{i}", [128, 2, 8, 64], BF16) for i in range(2)]
            ptn = [sbD(f"ptn{i}", [128, 768], BF16) for i in range(4)]
            recN = [sbD(f"recN{i}", [128, 512], F32) for i in range(2)]
            yblk = [sbD(f"yblk{i}", [128, 8, 512], BF16) for i in range(2)]
            cnts.update({'pn': 0})
            nfifo = []

            def ndrain(n):
                while len(nfifo) > n:
                    nfifo.pop(0)()
            have = [set(), set()]

            def ensure_win(rr):
                rr0 = min(max(rr - 4, 0), 56)
                pc_ = rr0 % 2
                t0_ = (rr0 - pc_) // 2
                for i in range(4):
                    tau = t0_ + i
                    if tau in have[pc_]:
                        continue
                    sl = tau % 4
                    have[pc_] = {x for x in have[pc_] if x % 4 != sl}
                    have[pc_].add(tau)
                    tok = (2 * tau + pc_) * 64
                    p.dma('sp', kwin[pc_][:, :, sl * 128:(sl + 1) * 128],
                          FM1[8 * 128:16 * 128, tok:tok + 128].rearrange("(k p) n -> p k n", p=128), w=[('kwin', pc_, sl)])
                    p.dma('sp', vwin[pc_][:, sl, :, :].rearrange("p h e -> p (h e)"),
                          TM1[tok:tok + 128, :, :].rearrange("p h e -> p (h e)"), w=[('vwin', pc_, sl)])
            for r in range(64):
                r0 = min(max(r - 4, 0), 56)
                dl = r - r0
                wi = r % 2
                ndrain(0)
                ensure_win(r)
                if r + 1 < 64:
                    ensure_win(r + 1)
                pc = r0 % 2
                tau0 = (r0 - pc) // 2
                slot = [(tau0 + i) % 4 for i in range(4)]
                WK = [('kwin', pc, sl) for sl in slot] + [('vwin', pc, sl) for sl in slot]
                p.dma('sp', qrow[wi][:], FM1[0:8 * 128, r * 64:(r + 1) * 64].rearrange("(k p) n -> p k n", p=128),
                      w=[('qrow', wi)])
                for par in range(2):
                    p.op('dve', lambda v, par=par: v.tensor_scalar_mul(out=qrm[wi][:, par, :, :], in0=qrow[wi][:],
                                                                        scalar1=hv[:, 6 + par:7 + par]),
                         r=[('qrow', wi), 'hv'], w=[('qrm', wi)])
                accs = [(p.banks[4 + 2 * wi + par], ('ps', 4 + 2 * wi + par)) for par in range(2)]
                for m in range(8):
                    si = rot('as', 2)
                    sb1, sb1k = p.banks[2 * si], ('ps', 2 * si)
                    sb2, sb2k = p.banks[2 * si + 1], ('ps', 2 * si + 1)
                    qpair = qrm[wi][:, :, m, :]
                    fns = []
                    for i in range(4):
                        j = 2 * i - dl + 7
                        fns.append(lambda tt, i=i: tt.matmul(sb1[:, i * 128:(i + 1) * 128], lhsT=kwin[pc][:, m, slot[i] * 128:(slot[i] + 1) * 128],
                                                             rhs=qpair, start=True, stop=False))
                        fns.append(lambda tt, i=i, j=j: tt.matmul(sb1[:, i * 128:(i + 1) * 128], lhsT=ident_b[:],
                                                                  rhs=BT[:, 2 * m:2 * m + 2, j, :], start=False, stop=True))
                    p.mmg(fns, r=WK[0:4] + [('qrm', wi), 'BT', 'ident_b'], w=[sb1k])
                    p.mmg([(lambda tt, kt=kt: tt.matmul(sb2[:, kt * 128:(kt + 1) * 128], lhsT=ctxK[:, m, kt * 128:(kt + 1) * 128],
                                                        rhs=qpair, start=True, stop=True)) for kt in range(2)],
                          r=[('qrm', wi), 'ctxK'], w=[sb2k])
                    pi = rot('pn', 4)
                    p.op('act', lambda a: a.activation(out=ptn[pi][:, 0:512], in_=sb1[:], func=AF.Exp, scale=0.125),
                         r=[sb1k], w=[('ptn', pi)])
                    p.op('act', lambda a: a.activation(out=ptn[pi][:, 512:768], in_=sb2[:, 0:256], func=AF.Exp, scale=0.125),
                         r=[sb2k], w=[('ptn', pi)])

                    def pv(m=m, pi=pi, pc=pc, slot=slot, WK=WK, accs=accs):
                        for par in range(2):
                            h = 2 * m + par
                            acc, acck = accs[par]
                            fns = []
                            for i in range(4):
                                fns.append(lambda tt, i=i: tt.matmul(acc[:, m * 64:(m + 1) * 64], lhsT=vwin[pc][:, slot[i], h, :],
                                                                     rhs=ptn[pi][:, i * 128 + par * 64:i * 128 + par * 64 + 64],
                                                                     start=(i == 0), stop=False))
                            for kt in range(2):
                                fns.append(lambda tt, kt=kt: tt.matmul(acc[:, m * 64:(m + 1) * 64], lhsT=ctxV[:, kt, h, :],
                                                                       rhs=ptn[pi][:, 512 + kt * 128 + par * 64:512 + kt * 128 + par * 64 + 64],
                                                                       start=False, stop=(kt == 1)))
                            p.mmg(fns, r=WK[4:8] + [('ptn', pi), 'ctxV'], w=[acck])
                    nfifo.append(pv)
                    ndrain(1)

                def rownorm(r=r, accs=accs):
                    yi = (r // 8) % 2
                    for par in range(2):
                        acc, acck = accs[par]
                        ri = rot('ar', 2)
                        p.op('dve', lambda v: v.reciprocal(out=recN[ri][0:64, :], in_=acc[64:128, :]), r=[acck], w=[('recN', ri)])
                        p.op('dve', lambda v: v.tensor_tensor(
                            out=yblk[yi][par * 64:(par + 1) * 64, :, (r % 8) * 64:(r % 8 + 1) * 64],
                            in0=acc[0:64, :].rearrange("p (m q) -> p m q", m=8),
                            in1=recN[ri][0:64, :].rearrange("p (m q) -> p m q", m=8),
                            op=ALU.mult), r=[acck, ('recN', ri)], w=[('yblk', yi)])
                    if r % 8 == 7:
                        c0 = (r - 7) * 64
                        p.dma('sp', YC[:, c0:c0 + 512].rearrange("(k p) n -> p k n", p=128), yblk[yi][:], r=[('yblk', yi)])
                nfifo.append(rownorm)
            ndrain(0)
            p.barrier()
        tile_scope('E')
    return nc


_NC = None


def _consts():
    f32 = np.float32
    t = np.arange(4096)
    rows, cols = t // 64, t % 64
    inv = (10000.0 ** (-np.arange(16, dtype=np.float64) / 16))
    cosT = np.zeros((128, 4096), f32)
    sinT = np.zeros((128, 4096), f32)
    for pp in range(128):
        i = pp % 64
        pos = rows if i < 32 else cols
        ang = (pos.astype(np.float32)[:, None] * inv.astype(np.float32)[None, :])[:, i % 16].astype(np.float32)
        cosT[pp] = np.cos(ang)
        sinT[pp] = np.sin(ang)
    ropeT = np.ascontiguousarray(np.stack([cosT, sinT], 1))
    pm = np.zeros((128, 128), f32)
    for base in range(0, 128, 32):
        for j in range(16):
            pm[base + 16 + j, base + j] = -1.0
            pm[base + j, base + 16 + j] = 1.0
    m = np.arange(128)[:, None].astype(f32)
    n = np.arange(128)[None, :].astype(f32)
    rc = np.concatenate([np.broadcast_to(n + 1, (128, 128)), np.broadcast_to(128 - n, (128, 128)),
                         np.maximum(n - m, 0), np.maximum(m - n, 0), (n >= m).astype(f32), (m > n).astype(f32),
                         127 - m, m, np.full((128, 64), 128.0, f32)], 1).astype(f32)
    cc = np.arange(64)
    c0 = np.clip(cc - 8, 0, 48)
    kc = np.arange(64)[:, None]
    m01 = ((kc >= c0[None, :]) & (kc < c0[None, :] + 16)).astype(f32)
    m01 = np.concatenate([m01, m01], 0)
    nmask = np.ascontiguousarray(np.stack([m01, (m01 - 1.0) * 240000.0], 1).astype(f32))
    return ropeT, pm, np.ascontiguousarray(rc), nmask


def kernel(**inp):
    global _NC
    f32 = np.float32
    xs, xp = np.asarray(inp['x_sample'], f32), np.asarray(inp['x_prompt'], f32)
    c, c_ctx = np.asarray(inp['c'], f32), np.asarray(inp['c_ctx'], f32)
    vec12 = np.concatenate([np.asarray(inp['norm_pre'], f32).reshape(6, D), np.asarray(inp['norm_post'], f32).reshape(6, D)], 0)
    vecs = np.ascontiguousarray(vec12.reshape(12, 8, 128).transpose(2, 0, 1))
    mod_bT = np.ascontiguousarray(np.asarray(inp['mod_b'], f32).reshape(2, 72, 128).transpose(2, 0, 1))
    ropeT, pm, rc, nmask = _consts()
    rpb_pad = np.pad(np.asarray(inp['na_rpb'], f32)[0].reshape(-1), (64, 64))
    hvec = np.zeros((128, 8), f32)
    hvec[:, 0] = np.tile(np.asarray(inp['gqa_q_norm'], f32)[0], 2)
    hvec[:, 1] = np.tile(np.asarray(inp['gqa_k_norm'], f32)[0], 2)
    hvec[:, 2:6] = np.asarray(inp['ret_gn'], f32)[0].reshape(4, 128).T
    hvec[0:64, 6] = 1.0
    hvec[64:128, 7] = 1.0
    dec_in = np.ascontiguousarray(np.stack([np.asarray(inp['ret_decay_fwd'], f32)[0], np.asarray(inp['ret_decay_bwd'], f32)[0]], 0))
    shared = dict(mod_w=np.asarray(inp['mod_w'], f32), mod_bT=mod_bT, vecs=vecs,
                  ffn_w_in=np.asarray(inp['ffn_w_in'], f32), ffn_w_out=np.asarray(inp['ffn_w_out'], f32),
                  identf=np.eye(128, dtype=f32), ab_w_in=np.asarray(inp['ab_w_in'], f32)[0], hvec=hvec, ropeT=ropeT,
                  pmat=pm, rc=rc, dec_in=dec_in,
                  ab_w_out=np.asarray(inp['ab_w_out'], f32)[0], na_w_qkv=np.asarray(inp['na_w_qkv'], f32)[0],
                  na_w_out=np.asarray(inp['na_w_out'], f32)[0], rpb_pad=rpb_pad, nmask=nmask,
                  identb=np.eye(128, dtype=f32))
    in_maps = []
    for i in range(8):
        x = np.concatenate([xs[i], xp[4 * i:4 * i + 4].reshape(1024, D)], 0)
        condT = np.ascontiguousarray(np.stack([c[i], c_ctx], 1))
        in_maps.append(dict(x=np.ascontiguousarray(x), condT=condT,
                            st_f=np.ascontiguousarray(np.asarray(inp['state_ret_fwd'], f32)[i, 0]),
                            st_b=np.ascontiguousarray(np.asarray(inp['state_ret_bwd'], f32)[i, 0]),
                            cgk=np.ascontiguousarray(np.asarray(inp['cache_gqa_k'], f32)[i, 0].reshape(256, 128)),
                            cgv=np.ascontiguousarray(np.asarray(inp['cache_gqa_v'], f32)[i, 0].reshape(256, 128)),
                            cnk=np.ascontiguousarray(np.asarray(inp['cache_na_k'], f32)[i, 0].reshape(256, 1024)),
                            cnv=np.ascontiguousarray(np.asarray(inp['cache_na_v'], f32)[i, 0].reshape(256, 1024)),
                            **shared))
    if _NC is None:
        _NC = build_program()
    res = run_bass_kernel_spmd(_NC, in_maps, core_ids=list(range(8)))
    R = res.results
    ys = np.stack([r["y"][:4096] for r in R], 0)
    yp = np.concatenate([r["y"][4096:].reshape(4, 256, D) for r in R], 0)
    gk = np.concatenate([r["o_gk"].reshape(4, 1, 256, 2, 64) for r in R], 0)
    gv = np.concatenate([r["o_gv"].reshape(4, 1, 256, 2, 64) for r in R], 0)
    rf = np.concatenate([r["o_rf"].reshape(4, 1, 8, 64, 64) for r in R], 0)
    rb = np.concatenate([r["o_rb"].reshape(4, 1, 8, 64, 64) for r in R], 0)
    nk = np.concatenate([r["o_nk"].reshape(4, 1, 256, 16, 64) for r in R], 0)
    nv = np.concatenate([r["o_nv"].reshape(4, 1, 256, 16, 64) for r in R], 0)
    return yp, ys, gk, gv, rf, rb, nk, nv
```

```python
import numpy as np
from contextlib import ExitStack
import concourse.bass as bass
import concourse.mybir as mybir
from concourse.bass_utils import run_bass_kernel_spmd
import ml_dtypes

F32, BF16 = mybir.dt.float32, mybir.dt.bfloat16
AF = mybir.ActivationFunctionType
ALU = mybir.AluOpType
AX = mybir.AxisListType

D = 1024
DFF = 2816
NJ = DFF // 128
TS = 1024
SUB = 512
NS = TS // SUB
NT = 5
NTOK = NT * TS
EPS = 1e-6


class P:
    def __init__(self, nc, es):
        self.nc = nc
        self.E = {'pe': nc.tensor, 'act': nc.scalar, 'dve': nc.vector, 'pool': nc.gpsimd, 'sp': nc.sync}
        self.sems = {}
        self.cnt = {}
        for k in self.E:
            self.sems[k] = es.enter_context(nc.semaphore("s_" + k))
            self.cnt[k] = 0
        self.NDS = 12
        for q in ('pool', 'sp'):
            for i in range(self.NDS):
                key = ('d', q, i)
                self.sems[key] = es.enter_context(nc.semaphore(f"d_{q}{i}"))
                self.cnt[key] = 0
        self.dq = {'pool': 0, 'sp': 0}
        self.waited = {}
        self.W = {}
        self.R = {}
        self.bank_i = 0
        self.banks = []

    def _deps(self, e, r, w, is_dma):
        deps = {}

        def add(sk, v, raw):
            if sk == e and not is_dma and e == 'pe':
                return
            if deps.get(sk, 0) < v:
                deps[sk] = v
        for k in r:
            for sk, v in self.W.get(k, {}).items():
                add(sk, v, True)
        for k in w:
            for sk, v in self.W.get(k, {}).items():
                add(sk, v, False)
            for sk, v in self.R.get(k, {}).items():
                add(sk, v, False)
        return deps

    def _wait(self, e, deps):
        for sk, v in deps.items():
            if self.waited.get((e, sk), 0) >= v:
                continue
            self.E[e].wait_ge(self.sems[sk], v)
            self.waited[(e, sk)] = v

    def _reg(self, ev, r, w):
        sk, v = ev
        for k in r:
            self.R.setdefault(k, {})[sk] = v
        for k in w:
            self.W[k] = {sk: v}
            self.R[k] = {}

    def op(self, e, fn, r=(), w=()):
        self._wait(e, self._deps(e, r, w, False))
        ins = fn(self.E[e])
        ins.then_inc(self.sems[e], 1)
        self.cnt[e] += 1
        self._reg((e, self.cnt[e]), r, w)

    def mmg(self, fns, r=(), w=()):
        self._wait('pe', self._deps('pe', r, w, False))
        pe = self.E['pe']
        for f in fns[:-1]:
            f(pe)
        ins = fns[-1](pe)
        ins.then_inc(self.sems['pe'], 1)
        self.cnt['pe'] += 1
        self._reg(('pe', self.cnt['pe']), r, w)

    def dma(self, q, out, in_, r=(), w=(), **kw):
        self._wait(q, self._deps(q, r, w, True))
        i = self.dq[q] % self.NDS
        self.dq[q] += 1
        key = ('d', q, i)
        self.E[q].dma_start(out=out, in_=in_, **kw).then_inc(self.sems[key], 16)
        self.cnt[key] += 16
        self._reg((key, self.cnt[key]), r, w)

    def barrier(self):
        for e in self.E:
            for sk, v in self.cnt.items():
                if sk == e or v == 0:
                    continue
                if self.waited.get((e, sk), 0) >= v:
                    continue
                self.E[e].wait_ge(self.sems[sk], v)
                self.waited[(e, sk)] = v
        self.W = {}
        self.R = {}

    def bank(self):
        i = self.bank_i % 8
        self.bank_i += 1
        return self.banks[i], ('ps', i)


def build_program():
    nc = bass.Bass("TRN2", target_bir_lowering=False)
    dt = nc.dram_tensor
    x_in = dt("x", [NTOK, D], F32, kind="ExternalInput").ap()
    condT = dt("condT", [D, 2], F32, kind="ExternalInput").ap()
    mod_w = dt("mod_w", [2, D, 9 * D], F32, kind="ExternalInput").ap()
    mod_bT = dt("mod_bT", [128, 2, 72], F32, kind="ExternalInput").ap()
    vecs = dt("vecs", [128, 12, 8], F32, kind="ExternalInput").ap()
    ffn_w_in = dt("ffn_w_in", [2, 2, D, 2 * DFF], F32, kind="ExternalInput").ap()
    ffn_w_out = dt("ffn_w_out", [2, 2, DFF, D], F32, kind="ExternalInput").ap()
    identf = dt("identf", [128, 128], F32, kind="ExternalInput").ap()
    y_out = dt("y", [NTOK, D], F32, kind="ExternalOutput").ap()
    ab_w_in = dt("ab_w_in", [D, 2816], F32, kind="ExternalInput").ap()
    hvec = dt("hvec", [128, 8], F32, kind="ExternalInput").ap()
    ropeT = dt("ropeT", [128, 2, 4096], F32, kind="ExternalInput").ap()
    pmat = dt("pmat", [128, 128], F32, kind="ExternalInput").ap()
    rc = dt("rc", [128, 6 * 128 + 66], F32, kind="ExternalInput").ap()
    dec_in = dt("dec_in", [2, 8], F32, kind="ExternalInput").ap()
    st_f_in = dt("st_f", [8, 64, 64], F32, kind="ExternalInput").ap()
    st_b_in = dt("st_b", [8, 64, 64], F32, kind="ExternalInput").ap()
    o_gk = dt("o_gk", [1024, 128], F32, kind="ExternalOutput").ap()
    o_gv = dt("o_gv", [1024, 128], F32, kind="ExternalOutput").ap()
    o_rf = dt("o_rf", [4, 8, 64, 64], F32, kind="ExternalOutput").ap()
    o_rb = dt("o_rb", [4, 8, 64, 64], F32, kind="ExternalOutput").ap()
    ab_w_out = dt("ab_w_out", [D, D], F32, kind="ExternalInput").ap()
    na_w_qkv = dt("na_w_qkv", [D, 3 * D], F32, kind="ExternalInput").ap()
    na_w_out = dt("na_w_out", [D, D], F32, kind="ExternalInput").ap()
    cgk = dt("cgk", [256, 128], F32, kind="ExternalInput").ap()
    cgv = dt("cgv", [256, 128], F32, kind="ExternalInput").ap()
    cnk = dt("cnk", [256, 1024], F32, kind="ExternalInput").ap()
    cnv = dt("cnv", [256, 1024], F32, kind="ExternalInput").ap()
    o_nk = dt("o_nk", [1024, 1024], F32, kind="ExternalOutput").ap()
    o_nv = dt("o_nv", [1024, 1024], F32, kind="ExternalOutput").ap()
    rpb_pad = dt("rpb_pad", [64 + 16 * 15 * 31 + 64], F32, kind="ExternalInput").ap()
    nmask = dt("nmask", [128, 2, 64], F32, kind="ExternalInput").ap()
    identb = dt("identb", [128, 128], F32, kind="ExternalInput").ap()
    FM1 = dt("FM1", [16 * 128, NTOK + 256], BF16).ap()
    TM1 = dt("TM1", [NTOK, 16, 128], BF16).ap()
    XS = dt("XS", [D, NTOK], F32).ap()
    FM0 = dt("FM0", [17 * 128, NTOK + 256], BF16).ap()
    TM0 = dt("TM0", [NTOK, 1152], BF16).ap()
    YC = dt("YC", [D, NTOK], BF16).ap()

    with ExitStack() as es:
        p = P(nc, es)
        sb = lambda name, shape, dtype: es.enter_context(nc.sbuf_tensor(name, shape, dtype))
        psum_all = es.enter_context(nc.psum_tensor("psum_all", [128, 8, SUB], F32))
        for i in range(8):
            p.banks.append(psum_all[:, i, :])
        ident_f = sb("ident_f", [128, 128], F32)
        onesD = sb("onesD", [128, 128], BF16)
        eps_t = sb("eps_t", [128, 1], F32)
        scal = sb("scal", [128, 2 * 3 * 2 * 3, 8], F32)
        vec_sb = sb("vec_sb", [128, 12, 8], F32)
        p.dma('sp', ident_f[:], identf, w=['ident_f'])
        p.dma('sp', vec_sb[:], vecs, w=['vec_sb'])
        p.op('dve', lambda v: v.memset(onesD[:], 1.0 / D), w=['onesD'])
        p.op('dve', lambda v: v.memset(eps_t[:], EPS), w=['eps_t'])

        bones = sb("bones", [128, 128], BF16)
        pm_b = sb("pm_b", [128, 128], BF16)
        hv = sb("hv", [128, 8], F32)
        p.op('dve', lambda v: v.memset(bones[:], 0.0), w=['bones'])
        p.op('dve', lambda v: v.memset(bones[0:64, 0:64], 1.0 / 64), w=['bones'])
        p.op('dve', lambda v: v.memset(bones[64:128, 64:128], 1.0 / 64), w=['bones'])
        p.dma('pool', pm_b[:], pmat, w=['pm_b'])
        p.dma('sp', hv[:], hvec, w=['hv'])

        def sc(l, s, c, which):
            return scal[:, ((l * 3 + s) * 2 + c) * 3 + which, :]

        import os
        DBG = int(os.environ.get("KDEBUG", "9"))
        cond_f = sb("cond_f", [128, 8, 2], F32)
        cond_b = sb("cond_b", [128, 8, 2], BF16)
        modb = sb("modb", [128, 2, 72], F32)
        modsb = sb("modsb", [128, 72, 2], F32)
        p.dma('sp', cond_f[:], condT.rearrange("(k p) c -> p k c", p=128), w=['cond_f'])
        p.dma('sp', modb[:], mod_bT, w=['modb'])
        p.op('act', lambda a: a.activation(out=cond_b[:], in_=cond_f[:], func=AF.Silu), r=['cond_f'], w=['cond_b'])

        def mod_layer(l, mwbufs, ncol, bk, bkey, tag):
            jper = ncol // 128
            for g in range(72 // jper):
                buf = mwbufs[g % 2]
                p.dma('pool', buf[:, :, 0:ncol], mod_w[l, :, g * ncol:(g + 1) * ncol].rearrange("(k p) n -> p k n", p=128),
                      w=[(tag, g % 2)])
                for jj in range(jper):
                    j = g * jper + jj
                    p.mmg([(lambda t, k=k: t.matmul(bk[:, 2 * j:2 * j + 2], lhsT=buf[:, k, jj * 128:(jj + 1) * 128],
                                                    rhs=cond_b[:, k, :], start=(k == 0), stop=(k == 7))) for k in range(8)],
                          r=[(tag, g % 2), 'cond_b'], w=[bkey])
                yield
            for c in range(2):
                p.op('dve', lambda v, c=c: v.tensor_tensor(
                    out=modsb[:, :, c], in0=bk[:, 0:144].rearrange("p (j c) -> p j c", c=2)[:, :, c],
                    in1=modb[:, l, :], op=ALU.add), r=[bkey, 'modb'], w=[('modsb', c)])
            for s in range(3):
                wgt = 1.0 if s == 1 else 0.5
                for c in range(2):
                    shift = modsb[:, (3 * s) * 8:(3 * s) * 8 + 8, c]
                    scale = modsb[:, (3 * s + 1) * 8:(3 * s + 1) * 8 + 8, c]
                    gate = modsb[:, (3 * s + 2) * 8:(3 * s + 2) * 8 + 8, c]
                    gpre = vec_sb[:, l * 3 + s, :]
                    gpost = vec_sb[:, 6 + l * 3 + s, :]
                    p.op('dve', lambda v: v.scalar_tensor_tensor(
                        out=sc(l, s, c, 0), in0=scale, scalar=1.0, in1=gpre, op0=ALU.add, op1=ALU.mult),
                        r=[('modsb', c), 'vec_sb'], w=['scal'])
                    p.op('dve', lambda v: v.tensor_copy(out=sc(l, s, c, 1), in_=shift), r=[('modsb', c)], w=['scal'])
                    p.op('dve', lambda v: v.scalar_tensor_tensor(
                        out=sc(l, s, c, 2), in0=gate, scalar=wgt, in1=gpost, op0=ALU.mult, op1=ALU.mult),
                        r=[('modsb', c), 'vec_sb'], w=['scal'])
            yield

        with ExitStack() as es0:
            mw = [es0.enter_context(nc.sbuf_tensor(f"mw{i}", [128, 8, 1152], BF16)) for i in range(2)]
            bk0, bk0k = p.bank()
            for _ in mod_layer(0, mw, 1152, bk0, bk0k, 'mw'):
                pass
            p.barrier()

        def tile_scope(which):
            with ExitStack() as esT:
                sbT = lambda name, shape, dtype: esT.enter_context(nc.sbuf_tensor(name + '_' + which, shape, dtype))
                xT = sbT("xT", [128, 8, TS], F32)
                hy = sbT("hy", [128, 8, TS], F32)
                def hTv(k, u):
                    return hy[:, k, u * SUB:u * SUB + SUB // 2].bitcast(BF16)
                aT = sbT("aT", [128, NJ, TS], BF16)
                sq = sbT("sq", [128, 8, SUB], BF16)
                rstd = [sbT(f"rstd{i}", [128, SUB], F32) for i in range(2)]
                tmpf = [sbT(f"tmpf{i}", [128, SUB], F32) for i in range(2)]
                sil = [sbT(f"sil{i}", [128, SUB], F32) for i in range(2)]
                NWB = 3
                wbuf = [sbT(f"wbuf{i}", [128, 8, 2, 256], BF16) for i in range(NWB)]
                wobuf = [sbT(f"wobuf{i}", [128, NJ, 256], BF16) for i in range(2)]
                cnts = {'wb': 0, 'wo': 0, 'rs': 0, 'tf': 0, 'sl': 0}

                def rot(name, n):
                    i = cnts[name] % n
                    cnts[name] += 1
                    return i

                def rstd_of(src_fn, src_keys, nchunks, lhsT_ones, lkey):
                    for k in range(nchunks):
                        p.op('act', lambda a, k=k: a.activation(out=sq[:, k, :], in_=src_fn(k), func=AF.Square),
                             r=[src_keys[k]], w=[('sq', k)])
                    bk, bkey = p.bank()
                    p.mmg([(lambda t, k=k: t.matmul(bk[:], lhsT=lhsT_ones, rhs=sq[:, k, :], start=(k == 0),
                                                    stop=(k == nchunks - 1))) for k in range(nchunks)],
                          r=[('sq', k) for k in range(nchunks)] + [lkey], w=[bkey])
                    ri = rot('rs', 2)
                    p.op('act', lambda a: a.activation(out=rstd[ri][:], in_=bk[:], func=AF.Ln, bias=eps_t[:], scale=1.0),
                         r=[bkey, 'eps_t'], w=[('rstd', ri)])
                    p.op('act', lambda a: a.activation(out=rstd[ri][:], in_=rstd[ri][:], func=AF.Exp, scale=-0.5),
                         r=[('rstd', ri)], w=[('rstd', ri)])
                    return ri

                HPEND = {}

                def ensure_h(u):
                    if u in HPEND:
                        HPEND.pop(u)()

                PRE = {'done': None, 'next': None}

                def prenorm(l, s, c):
                    if PRE['done'] == (l, s, c):
                        PRE['done'] = None
                        return
                    for u in range(NS):
                        ensure_h(u)
                        HPEND[u] = (lambda u=u: prenorm_u(l, s, c, u))

                def early_pre0():
                    if PRE['next'] is not None:
                        l, s, c = PRE['next']
                        PRE['next'] = None
                        prenorm(l, s, c)
                        PRE['done'] = (l, s, c)
                        ensure_h(0)

                def prenorm_u(l, s, c, u):
                    if True:
                        cols = slice(u * SUB, (u + 1) * SUB)
                        ri = rstd_of(lambda k: xT[:, k, cols], [('xT', k, u) for k in range(8)], 8, onesD[:], 'onesD')
                        for k in range(8):
                            ti = rot('tf', 2)
                            p.op('dve', lambda v, k=k, ti=ti: v.scalar_tensor_tensor(
                                out=tmpf[ti][:], in0=xT[:, k, cols], scalar=sc(l, s, c, 0)[:, k:k + 1], in1=rstd[ri][:],
                                op0=ALU.mult, op1=ALU.mult), r=[('xT', k, u), ('rstd', ri), 'scal'], w=[('tmpf', ti)])
                            p.op('act', lambda a, k=k, ti=ti: a.activation(
                                out=hTv(k, u), in_=tmpf[ti][:], func=AF.Identity, bias=sc(l, s, c, 1)[:, k:k + 1], scale=1.0),
                                r=[('tmpf', ti), 'scal'], w=[('hy', u)])

                def postres(l, s, c, u):
                    cols = slice(u * SUB, (u + 1) * SUB)
                    bk, bkey = p.bank()
                    p.mmg([(lambda t, k=k: t.matmul(bk[:], lhsT=onesD[:], rhs=sq[:, k, :], start=(k == 0), stop=(k == 7)))
                           for k in range(8)], r=[('sq', k) for k in range(8)] + ['onesD'], w=[bkey])
                    ri = rot('rs', 2)
                    p.op('act', lambda a: a.activation(out=rstd[ri][:], in_=bk[:], func=AF.Ln, bias=eps_t[:], scale=1.0),
                         r=[bkey, 'eps_t'], w=[('rstd', ri)])
                    p.op('act', lambda a: a.activation(out=rstd[ri][:], in_=rstd[ri][:], func=AF.Exp, scale=-0.5),
                         r=[('rstd', ri)], w=[('rstd', ri)])
                    for k in range(8):
                        ti = rot('tf', 2)
                        p.op('dve', lambda v, k=k, ti=ti: v.tensor_tensor(out=tmpf[ti][:], in0=hy[:, k, cols], in1=rstd[ri][:],
                                                                          op=ALU.mult),
                             r=[('hy', u), ('rstd', ri)], w=[('tmpf', ti)])
                        p.op('dve', lambda v, k=k, ti=ti: v.scalar_tensor_tensor(
                            out=xT[:, k, cols], in0=tmpf[ti][:], scalar=sc(l, s, c, 2)[:, k:k + 1], in1=xT[:, k, cols],
                            op0=ALU.mult, op1=ALU.add), r=[('tmpf', ti), ('xT', k, u), 'scal'], w=[('xT', k, u)])

                def evac_y(bk, bkey, m, u):
                    cols = slice(u * SUB, (u + 1) * SUB)
                    p.op('act', lambda a: a.activation(out=sq[:, m, :], in_=bk[:], func=AF.Square), r=[bkey], w=[('sq', m)])
                    p.op('dve', lambda v: v.tensor_copy(out=hy[:, m, cols], in_=bk[:]), r=[bkey], w=[('hy', u)])

                def ffn(l, i, s, c, after_mm1=None):
                    prenorm(l, s, c)
                    w_in = ffn_w_in[l, i]
                    w_out = ffn_w_out[l, i]
                    wmap = {}

                    def load_in(jp):
                        wi = rot('wb', NWB)
                        wb = wbuf[wi]
                        for half in range(2):
                            p.dma('pool', wb[:, :, half, :],
                                  w_in[:, half * DFF + jp * 256: half * DFF + (jp + 1) * 256].rearrange("(k p) n -> p k n", p=128),
                                  w=[('wbuf', wi)])
                        wmap[jp] = wi

                    def mm1(jp, u):
                        wi = wmap[jp]
                        wb = wbuf[wi]
                        ensure_h(u)
                        for jj in range(2):
                            j = jp * 2 + jj
                            cols = slice(u * SUB, (u + 1) * SUB)
                            bg, bgk = p.bank()
                            bu, buk = p.bank()
                            p.mmg([(lambda t, k=k: t.matmul(bg[:], lhsT=wb[:, k, 0, jj * 128:(jj + 1) * 128], rhs=hTv(k, u),
                                                            start=(k == 0), stop=(k == 7))) for k in range(8)],
                                  r=[('wbuf', wi), ('hy', u)], w=[bgk])
                            p.mmg([(lambda t, k=k: t.matmul(bu[:], lhsT=wb[:, k, 1, jj * 128:(jj + 1) * 128], rhs=hTv(k, u),
                                                            start=(k == 0), stop=(k == 7))) for k in range(8)],
                                  r=[('wbuf', wi), ('hy', u)], w=[buk])
                            si = rot('sl', 2)
                            p.op('act', lambda a, si=si, bg=bg: a.activation(out=sil[si][:], in_=bg[:], func=AF.Silu),
                                 r=[bgk], w=[('sil', si)])
                            p.op('dve', lambda v, si=si, bu=bu, j=j, cols=cols: v.tensor_tensor(
                                out=aT[:, j, cols], in0=bu[:], in1=sil[si][:], op=ALU.mult),
                                r=[buk, ('sil', si)], w=[('aT', j, u)])
                    load_in(0)
                    load_in(1)
                    mm1(0, 0)
                    mm1(1, 0)
                    load_in(2)
                    mm1(0, 1)
                    mm1(1, 1)
                    for jp in range(2, NJ // 2):
                        if jp + 1 < NJ // 2:
                            load_in(jp + 1)
                        mm1(jp, 0)
                        mm1(jp, 1)
                    omap = {}

                    def load_out(mp):
                        wi = rot('wo', 2)
                        p.dma('pool', wobuf[wi][:], w_out[:, mp * 256:(mp + 1) * 256].rearrange("(j p) n -> p j n", p=128),
                              w=[('wobuf', wi)])
                        omap[mp] = wi

                    def mm2(mp, u):
                        wi = omap[mp]
                        wo = wobuf[wi]
                        for mm in range(2):
                            m = mp * 2 + mm
                            cols = slice(u * SUB, (u + 1) * SUB)
                            bk, bkey = p.bank()
                            p.mmg([(lambda t, j=j: t.matmul(bk[:], lhsT=wo[:, j, mm * 128:(mm + 1) * 128], rhs=aT[:, j, cols],
                                                            start=(j == 0), stop=(j == NJ - 1))) for j in range(NJ)],
                                  r=[('wobuf', wi)] + [('aT', j, u) for j in range(NJ)], w=[bkey])
                            evac2(bk, bkey, m, u)
                    load_out(0)
                    load_out(1)
                    mm2(0, 0)
                    mm2(0, 1)
                    load_out(2)
                    mm2(1, 0)
                    mm2(1, 1)
                    load_out(3)
                    mm2(2, 0)
                    mm2(3, 0)
                    finish(l, s, c, 0)
                    mm2(2, 1)
                    early_pre0()
                    mm2(3, 1)
                    if after_mm1 is not None:
                        after_mm1()
                    finish(l, s, c, 1)

                sq2 = [sq, sbT("sq_b", [128, 8, SUB], BF16)]

                def evac2(bk, bkey, m, u):
                    cols = slice(u * SUB, (u + 1) * SUB)
                    p.op('dve', lambda v: v.tensor_copy(out=hy[:, m, cols], in_=bk[:]), r=[bkey], w=[('hy', u), ('yT', u), 'hy2'])
                    p.op('act', lambda a: a.activation(out=sq2[u][:, m, :], in_=hy[:, m, cols], func=AF.Square),
                         r=[('hy', u)], w=[('sq', u, m)])

                def finish(l, s, c, u):
                    cols = slice(u * SUB, (u + 1) * SUB)
                    bk, bkey = p.bank()
                    p.mmg([(lambda t, k=k: t.matmul(bk[:], lhsT=onesD[:], rhs=sq2[u][:, k, :], start=(k == 0), stop=(k == 7)))
                           for k in range(8)], r=[('sq', u, k) for k in range(8)] + ['onesD'], w=[bkey])
                    ri = rot('rs', 2)
                    p.op('act', lambda a: a.activation(out=rstd[ri][:], in_=bk[:], func=AF.Ln, bias=eps_t[:], scale=1.0),
                         r=[bkey, 'eps_t'], w=[('rstd', ri)])
                    p.op('act', lambda a: a.activation(out=rstd[ri][:], in_=rstd[ri][:], func=AF.Exp, scale=-0.5),
                         r=[('rstd', ri)], w=[('rstd', ri)])
                    for k in range(8):
                        ti = rot('tf', 2)
                        p.op('dve', lambda v, k=k, ti=ti: v.tensor_tensor(out=tmpf[ti][:], in0=hy[:, k, cols], in1=rstd[ri][:],
                                                                          op=ALU.mult),
                             r=[('hy', u), ('yT', u), ('rstd', ri)], w=[('tmpf', ti)])
                        p.op('dve', lambda v, k=k, ti=ti: v.scalar_tensor_tensor(
                            out=xT[:, k, cols], in0=tmpf[ti][:], scalar=sc(l, s, c, 2)[:, k:k + 1], in1=xT[:, k, cols],
                            op0=ALU.mult, op1=ALU.add), r=[('tmpf', ti), ('xT', k, u), 'scal'], w=[('xT', k, u)])

                def load_x_tokmajor(t):
                    p.dma('sp', hy[:], x_in[t * TS:(t + 1) * TS, :].rearrange("(g p) d -> p g d", p=128),
                          w=[('hy', 0), ('hy', 1), 'hy2'])
                    for k in range(8):
                        for u in range(NS):
                            bk, bkey = p.bank()
                            p.mmg([(lambda tt, gi=gi: tt.transpose(bk[:, gi * 128:(gi + 1) * 128],
                                                                   hy[:, u * 4 + gi, k * 128:(k + 1) * 128], ident_f[:]))
                                   for gi in range(4)], r=[('hy', 0), ('hy', 1), 'ident_f'], w=[bkey])
                            eng = 'act' if (k + u) % 2 == 0 else 'dve'
                            if eng == 'act':
                                p.op('act', lambda a, k=k, u=u: a.copy(out=xT[:, k, u * SUB:(u + 1) * SUB], in_=bk[:]),
                                     r=[bkey], w=[('xT', k, u)])
                            else:
                                p.op('dve', lambda v, k=k, u=u: v.tensor_copy(out=xT[:, k, u * SUB:(u + 1) * SUB], in_=bk[:]),
                                     r=[bkey], w=[('xT', k, u)])

                def store_y_tokmajor(t):
                    for g in range(8):
                        u = g // 4
                        for half in range(2):
                            bk, bkey = p.bank()
                            p.mmg([(lambda tt, kk=kk: tt.transpose(bk[:, kk * 128:(kk + 1) * 128],
                                                                   xT[:, half * 4 + kk, g * 128:(g + 1) * 128], ident_f[:]))
                                   for kk in range(4)], r=[('xT', half * 4 + kk, u) for kk in range(4)] + ['ident_f'], w=[bkey])
                            eng = 'act' if half == 0 else 'dve'
                            if eng == 'act':
                                p.op('act', lambda a, g=g, half=half: a.copy(out=hy[:, g, half * 512:(half + 1) * 512], in_=bk[:]),
                                     r=[bkey], w=[('hy', u), 'hy2'])
                            else:
                                p.op('dve', lambda v, g=g, half=half: v.tensor_copy(out=hy[:, g, half * 512:(half + 1) * 512],
                                                                                    in_=bk[:]), r=[bkey], w=[('hy', u), 'hy2'])
                    if t + 1 < NT:
                        load_x_scratch(t + 1)
                    p.dma('sp', y_out[t * TS:(t + 1) * TS, :].rearrange("(g p) d -> p g d", p=128), hy[:],
                          r=[('hy', 0), ('hy', 1)])

                stgb = [sbT(f"stgb{i}", [128, TS], BF16) for i in range(2)]
                sqn = [sbT(f"sqn{i}", [128, SUB], BF16) for i in range(2)]
                qnb = [sbT(f"qnb{i}", [128, SUB], BF16) for i in range(2)]
                rope_sb = [sbT(f"rope_sb{i}", [128, 2, SUB], F32) for i in range(2)]
                ostg = sq2[1][:, 0:4, :].bitcast(F32).rearrange("p a (b c) -> p (a b) c", b=2)
                ostg2 = sq2[1][:, 4:8, :].bitcast(F32).rearrange("p a (b c) -> p (a b) c", b=2)
                OSTG_KEYS = [('sq', 1, m) for m in range(8)]
                tstg = aT[:, 0:9, :].rearrange("p a b -> p (a b)").rearrange("p (g n) -> p g n", g=8)
                TSTG_KEYS = [('aT', j, u) for j in range(9) for u in range(NS)]
                cnts.update({'sg': 0, 'sn': 0, 'qb': 0})

                def wflat(wb):
                    return wb[:].rearrange("p k a b -> p k (a b)")

                def proj_fm(W, col0, nch, t, evac):
                    wi = rot('wb', NWB)
                    wb = wflat(wbuf[wi])
                    p.dma('pool', wb[:, :, 0:nch * 128], W[:, col0:col0 + nch * 128].rearrange("(k p) n -> p k n", p=128),
                          w=[('wbuf', wi)])
                    if 1 in HPEND:
                        ensure_h(0)
                    for ci in range(nch):
                        for u in range(NS):
                            ensure_h(u)
                            cols = slice(u * SUB, (u + 1) * SUB)
                            bk, bkey = p.bank()
                            p.mmg([(lambda tt, k=k: tt.matmul(bk[:], lhsT=wb[:, k, ci * 128:(ci + 1) * 128], rhs=hTv(k, u),
                                                              start=(k == 0), stop=(k == 7))) for k in range(8)],
                                  r=[('wbuf', wi), ('hy', u)], w=[bkey])
                            evac(ci, u, bk, bkey)

                def proj_tm(W, col0, ncol, evac):
                    wi = rot('wb', NWB)
                    wb = wflat(wbuf[wi])
                    p.dma('pool', wb[:, :, 0:ncol], W[:, col0:col0 + ncol].rearrange("(k p) n -> p k n", p=128),
                          w=[('wbuf', wi)])
                    for g in range(8):
                        u = g // 4
                        ensure_h(u)
                        bk, bkey = p.bank()
                        p.mmg([(lambda tt, k=k: tt.matmul(bk[:, 0:ncol], lhsT=hTv(k, u)[:, (g % 4) * 128:(g % 4 + 1) * 128], rhs=wb[:, k, 0:ncol],
                                                          start=(k == 0), stop=(k == 7))) for k in range(8)],
                              r=[('wbuf', wi), ('hy', u)], w=[bkey])
                        evac(g, bk, bkey)

                def fm_store(FM, chunk, t, si):
                    p.dma('sp', FM[chunk * 128:(chunk + 1) * 128, t * TS:(t + 1) * TS], stgb[si][:], r=[('stgb', si)])

                def simple_evac(FM, chunk0, t, kind):
                    st = {}

                    def ev(ci, u, bk, bkey):
                        if u == 0:
                            st['si'] = rot('sg', 2)
                        si = st['si']
                        cols = slice(u * SUB, (u + 1) * SUB)
                        if kind == 'copy':
                            p.op('act', lambda a: a.copy(out=stgb[si][:, cols], in_=bk[:]), r=[bkey], w=[('stgb', si)])
                        elif kind == 'scale':
                            p.op('act', lambda a: a.mul(out=stgb[si][:, cols], in_=bk[:], mul=0.125), r=[bkey], w=[('stgb', si)])
                        elif kind == 'silu':
                            p.op('act', lambda a: a.activation(out=stgb[si][:, cols], in_=bk[:], func=AF.Silu), r=[bkey],
                                 w=[('stgb', si)])
                        if u == NS - 1:
                            fm_store(FM, chunk0 + ci, t, si)
                    return ev

                def normrope_evac(FM, chunk0, t, c, wcol, outk):
                    st = {}

                    def ev(ci, u, bk, bkey):
                        if u == 0:
                            st['si'] = rot('sg', 2)
                        si = st['si']
                        cols = slice(u * SUB, (u + 1) * SUB)
                        t0 = rot('tf', 2)
                        p.op('act', lambda a: a.copy(out=tmpf[t0][:], in_=bk[:]), r=[bkey], w=[('tmpf', t0)])
                        sn = rot('sn', 2)
                        p.op('act', lambda a: a.activation(out=sqn[sn][:], in_=tmpf[t0][:], func=AF.Square),
                             r=[('tmpf', t0)], w=[('sqn', sn)])
                        b2, b2k = p.bank()
                        p.mmg([lambda tt: tt.matmul(b2[:], lhsT=bones[:], rhs=sqn[sn][:], start=True, stop=True)],
                              r=[('sqn', sn), 'bones'], w=[b2k])
                        ri = rot('rs', 2)
                        p.op('act', lambda a: a.activation(out=rstd[ri][:], in_=b2[:], func=AF.Ln, bias=eps_t[:], scale=1.0),
                             r=[b2k, 'eps_t'], w=[('rstd', ri)])
                        p.op('act', lambda a: a.activation(out=rstd[ri][:], in_=rstd[ri][:], func=AF.Exp, scale=-0.5),
                             r=[('rstd', ri)], w=[('rstd', ri)])
                        p.op('dve', lambda v: v.scalar_tensor_tensor(out=tmpf[t0][:], in0=tmpf[t0][:], scalar=hv[:, wcol:wcol + 1],
                                                                     in1=rstd[ri][:], op0=ALU.mult, op1=ALU.mult),
                             r=[('tmpf', t0), ('rstd', ri), 'hv'], w=[('tmpf', t0)])
                        if outk and c == 1:
                            b3, b3k = p.bank()
                            p.mmg([(lambda tt, g=g: tt.transpose(b3[:, g * 128:(g + 1) * 128], tmpf[t0][:, g * 128:(g + 1) * 128],
                                                                 ident_f[:])) for g in range(4)],
                                  r=[('tmpf', t0), 'ident_f'], w=[b3k])
                            p.op('dve', lambda v: v.tensor_copy(out=ostg[:, u * 4:(u + 1) * 4, :],
                                                                in_=b3[:].rearrange("p (g n) -> p g n", g=4)),
                                 r=[b3k], w=OSTG_KEYS)
                            if u == NS - 1:
                                p.dma('sp', o_gk.rearrange("(g p) n -> p g n", p=128), ostg, r=OSTG_KEYS)
                        if c == 0:
                            qi = rot('qb', 2)
                            p.op('act', lambda a: a.copy(out=qnb[qi][:], in_=tmpf[t0][:]), r=[('tmpf', t0)], w=[('qnb', qi)])
                            b3, b3k = p.bank()
                            p.mmg([lambda tt: tt.matmul(b3[:], lhsT=pm_b[:], rhs=qnb[qi][:], start=True, stop=True)],
                                  r=[('qnb', qi), 'pm_b'], w=[b3k])
                            t1 = rot('tf', 2)
                            p.op('dve', lambda v: v.tensor_tensor(out=tmpf[t1][:], in0=b3[:], in1=rope_sb[u][:, 1, :], op=ALU.mult),
                                 r=[b3k, ('rope', u)], w=[('tmpf', t1)])
                            p.op('dve', lambda v: v.tensor_tensor(out=tmpf[t0][:], in0=tmpf[t0][:], in1=rope_sb[u][:, 0, :],
                                                                  op=ALU.mult), r=[('tmpf', t0), ('rope', u)], w=[('tmpf', t0)])
                            p.op('dve', lambda v: v.tensor_tensor(out=stgb[si][:, cols], in0=tmpf[t0][:], in1=tmpf[t1][:],
                                                                  op=ALU.add), r=[('tmpf', t0), ('tmpf', t1)], w=[('stgb', si)])
                        else:
                            p.op('act', lambda a: a.copy(out=stgb[si][:, cols], in_=tmpf[t0][:]), r=[('tmpf', t0)],
                                 w=[('stgb', si)])
                        if u == NS - 1:
                            fm_store(FM, chunk0 + ci, t, si)
                    return ev

                def inproj_ab(t, c):
                    W = ab_w_in
                    if c == 0:
                        for u in range(NS):
                            p.dma('sp', rope_sb[u][:], ropeT[:, :, t * TS + u * SUB:t * TS + (u + 1) * SUB], w=[('rope', u)])
                    proj_fm(W, 0, 4, t, simple_evac(FM0, 0, t, 'copy'))
                    proj_fm(W, 512, 4, t, simple_evac(FM0, 4, t, 'scale'))
                    proj_fm(W, 1536, 4, t, simple_evac(FM0, 8, t, 'silu'))
                    proj_fm(W, 2048, 4, t, normrope_evac(FM0, 12, t, c, 0, False))
                    proj_fm(W, 2560, 1, t, normrope_evac(FM0, 16, t, c, 1, True))

                    def ev_k(g, bk, bkey):
                        p.op('act', lambda a: a.mul(out=tstg[:, g, 0:512], in_=bk[:], mul=0.125), r=[bkey], w=TSTG_KEYS)

                    def ev_v(g, bk, bkey):
                        p.op('dve', lambda v: v.tensor_copy(out=tstg[:, g, 512:1024], in_=bk[:]), r=[bkey], w=TSTG_KEYS)

                    def ev_gv(g, bk, bkey):
                        p.op('dve', lambda v: v.tensor_copy(out=tstg[:, g, 1024:1152], in_=bk[:, 0:128]), r=[bkey], w=TSTG_KEYS)
                        if c == 1:
                            p.op('dve', lambda v: v.tensor_copy(out=ostg2[:, g, :], in_=bk[:, 0:128]), r=[bkey], w=OSTG_KEYS)
                    proj_tm(W, 512, 512, ev_k)
                    proj_tm(W, 1024, 512, ev_v)
                    proj_tm(W, 2688, 128, ev_gv)
                    if c == 1:
                        p.dma('sp', o_gv.rearrange("(g p) n -> p g n", p=128), ostg2, r=OSTG_KEYS)
                    p.dma('sp', TM0[t * TS:(t + 1) * TS, :].rearrange("(g p) n -> p g n", p=128), tstg, r=TSTG_KEYS)

                if which == 'A':
                    for t in range(NT):
                        c = 0 if t < 4 else 1
                        load_x_tokmajor(t)
                        PRE['next'] = (0, 1, c)
                        ffn(0, 0, 0, c)
                        prenorm(0, 1, c)
                        inproj_ab(t, c)
                        p.dma('sp', XS.rearrange("(k p) n -> p k n", p=128)[:, :, t * TS:(t + 1) * TS], xT[:],
                              r=[('xT', k, u) for k in range(8) for u in range(NS)])
                    p.barrier()
                def load_x_scratch(t):
                    p.dma('sp', xT[:], XS.rearrange("(k p) n -> p k n", p=128)[:, :, t * TS:(t + 1) * TS],
                          w=[('xT', k, u) for k in range(8) for u in range(NS)])

                def load_ycat(t):
                    p.dma('sp', aT[:, 0:8, :], YC.rearrange("(k p) n -> p k n", p=128)[:, :, t * TS:(t + 1) * TS],
                          w=[('aT', j, u) for j in range(8) for u in range(NS)])

                def outproj(Wo, l, c):
                    omap = {}

                    def load_out(mp):
                        wi = rot('wo', 2)
                        p.dma('pool', wobuf[wi][:, 0:8, :], Wo[:, mp * 256:(mp + 1) * 256].rearrange("(j p) n -> p j n", p=128),
                              w=[('wobuf', wi)])
                        omap[mp] = wi

                    def mmo(mp, u):
                        wi = omap[mp]
                        wo = wobuf[wi]
                        for mm in range(2):
                            m = mp * 2 + mm
                            cols = slice(u * SUB, (u + 1) * SUB)
                            bk, bkey = p.bank()
                            p.mmg([(lambda t_, j=j: t_.matmul(bk[:], lhsT=wo[:, j, mm * 128:(mm + 1) * 128], rhs=aT[:, j, cols],
                                                              start=(j == 0), stop=(j == 7))) for j in range(8)],
                                  r=[('wobuf', wi)] + [('aT', j, u) for j in range(8)], w=[bkey])
                            evac2(bk, bkey, m, u)
                    load_out(0)
                    load_out(1)
                    mmo(0, 0)
                    mmo(0, 1)
                    load_out(2)
                    mmo(1, 0)
                    mmo(1, 1)
                    load_out(3)
                    mmo(2, 0)
                    mmo(3, 0)
                    finish(l, 1, c, 0)
                    mmo(2, 1)
                    early_pre0()
                    mmo(3, 1)
                    finish(l, 1, c, 1)

                tstg1 = aT[:, 0:16, :].rearrange("p a b -> p (a b)").rearrange("p (g h e) -> p g h e", g=8, h=16)
                TSTG1_KEYS = [('aT', j, u) for j in range(16) for u in range(NS)]
                def ostgN(gq, cb):
                    return hy[:, gq * 2 + cb, :].rearrange("p (u h w) -> p u h w", u=2, h=2)[:, :, 1, :]
                ostgN_all = bass.AP(hy, 256, [[8 * TS, 128], [512, 16], [1, 256]])

                def inproj_na(t, c):
                    W = na_w_qkv
                    proj_fm(W, 0, 4, t, simple_evac(FM1, 0, t, 'copy'))
                    proj_fm(W, 512, 4, t, simple_evac(FM1, 4, t, 'copy'))
                    proj_fm(W, 1024, 4, t, simple_evac(FM1, 8, t, 'copy'))
                    proj_fm(W, 1536, 4, t, simple_evac(FM1, 12, t, 'copy'))
                    for g in range(8):
                        p.op('dve', lambda v, g=g: v.memset(tstg1[:, g, :, 64:128], 1.0), w=TSTG1_KEYS)
                    for cb in range(2):
                        def ev_v(g, bk, bkey, cb=cb):
                            p.op('dve', lambda v: v.tensor_copy(out=tstg1[:, g, cb * 8:(cb + 1) * 8, 0:64],
                                                                in_=bk[:].rearrange("p (h e) -> p h e", h=8)), r=[bkey],
                                 w=TSTG1_KEYS)
                        proj_tm(W, 2048 + cb * 512, 512, ev_v)
                    p.dma('sp', TM1[t * TS:(t + 1) * TS, :, :].rearrange("(g p) h e -> p g (h e)", p=128),
                          tstg1.rearrange("p g h e -> p g (h e)"), r=TSTG1_KEYS)
                    if c == 1:
                        for which, o_d in ((1, o_nk), (2, o_nv)):
                            for gh in range(2):
                                for cb in range(2):
                                    def ev_o(g, bk, bkey, cb=cb, gh=gh):
                                        if g // 4 == gh:
                                            p.op('act', lambda a: a.copy(out=ostgN(g % 4, cb), in_=bk[:].rearrange("p (u w) -> p u w", u=2)),
                                                 r=[bkey], w=['hy2'])
                                    proj_tm(W, which * 1024 + cb * 512, 512, ev_o)
                                for gq in range(4):
                                    p.dma('sp', o_d[gh * 512 + gq * 128:gh * 512 + (gq + 1) * 128, :].rearrange("p (q w) -> p q w", w=256),
                                          bass.AP(hy, 256 + gq * 2048, [[8 * TS, 128], [512, 4], [1, 256]]), r=['hy2'])

                if which == 'C':
                    load_x_scratch(0)
                    for t in range(NT):
                        c = 0 if t < 4 else 1
                        load_ycat(t)
                        PRE['next'] = (0, 2, c)
                        outproj(ab_w_out, 0, c)
                        PRE['next'] = (1, 0, c)
                        ffn(0, 1, 2, c)
                        PRE['next'] = (1, 1, c)
                        ffn(1, 0, 0, c)
                        prenorm(1, 1, c)
                        ensure_h(0)
                        ensure_h(1)
                        p.dma('sp', XS.rearrange("(k p) n -> p k n", p=128)[:, :, t * TS:(t + 1) * TS], xT[:],
                              r=[('xT', k, u) for k in range(8) for u in range(NS)])
                        if t + 1 < NT:
                            load_x_scratch(t + 1)
                        inproj_na(t, c)
                    p.barrier()
                if which == 'E':
                    load_x_scratch(0)
                    load_ycat(0)
                    for t in range(NT):
                        c = 0 if t < 4 else 1
                        PRE['next'] = (1, 2, c)
                        outproj(na_w_out, 1, c)
                        ffn(1, 1, 2, c, after_mm1=(lambda t=t: load_ycat(t + 1)) if t + 1 < NT else None)
                        store_y_tokmajor(t)
                    p.barrier()
        tile_scope('A')
        with ExitStack() as esB:
            sbB = lambda name, shape, dtype: esB.enter_context(nc.sbuf_tensor(name, shape, dtype))
            rc_sb = sbB("rc_sb", [128, 6 * 128 + 66], F32)
            n1, cn, dpos, dneg = rc_sb[:, 0:128], rc_sb[:, 128:256], rc_sb[:, 256:384], rc_sb[:, 384:512]
            mge, mlt = rc_sb[:, 512:640], rc_sb[:, 640:768]
            posf, posb, c128 = rc_sb[:, 768:769], rc_sb[:, 769:770], rc_sb[:, 770:834]
            lg = sbB("lg", [128, 2, 8], F32)
            DM = sbB("DM", [128, 8, 128], F32)
            tmpd = sbB("tmpd", [128, 128], F32)
            QF = sbB("QF", [128, 4, 128], F32)
            QB = sbB("QB", [128, 4, 128], F32)
            KF = sbB("KF", [128, 8], F32)
            KB = sbB("KB", [128, 8], F32)
            CDF = sbB("CDF", [128, 4, 64], F32)
            CDB = sbB("CDB", [128, 4, 64], F32)
            eps5 = sbB("eps5", [128, 1], F32)
            p.op('dve', lambda v: v.memset(eps5[:], 1e-5), w=['eps5'])
            p.dma('sp', rc_sb[:], rc, w=['rc'])
            p.dma('sp', lg[:].rearrange("p a b -> p (a b)"), bass.AP(dec_in.tensor, 0, [[0, 128], [1, 16]]), w=['lg'])
            lgf = lg[:].rearrange("p a b -> p (a b)")
            p.op('act', lambda a: a.activation(out=lgf, in_=lgf, func=AF.Exp, scale=-1.0), r=['lg'], w=['lg'])
            p.op('dve', lambda v: v.tensor_scalar_add(out=lgf, in0=lgf, scalar1=1.0), r=['lg'], w=['lg'])
            p.op('act', lambda a: a.activation(out=lgf, in_=lgf, func=AF.Ln), r=['lg'], w=['lg'])
            p.op('dve', lambda v: v.tensor_scalar_mul(out=lgf, in0=lgf, scalar1=-1.0), r=['lg'], w=['lg'])
            for h in range(8):
                par, pr = h % 2, h // 2
                rows = slice(par * 64, par * 64 + 64)
                p.op('act', lambda a: a.activation(out=QF[rows, pr, :], in_=n1[rows, :], func=AF.Exp, scale=lg[rows, 0, h:h + 1]),
                     r=['lg', 'rc'], w=['tab'])
                p.op('act', lambda a: a.activation(out=QB[rows, pr, :], in_=cn[rows, :], func=AF.Exp, scale=lg[rows, 1, h:h + 1]),
                     r=['lg', 'rc'], w=['tab'])
                p.op('act', lambda a: a.activation(out=CDF[rows, pr, :], in_=c128[rows, :], func=AF.Exp, scale=lg[rows, 0, h:h + 1]),
                     r=['lg', 'rc'], w=['tab'])
                p.op('act', lambda a: a.activation(out=CDB[rows, pr, :], in_=c128[rows, :], func=AF.Exp, scale=lg[rows, 1, h:h + 1]),
                     r=['lg', 'rc'], w=['tab'])
                p.op('act', lambda a: a.activation(out=DM[:, h, :], in_=dpos, func=AF.Exp, scale=lg[:, 0, h:h + 1]),
                     r=['lg', 'rc'], w=[('DM', h)])
                p.op('dve', lambda v: v.tensor_tensor(out=DM[:, h, :], in0=DM[:, h, :], in1=mge, op=ALU.mult),
                     r=[('DM', h), 'rc'], w=[('DM', h)])
                p.op('act', lambda a: a.activation(out=tmpd[:], in_=dneg, func=AF.Exp, scale=lg[:, 1, h:h + 1]),
                     r=['lg', 'rc'], w=['tmpd'])
                p.op('dve', lambda v: v.tensor_tensor(out=tmpd[:], in0=tmpd[:], in1=mlt, op=ALU.mult),
                     r=['tmpd', 'rc'], w=['tmpd'])
                p.op('dve', lambda v: v.tensor_tensor(out=DM[:, h, :], in0=DM[:, h, :], in1=tmpd[:], op=ALU.add),
                     r=[('DM', h), 'tmpd'], w=[('DM', h)])
            p.op('act', lambda a: a.activation(out=KF[:], in_=lg[:, 0, :], func=AF.Exp, scale=posf), r=['lg', 'rc'], w=['tab'])
            p.op('act', lambda a: a.activation(out=KB[:], in_=lg[:, 1, :], func=AF.Exp, scale=posb), r=['lg', 'rc'], w=['tab'])

            PS = [sbB("PSf", [128, 32, 4, 64], BF16), sbB("PSb", [128, 32, 4, 64], BF16)]
            stt = [sbB("stf", [128, 4, 64], F32), sbB("stb", [128, 4, 64], F32)]
            kvin = [sbB(f"kvin{i}", [128, 1024], BF16) for i in range(3)]
            kd = [sbB(f"kd{i}", [128, 512], BF16) for i in range(2)]
            qk_sb = [sbB(f"qk_sb{i}", [128, 8, 512], BF16) for i in range(2)]
            sg_sb = [sbB(f"sg_sb{i}", [128, 4, 512], BF16) for i in range(2)]
            v_sb = [sbB(f"v_sb{i}", [128, 4, 512], BF16) for i in range(2)]
            qfd = [sbB(f"qfd{i}", [128, 4, 128], BF16) for i in range(2)]
            qbd = [sbB(f"qbd{i}", [128, 4, 128], BF16) for i in range(2)]
            qm = [sbB(f"qm{i}", [128, 2, 4, 128], BF16) for i in range(2)]
            ptb = [sbB(f"ptb{i}", [128, 512], BF16) for i in range(4)]
            gnf = [sbB(f"gnf{i}", [128, 512], F32) for i in range(2)]
            gnm = [sbB(f"gnm{i}", [128, 512], F32) for i in range(2)]
            gnb = [sbB(f"gnb{i}", [128, 512], BF16) for i in range(2)]
            gns = [sbB(f"gns{i}", [128, 512], BF16) for i in range(2)]
            gnr = [sbB(f"gnr{i}", [128, 512], F32) for i in range(2)]
            ystg = [sbB(f"ystg{i}", [128, 4, 512], BF16) for i in range(2)]
            cnts = {'kv': 0, 'kd': 0, 'blk': 0, 'qd': 0, 'pt': 0, 'gn': 0, 'sbk': 0, 'rb': 0}

            def rot(name, n):
                i = cnts[name] % n
                cnts[name] += 1
                return i
            st_in = [st_f_in, st_b_in]
            seqs = [(0, 32, None)] + [(4096 + 256 * b, 2, b) for b in range(4)]
            DBGB = int(os.environ.get("KDEBUGB", "9"))
            o_st = [o_rf, o_rb]
            KT = [KF, KB]
            CD = [CDF, CDB]

            def sbank():
                i = cnts['sbk'] % 4
                cnts['sbk'] += 1
                return p.banks[i], ('ps', i)

            def ret_states(T0, NCH, bidx, banks=None):
                for d in range(2):
                    st = stt[d]
                    if bidx is None:
                        for par in range(2):
                            p.dma('sp', st[par * 64:(par + 1) * 64, :, :],
                                  st_in[d].rearrange("(pr par) dk dv -> par dk pr dv", par=2)[par], w=[('st', d)])
                    else:
                        p.op('dve', lambda v: v.memset(st[:], 0.0), w=[('st', d)])
                    order = range(NCH) if d == 0 else range(NCH - 1, -1, -1)
                    for cc in order:
                        ki = rot('kv', 3)
                        p.dma('sp', kvin[ki][:], TM0[T0 + cc * 128:T0 + (cc + 1) * 128, 0:1024], w=[('kvin', ki)])
                        p.op('act', lambda a: a.copy(out=PS[d][:, cc, :, :], in_=st[:]), r=[('st', d)], w=[('PS', d, cc)])
                        di = rot('kd', 2)
                        p.op('dve', lambda v: v.tensor_tensor(
                            out=kd[di][:].rearrange("p (h e) -> p h e", h=8),
                            in0=kvin[ki][:, 0:512].rearrange("p (h e) -> p h e", h=8),
                            in1=KT[d][:].unsqueeze(2).to_broadcast([128, 8, 64]), op=ALU.mult),
                            r=[('kvin', ki), 'tab'], w=[('kd', di)])
                        if banks is None:
                            bk, bkey = sbank()
                        else:
                            bi_ = banks[rot('rb', 2) % len(banks)]
                            bk, bkey = p.banks[bi_], ('ps', bi_)
                        p.mmg([(lambda tt, h=h: tt.matmul(
                            bk[(h % 2) * 64:(h % 2) * 64 + 64, (h // 2) * 64:(h // 2) * 64 + 64],
                            lhsT=kd[di][:, h * 64:(h + 1) * 64], rhs=kvin[ki][:, 512 + h * 64:512 + (h + 1) * 64],
                            start=True, stop=True)) for h in range(8)], r=[('kd', di), ('kvin', ki)], w=[bkey])
                        p.op('dve', lambda v: v.tensor_tensor(out=st[:], in0=st[:], in1=CD[d][:], op=ALU.mult),
                             r=[('st', d), 'tab'], w=[('st', d)])
                        p.op('dve', lambda v: v.tensor_tensor(
                            out=st[:].rearrange("p a b -> p (a b)"), in0=bk[:, 0:256],
                            in1=st[:].rearrange("p a b -> p (a b)"), op=ALU.add), r=[bkey, ('st', d)], w=[('st', d)])
                        yield
                    if bidx is not None:
                        for par in range(2):
                            p.dma('sp', o_st[d][bidx].rearrange("(pr par) dk dv -> par dk pr dv", par=2)[par],
                                  st[par * 64:(par + 1) * 64, :, :], r=[('st', d)])

            def ret_out(T0, NCH):
                CPB = min(4, NCH)
                rfifo = []
                for blk in range(NCH // CPB):
                    N = CPB * 128
                    bi = rot('blk', 2)
                    c0 = T0 + blk * N
                    p.dma('sp', qk_sb[bi][:, :, 0:N], FM0[0:1024, c0:c0 + N].rearrange("(k p) n -> p k n", p=128),
                          w=[('qk', bi)])
                    p.dma('sp', sg_sb[bi][:, :, 0:N], FM0[1024:1536, c0:c0 + N].rearrange("(k p) n -> p k n", p=128),
                          w=[('sgs', bi)])
                    p.dma('sp', v_sb[bi][:, 0:CPB, :], TM0[c0:c0 + N, 512:1024].rearrange("(ci p) n -> p ci n", p=128),
                          w=[('vs', bi)])
                    OB = [(p.banks[4 + pr], ('ps', 4 + pr)) for pr in range(4)]
                    for ci in range(CPB):
                        cc = blk * CPB + ci
                        ccols = slice(ci * 128, (ci + 1) * 128)
                        qi = rot('qd', 2)
                        p.op('dve', lambda v: v.tensor_tensor(out=qfd[qi][:], in0=qk_sb[bi][:, 0:4, ccols], in1=QF[:], op=ALU.mult),
                             r=[('qk', bi), 'tab'], w=[('qfd', qi)])
                        p.op('dve', lambda v: v.tensor_tensor(out=qbd[qi][:], in0=qk_sb[bi][:, 0:4, ccols], in1=QB[:], op=ALU.mult),
                             r=[('qk', bi), 'tab'], w=[('qbd', qi)])
                        for par in range(2):
                            p.op('dve', lambda v, par=par: v.tensor_scalar_mul(
                                out=qm[qi][:, par, :, :], in0=qk_sb[bi][:, 0:4, ccols], scalar1=hv[:, 6 + par:7 + par]),
                                r=[('qk', bi), 'hv'], w=[('qm', qi)])
                        pts = []
                        for hb in range(2):
                            bk, bkey = sbank()
                            p.mmg([(lambda tt, hh=hh: tt.matmul(
                                bk[:, hh * 128:(hh + 1) * 128],
                                lhsT=qk_sb[bi][:, 4 + (hb * 4 + hh) // 2, ccols],
                                rhs=qm[qi][:, (hb * 4 + hh) % 2, (hb * 4 + hh) // 2, :],
                                start=True, stop=True)) for hh in range(4)], r=[('qk', bi), ('qm', qi)], w=[bkey])
                            pi = rot('pt', 4)
                            p.op('dve', lambda v, pi=pi, bk=bk: v.tensor_tensor(
                                out=ptb[pi][:], in0=bk[:], in1=DM[:, hb * 4:(hb + 1) * 4, :].rearrange("p a b -> p (a b)"),
                                op=ALU.mult), r=[bkey] + [('DM', hb * 4 + i) for i in range(4)], w=[('ptb', pi)])
                            pts.append(pi)
                        def stB(bi=bi, ci=ci, cc=cc, ccols=ccols, qi=qi, pts=pts, OB=OB):
                            for h in range(8 if DBGB >= 4 else 0):
                                par, pr = h % 2, h // 2
                                rows = slice(par * 64, par * 64 + 64)
                                ob, obk = OB[pr]
                                pi = pts[h // 4]
                                hh = h % 4
                                p.mmg([
                                    lambda tt: tt.matmul(ob[rows, ccols], lhsT=v_sb[bi][:, ci, h * 64:(h + 1) * 64],
                                                         rhs=ptb[pi][:, hh * 128:(hh + 1) * 128], start=True, stop=False),
                                    lambda tt: tt.matmul(ob[rows, ccols], lhsT=PS[0][rows, cc, pr, :], rhs=qfd[qi][rows, pr, :],
                                                         start=False, stop=False),
                                    lambda tt: tt.matmul(ob[rows, ccols], lhsT=PS[1][rows, cc, pr, :], rhs=qbd[qi][rows, pr, :],
                                                         start=False, stop=True)],
                                    r=[('vs', bi), ('ptb', pi), ('PS', 0, cc), ('PS', 1, cc), ('qfd', qi), ('qbd', qi)], w=[obk])
                        rfifo.append(stB)
                        if ci == CPB - 1:
                            def stGN(bi=bi, N=N, c0=c0, OB=OB):
                                yi = rot('gn', 2)
                                for pr in range(4 if DBGB >= 5 else 0):
                                    ob, obk = OB[pr]
                                    gi = pr % 2
                                    p.op('act', lambda a: a.copy(out=gnf[gi][:, 0:N], in_=ob[:, 0:N]), r=[obk], w=[('gnf', gi)])
                                    p.op('act', lambda a: a.copy(out=gnb[gi][:, 0:N], in_=gnf[gi][:, 0:N]), r=[('gnf', gi)], w=[('gnb', gi)])
                                    p.op('act', lambda a: a.activation(out=gns[gi][:, 0:N], in_=gnf[gi][:, 0:N], func=AF.Square),
                                         r=[('gnf', gi)], w=[('gns', gi)])
                                    bm, bmk = sbank()
                                    p.mmg([lambda tt: tt.matmul(bm[:, 0:N], lhsT=bones[:], rhs=gnb[gi][:, 0:N], start=True, stop=True)],
                                          r=[('gnb', gi), 'bones'], w=[bmk])
                                    bq, bqk = sbank()
                                    p.mmg([lambda tt: tt.matmul(bq[:, 0:N], lhsT=bones[:], rhs=gns[gi][:, 0:N], start=True, stop=True)],
                                          r=[('gns', gi), 'bones'], w=[bqk])
                                    p.op('act', lambda a: a.copy(out=gnm[gi][:, 0:N], in_=bm[:, 0:N]), r=[bmk], w=[('gnm', gi)])
                                    p.op('dve', lambda v: v.tensor_tensor(out=gnr[gi][:, 0:N], in0=gnm[gi][:, 0:N], in1=gnm[gi][:, 0:N],
                                                                          op=ALU.mult), r=[('gnm', gi)], w=[('gnr', gi)])
                                    p.op('dve', lambda v: v.tensor_tensor(out=gnr[gi][:, 0:N], in0=bq[:, 0:N], in1=gnr[gi][:, 0:N],
                                                                          op=ALU.subtract), r=[bqk, ('gnr', gi)], w=[('gnr', gi)])
                                    p.op('dve', lambda v: v.tensor_scalar_max(out=gnr[gi][:, 0:N], in0=gnr[gi][:, 0:N], scalar1=0.0),
                                         r=[('gnr', gi)], w=[('gnr', gi)])
                                    p.op('act', lambda a: a.activation(out=gnr[gi][:, 0:N], in_=gnr[gi][:, 0:N], func=AF.Ln, bias=eps5[:],
                                                                       scale=1.0), r=[('gnr', gi), 'eps5'], w=[('gnr', gi)])
                                    p.op('act', lambda a: a.activation(out=gnr[gi][:, 0:N], in_=gnr[gi][:, 0:N], func=AF.Exp, scale=-0.5),
                                         r=[('gnr', gi)], w=[('gnr', gi)])
                                    p.op('dve', lambda v: v.tensor_tensor(out=gnf[gi][:, 0:N], in0=gnf[gi][:, 0:N], in1=gnm[gi][:, 0:N],
                                                                          op=ALU.subtract), r=[('gnf', gi), ('gnm', gi)], w=[('gnf', gi)])
                                    p.op('dve', lambda v: v.tensor_tensor(out=gnf[gi][:, 0:N], in0=gnf[gi][:, 0:N], in1=gnr[gi][:, 0:N],
                                                                          op=ALU.mult), r=[('gnf', gi), ('gnr', gi)], w=[('gnf', gi)])
                                    p.op('dve', lambda v: v.scalar_tensor_tensor(
                                        out=ystg[yi][:, pr, 0:N], in0=gnf[gi][:, 0:N], scalar=hv[:, 2 + pr:3 + pr], in1=sg_sb[bi][:, pr, 0:N],
                                        op0=ALU.mult, op1=ALU.mult), r=[('gnf', gi), ('sgs', bi), 'hv'], w=[('ystg', yi)])
                                if DBGB >= 5:
                                    p.dma('sp', YC[0:512, c0:c0 + N].rearrange("(k p) n -> p k n", p=128), ystg[yi][:, :, 0:N],
                                          r=[('ystg', yi)])
                            rfifo.append(stGN)
                        while len(rfifo) > (2 if ci == CPB - 1 else 1):
                            rfifo.pop(0)()
                while rfifo:
                    rfifo.pop(0)()

            ptq = [sbB(f"ptq{i}", [128, 2, 512], BF16) for i in range(4)]
            q_sb = [sbB(f"q_sb{i}", [128, 4, 512], BF16) for i in range(2)]
            qmk = [sbB(f"qmk{i}", [128, 2, 4, 512], BF16) for i in range(1)]
            rec = [sbB(f"rec{i}", [128, 512], F32) for i in range(2)]
            yst = [sbB(f"yst{i}", [128, 4, 512], BF16) for i in range(2)]
            cnts.update({'aq': 0, 'ap': 0, 'ar': 0, 'ay': 0, 'as': 0, 'ao': 0})

            def attention(FMq, qchunk0, nheads, kT_fn, vaug_fn, kv_keys, NKT, T0, L, QN, ychunk0, npairs=3, bg=None):
                fifo = []
                LAG = 2

                def drain(n):
                    while len(fifo) > n:
                        fifo.pop(0)()
                for qt in range(L // QN):
                    c0 = T0 + qt * QN
                    for hg in range(nheads // 8):
                        qi = rot('aq', 2)
                        p.dma('sp', q_sb[qi][:, :, 0:QN],
                              FMq[(qchunk0 + hg * 4) * 128:(qchunk0 + hg * 4 + 4) * 128, c0:c0 + QN].rearrange("(k p) n -> p k n", p=128),
                              w=[('q_sb', qi)])
                        for par in range(2):
                            p.op('dve', lambda v, par=par: v.tensor_scalar_mul(
                                out=qmk[qi % len(qmk)][:, par, :, 0:QN], in0=q_sb[qi][:, :, 0:QN], scalar1=hv[:, 6 + par:7 + par]),
                                r=[('q_sb', qi), 'hv'], w=[('qmk', qi % len(qmk))])
                        yi = rot('ay', 2)
                        for hh in range(8):
                            h = hg * 8 + hh
                            if bg is not None:
                                next(bg, None)
                            ai = 6 + rot('ao', 2)
                            acc, acck = p.banks[ai], ('ps', ai)
                            for kp in range(NKT // 2):
                                b = rot('as', npairs)
                                bkeys = [('ps', 2 * b), ('ps', 2 * b + 1)]
                                p.mmg([(lambda tt, e=e: tt.matmul(p.banks[2 * b + e][:, 0:QN], lhsT=kT_fn(h, 2 * kp + e),
                                                                  rhs=qmk[qi % len(qmk)][:, h % 2, hh // 2, 0:QN], start=True, stop=True))
                                       for e in range(2)], r=[('qmk', qi % len(qmk))] + kv_keys, w=bkeys)
                                pi = rot('ap', 4)
                                p.op('act', lambda a: a.activation(out=ptq[pi][:, :, 0:QN], in_=psum_all[:, 2 * b:2 * b + 2, 0:QN],
                                                                   func=AF.Exp, scale=0.125), r=bkeys, w=[('ptq', pi)])

                                def pv(h=h, kp=kp, pi=pi, acc=acc, acck=acck):
                                    p.mmg([(lambda tt, e=e: tt.matmul(acc[:, 0:QN], lhsT=vaug_fn(h, 2 * kp + e), rhs=ptq[pi][:, e, 0:QN],
                                                                      start=(kp == 0 and e == 0), stop=(kp == NKT // 2 - 1 and e == 1)))
                                           for e in range(2)], r=[('ptq', pi)] + kv_keys, w=[acck])
                                fifo.append(pv)
                                drain(LAG)

                            def norm(h=h, hh=hh, acc=acc, acck=acck, yi=yi):
                                ri = rot('ar', 2)
                                p.op('dve', lambda v: v.reciprocal(out=rec[ri][0:64, 0:QN], in_=acc[64:128, 0:QN]), r=[acck],
                                     w=[('rec', ri)])
                                p.op('dve', lambda v: v.tensor_tensor(
                                    out=yst[yi][(h % 2) * 64:(h % 2) * 64 + 64, hh // 2, 0:QN], in0=acc[0:64, 0:QN],
                                    in1=rec[ri][0:64, 0:QN], op=ALU.mult), r=[acck, ('rec', ri)], w=[('yst', yi)])
                            fifo.append(norm)

                        def store(hg=hg, c0=c0, yi=yi):
                            p.dma('sp', YC[(ychunk0 + hg * 4) * 128:(ychunk0 + hg * 4 + 4) * 128, c0:c0 + QN].rearrange(
                                "(k p) n -> p k n", p=128), yst[yi][:, :, 0:QN], r=[('yst', yi)])
                        fifo.append(store)
                drain(0)

            kT2 = sbB("kT2", [128, 2, 4352], BF16)
            vaug = sbB("vaug", [128, 34, 2, 128], BF16)
            ck_f = sbB("ck_f", [128, 2, 128], F32)
            ck_b = sbB("ck_b", [128, 256], BF16)
            p.dma('sp', ck_f[:], cgk.rearrange("(g p) n -> p g n", p=128), w=['ck_f'])
            bk, bkey = sbank()
            p.mmg([(lambda tt, g=g: tt.transpose(bk[:, g * 128:(g + 1) * 128], ck_f[:, g, :], ident_f[:])) for g in range(2)],
                  r=['ck_f', 'ident_f'], w=[bkey])
            p.op('act', lambda a: a.copy(out=ck_b[:], in_=bk[:, 0:256]), r=[bkey], w=['ck_b'])
            p.dma('sp', FM0[16 * 128:17 * 128, NTOK:NTOK + 256], ck_b[:], r=['ck_b'], w=['FM0c'])

            def gqa(T0, L, sample, bg):
                NK = L + (256 if sample else 0)
                NKT = NK // 128
                for kvh in range(2):
                    for half in range(2):
                        src = FM0[16 * 128 + kvh * 64:16 * 128 + kvh * 64 + 64, :]
                        p.dma('sp', kT2[half * 64:half * 64 + 64, kvh, 0:L], src[:, T0:T0 + L], w=['kT2'])
                        if sample:
                            p.dma('sp', kT2[half * 64:half * 64 + 64, kvh, L:L + 256], src[:, NTOK:NTOK + 256],
                                  r=['FM0c'], w=['kT2'])
                p.op('dve', lambda v: v.memset(vaug[:], 1.0), w=['vaug'])
                for kvh in range(2):
                    p.dma('sp', vaug[:, 0:L // 128, kvh, 0:64],
                          TM0[T0:T0 + L, 1024 + kvh * 64:1088 + kvh * 64].rearrange("(kt p) e -> p kt e", p=128), w=['vaug'])
                    if sample:
                        p.dma('pool', vaug[:, L // 128:L // 128 + 2, kvh, 0:64],
                              cgv[:, kvh * 64:(kvh + 1) * 64].rearrange("(kt p) e -> p kt e", p=128), w=['vaug'])
                attention(FM0, 12, 8, lambda h, kt: kT2[:, h // 4, kt * 128:(kt + 1) * 128],
                          lambda h, kt: vaug[:, kt, h // 4, :], ['kT2', 'vaug'], NKT, T0, L, 512 if sample else 256, 4,
                          npairs=(2 if bg is not None else 3), bg=bg)

            mwB = [sbB(f"mwB{i}", [128, 8, 256], BF16) for i in range(2)]

            def background():
                g1 = ret_states(0, 32, None, banks=[4])
                g2 = mod_layer(1, mwB, 256, p.banks[5], ('ps', 5), 'mwB')
                d1 = d2 = False
                while not (d1 and d2):
                    if not d1:
                        try:
                            next(g1)
                        except StopIteration:
                            d1 = True
                    if not d2:
                        try:
                            next(g2)
                        except StopIteration:
                            d2 = True
                    yield
            bg = background()
            gqa(0, 4096, True, bg)
            for _ in bg:
                pass
            for (T0, NCH, bidx) in seqs[1:]:
                gqa(T0, NCH * 128, False, None)
            ret_out(0, 32)
            for (T0, NCH, bidx) in seqs[1:]:
                for _ in ret_states(T0, NCH, bidx):
                    pass
                ret_out(T0, NCH)
            p.barrier()
        tile_scope('C')
        with ExitStack() as esD:
            sbD = lambda name, shape, dtype: esD.enter_context(nc.sbuf_tensor(name + '_D', shape, dtype))
            cnts = {'aq': 0, 'ap': 0, 'ar': 0, 'ay': 0, 'as': 0, 'ao': 0}

            def rot(name, n):
                i = cnts[name] % n
                cnts[name] += 1
                return i
            ptq = [sbD(f"ptq{i}", [128, 2, 512], BF16) for i in range(4)]
            q_sb = [sbD(f"q_sb{i}", [128, 4, 512], BF16) for i in range(2)]
            qmk = [sbD(f"qmk{i}", [128, 2, 4, 512], BF16) for i in range(2)]
            rec = [sbD(f"rec{i}", [128, 512], F32) for i in range(2)]
            yst = [sbD(f"yst{i}", [128, 4, 512], BF16) for i in range(2)]
            knT = sbD("knT", [128, 8, 256], BF16)
            vaugN = sbD("vaugN", [128, 2, 16, 128], BF16)
            def attention(FMq, qchunk0, nheads, kT_fn, vaug_fn, kv_keys, NKT, T0, L, QN, ychunk0, npairs=3, bg=None):
                fifo = []
                LAG = 2

                def drain(n):
                    while len(fifo) > n:
                        fifo.pop(0)()
                for qt in range(L // QN):
                    c0 = T0 + qt * QN
                    for hg in range(nheads // 8):
                        qi = rot('aq', 2)
                        p.dma('sp', q_sb[qi][:, :, 0:QN],
                              FMq[(qchunk0 + hg * 4) * 128:(qchunk0 + hg * 4 + 4) * 128, c0:c0 + QN].rearrange("(k p) n -> p k n", p=128),
                              w=[('q_sb', qi)])
                        for par in range(2):
                            p.op('dve', lambda v, par=par: v.tensor_scalar_mul(
                                out=qmk[qi % len(qmk)][:, par, :, 0:QN], in0=q_sb[qi][:, :, 0:QN], scalar1=hv[:, 6 + par:7 + par]),
                                r=[('q_sb', qi), 'hv'], w=[('qmk', qi % len(qmk))])
                        yi = rot('ay', 2)
                        for hh in range(8):
                            h = hg * 8 + hh
                            if bg is not None:
                                next(bg, None)
                            ai = 6 + rot('ao', 2)
                            acc, acck = p.banks[ai], ('ps', ai)
                            for kp in range(NKT // 2):
                                b = rot('as', npairs)
                                bkeys = [('ps', 2 * b), ('ps', 2 * b + 1)]
                                p.mmg([(lambda tt, e=e: tt.matmul(p.banks[2 * b + e][:, 0:QN], lhsT=kT_fn(h, 2 * kp + e),
                                                                  rhs=qmk[qi % len(qmk)][:, h % 2, hh // 2, 0:QN], start=True, stop=True))
                                       for e in range(2)], r=[('qmk', qi % len(qmk))] + kv_keys, w=bkeys)
                                pi = rot('ap', 4)
                                p.op('act', lambda a: a.activation(out=ptq[pi][:, :, 0:QN], in_=psum_all[:, 2 * b:2 * b + 2, 0:QN],
                                                                   func=AF.Exp, scale=0.125), r=bkeys, w=[('ptq', pi)])

                                def pv(h=h, kp=kp, pi=pi, acc=acc, acck=acck):
                                    p.mmg([(lambda tt, e=e: tt.matmul(acc[:, 0:QN], lhsT=vaug_fn(h, 2 * kp + e), rhs=ptq[pi][:, e, 0:QN],
                                                                      start=(kp == 0 and e == 0), stop=(kp == NKT // 2 - 1 and e == 1)))
                                           for e in range(2)], r=[('ptq', pi)] + kv_keys, w=[acck])
                                fifo.append(pv)
                                drain(LAG)

                            def norm(h=h, hh=hh, acc=acc, acck=acck, yi=yi):
                                ri = rot('ar', 2)
                                p.op('dve', lambda v: v.reciprocal(out=rec[ri][0:64, 0:QN], in_=acc[64:128, 0:QN]), r=[acck],
                                     w=[('rec', ri)])
                                p.op('dve', lambda v: v.tensor_tensor(
                                    out=yst[yi][(h % 2) * 64:(h % 2) * 64 + 64, hh // 2, 0:QN], in0=acc[0:64, 0:QN],
                                    in1=rec[ri][0:64, 0:QN], op=ALU.mult), r=[acck, ('rec', ri)], w=[('yst', yi)])
                            fifo.append(norm)

                        def store(hg=hg, c0=c0, yi=yi):
                            p.dma('sp', YC[(ychunk0 + hg * 4) * 128:(ychunk0 + hg * 4 + 4) * 128, c0:c0 + QN].rearrange(
                                "(k p) n -> p k n", p=128), yst[yi][:, :, 0:QN], r=[('yst', yi)])
                        fifo.append(store)
                drain(0)


            def na_prompt(T0):
                p.dma('sp', knT[:], FM1[8 * 128:16 * 128, T0:T0 + 256].rearrange("(k p) n -> p k n", p=128), w=['knT'])
                p.dma('sp', vaugN[:].rearrange("p kt h e -> p kt (h e)"),
                      TM1[T0:T0 + 256, :, :].rearrange("(kt p) h e -> p kt (h e)", p=128), w=['vaugN'])
                attention(FM1, 0, 16, lambda h, kt: knT[:, h // 2, kt * 128:(kt + 1) * 128],
                          lambda h, kt: vaugN[:, kt, h, :], ['knT', 'vaugN'], 2, T0, 256, 256, 0)
            ident_b = sbD("ident_b", [128, 128], BF16)
            p.dma('pool', ident_b[:], identb, w=['ident_b'])
            nm = sbD("nm", [128, 2, 64], F32)
            p.dma('sp', nm[:], nmask, w=['nm'])
            BT = sbD("BT", [128, 16, 14, 64], BF16)
            graw = [sbD(f"graw{i}", [128, 14, 64], F32) for i in range(2)]
            gtmp = [sbD(f"gtmp{i}", [128, 14, 64], F32) for i in range(2)]
            def build_bt(h):
                gi = h % 2
                base = 64 + h * 465 - 48
                for half in range(2):
                    p.dma('sp', graw[gi][half * 64:(half + 1) * 64, :, :],
                          bass.AP(rpb_pad.tensor, base + half * 31, [[1, 64], [31, 14], [1, 64]]), w=[('graw', gi)])
                grev = bass.AP(graw[gi], 63, [[14 * 64, 128], [64, 14], [-1, 64]])
                p.op('dve', lambda v: v.tensor_tensor(out=gtmp[gi][:], in0=grev,
                                                      in1=nm[:, 0:1, :].to_broadcast([128, 14, 64]), op=ALU.mult),
                     r=[('graw', gi), 'nm'], w=[('gtmp', gi)])
                p.op('dve', lambda v: v.scalar_tensor_tensor(out=BT[:, h, :, :], in0=gtmp[gi][:], scalar=8.0,
                                                             in1=nm[:, 1:2, :].to_broadcast([128, 14, 64]),
                                                             op0=ALU.mult, op1=ALU.add),
                     r=[('gtmp', gi), 'nm'], w=['BT'])
            ctxK = sbD("ctxK", [128, 8, 256], BF16)
            ctxV = sbD("ctxV", [128, 2, 16, 128], BF16)
            ckf = sbD("ckf", [128, 2, 1024], F32)
            p.dma('sp', ckf[:], cnk.rearrange("(g p) n -> p g n", p=128), w=['ckf'])
            for k in range(8):
                si = rot('as', 4)
                bk, bkey = p.banks[si], ('ps', si)
                p.mmg([(lambda tt, g=g: tt.transpose(bk[:, g * 128:(g + 1) * 128], ckf[:, g, k * 128:(k + 1) * 128], ident_f[:]))
                       for g in range(2)], r=['ckf', 'ident_f'], w=[bkey])
                p.op('act', lambda a: a.copy(out=ctxK[:, k, :], in_=bk[:, 0:256]), r=[bkey], w=['ctxK'])
            p.op('dve', lambda v: v.memset(ctxV[:], 1.0), w=['ctxV'])
            for kt in range(2):
                p.dma('pool', ctxV[:, kt, :, 0:64], cnv[kt * 128:(kt + 1) * 128, :].rearrange("p (h e) -> p h e", h=16),
                      w=['ctxV'])
            for b in range(4):
                na_prompt(4096 + 256 * b)
                for h in range(4 * b, 4 * b + 4):
                    build_bt(h)
            kwin = [sbD(f"kwin{i}", [128, 8, 512], BF16) for i in range(2)]
            vwin = [sbD(f"vwin{i}", [128, 4, 16, 128], BF16) for i in range(2)]
            qrow = [sbD(f"qrow{i}", [128, 8, 64], BF16) for i in range(2)]
            qrm = [sbD(f"qrm{i}", [128, 2, 8, 64], BF16) for i in range(2)]
            ptn = [sbD(f"ptn{i}", [128, 768], BF16) for i in range(4)]
            recN = [sbD(f"recN{i}", [128, 512], F32) for i in range(2)]
            yblk = [sbD(f"yblk{i}", [128, 8, 512], BF16) for i in range(2)]
            cnts.update({'pn': 0})
            nfifo = []

            def ndrain(n):
                while len(nfifo) > n:
                    nfifo.pop(0)()
            for r in range(64):
                r0 = min(max(r - 4, 0), 56)
                dl = r - r0
                wi = r % 2
                p.dma('sp', kwin[wi][:], FM1[8 * 128:16 * 128, r0 * 64:r0 * 64 + 512].rearrange("(k p) n -> p k n", p=128),
                      w=[('kwin', wi)])
                p.dma('sp', vwin[wi][:].rearrange("p kt h e -> p kt (h e)"),
                      TM1[r0 * 64:r0 * 64 + 512, :, :].rearrange("(kt p) h e -> p kt (h e)", p=128), w=[('vwin', wi)])
                p.dma('sp', qrow[wi][:], FM1[0:8 * 128, r * 64:(r + 1) * 64].rearrange("(k p) n -> p k n", p=128),
                      w=[('qrow', wi)])
                for par in range(2):
                    p.op('dve', lambda v, par=par: v.tensor_scalar_mul(out=qrm[wi][:, par, :, :], in0=qrow[wi][:],
                                                                        scalar1=hv[:, 6 + par:7 + par]),
                         r=[('qrow', wi), 'hv'], w=[('qrm', wi)])
                accs = [(p.banks[4 + 2 * wi + par], ('ps', 4 + 2 * wi + par)) for par in range(2)]
                for m in range(8):
                    si = rot('as', 2)
                    sb1, sb1k = p.banks[2 * si], ('ps', 2 * si)
                    sb2, sb2k = p.banks[2 * si + 1], ('ps', 2 * si + 1)
                    qpair = qrm[wi][:, :, m, :]
                    fns = []
                    for i in range(4):
                        j = 2 * i - dl + 7
                        fns.append(lambda tt, i=i: tt.matmul(sb1[:, i * 128:(i + 1) * 128], lhsT=kwin[wi][:, m, i * 128:(i + 1) * 128],
                                                             rhs=qpair, start=True, stop=False))
                        fns.append(lambda tt, i=i, j=j: tt.matmul(sb1[:, i * 128:(i + 1) * 128], lhsT=ident_b[:],
                                                                  rhs=BT[:, 2 * m:2 * m + 2, j, :], start=False, stop=True))
                    p.mmg(fns, r=[('kwin', wi), ('qrm', wi), 'BT', 'ident_b'], w=[sb1k])
                    p.mmg([(lambda tt, kt=kt: tt.matmul(sb2[:, kt * 128:(kt + 1) * 128], lhsT=ctxK[:, m, kt * 128:(kt + 1) * 128],
                                                        rhs=qpair, start=True, stop=True)) for kt in range(2)],
                          r=[('qrm', wi), 'ctxK'], w=[sb2k])
                    pi = rot('pn', 4)
                    p.op('act', lambda a: a.activation(out=ptn[pi][:, 0:512], in_=sb1[:], func=AF.Exp, scale=0.125),
                         r=[sb1k], w=[('ptn', pi)])
                    p.op('act', lambda a: a.activation(out=ptn[pi][:, 512:768], in_=sb2[:, 0:256], func=AF.Exp, scale=0.125),
                         r=[sb2k], w=[('ptn', pi)])

                    def pv(m=m, pi=pi, wi=wi, accs=accs):
                        for par in range(2):
                            h = 2 * m + par
                            acc, acck = accs[par]
                            fns = []
                            for i in range(4):
                                fns.append(lambda tt, i=i: tt.matmul(acc[:, m * 64:(m + 1) * 64], lhsT=vwin[wi][:, i, h, :],
                                                                     rhs=ptn[pi][:, i * 128 + par * 64:i * 128 + par * 64 + 64],
                                                                     start=(i == 0), stop=False))
                            for kt in range(2):
                                fns.append(lambda tt, kt=kt: tt.matmul(acc[:, m * 64:(m + 1) * 64], lhsT=ctxV[:, kt, h, :],
                                                                       rhs=ptn[pi][:, 512 + kt * 128 + par * 64:512 + kt * 128 + par * 64 + 64],
                                                                       start=False, stop=(kt == 1)))
                            p.mmg(fns, r=[('vwin', wi), ('ptn', pi), 'ctxV'], w=[acck])
                    nfifo.append(pv)
                    ndrain(1)

                def rownorm(r=r, accs=accs):
                    yi = (r // 8) % 2
                    for par in range(2):
                        acc, acck = accs[par]
                        ri = rot('ar', 2)
                        p.op('dve', lambda v: v.reciprocal(out=recN[ri][0:64, :], in_=acc[64:128, :]), r=[acck], w=[('recN', ri)])
                        p.op('dve', lambda v: v.tensor_tensor(
                            out=yblk[yi][par * 64:(par + 1) * 64, :, (r % 8) * 64:(r % 8 + 1) * 64],
                            in0=acc[0:64, :].rearrange("p (m q) -> p m q", m=8),
                            in1=recN[ri][0:64, :].rearrange("p (m q) -> p m q", m=8),
                            op=ALU.mult), r=[acck, ('recN', ri)], w=[('yblk', yi)])
                    if r % 8 == 7:
                        c0 = (r - 7) * 64
                        p.dma('sp', YC[:, c0:c0 + 512].rearrange("(k p) n -> p k n", p=128), yblk[yi][:], r=[('yblk', yi)])
                nfifo.append(rownorm)
            ndrain(0)
            p.barrier()
        tile_scope('E')
    return nc


_NC = None


def _consts():
    f32 = np.float32
    t = np.arange(4096)
    rows, cols = t // 64, t % 64
    inv = (10000.0 ** (-np.arange(16, dtype=np.float64) / 16))
    cosT = np.zeros((128, 4096), f32)
    sinT = np.zeros((128, 4096), f32)
    for pp in range(128):
        i = pp % 64
        pos = rows if i < 32 else cols
        ang = (pos.astype(np.float32)[:, None] * inv.astype(np.float32)[None, :])[:, i % 16].astype(np.float32)
        cosT[pp] = np.cos(ang)
        sinT[pp] = np.sin(ang)
    ropeT = np.ascontiguousarray(np.stack([cosT, sinT], 1))
    pm = np.zeros((128, 128), f32)
    for base in range(0, 128, 32):
        for j in range(16):
            pm[base + 16 + j, base + j] = -1.0
            pm[base + j, base + 16 + j] = 1.0
    m = np.arange(128)[:, None].astype(f32)
    n = np.arange(128)[None, :].astype(f32)
    rc = np.concatenate([np.broadcast_to(n + 1, (128, 128)), np.broadcast_to(128 - n, (128, 128)),
                         np.maximum(n - m, 0), np.maximum(m - n, 0), (n >= m).astype(f32), (m > n).astype(f32),
                         127 - m, m, np.full((128, 64), 128.0, f32)], 1).astype(f32)
    cc = np.arange(64)
    c0 = np.clip(cc - 8, 0, 48)
    kc = np.arange(64)[:, None]
    m01 = ((kc >= c0[None, :]) & (kc < c0[None, :] + 16)).astype(f32)
    m01 = np.concatenate([m01, m01], 0)
    nmask = np.ascontiguousarray(np.stack([m01, (m01 - 1.0) * 240000.0], 1).astype(f32))
    return ropeT, pm, np.ascontiguousarray(rc), nmask


def kernel(**inp):
    global _NC
    f32 = np.float32
    xs, xp = np.asarray(inp['x_sample'], f32), np.asarray(inp['x_prompt'], f32)
    c, c_ctx = np.asarray(inp['c'], f32), np.asarray(inp['c_ctx'], f32)
    vec12 = np.concatenate([np.asarray(inp['norm_pre'], f32).reshape(6, D), np.asarray(inp['norm_post'], f32).reshape(6, D)], 0)
    vecs = np.ascontiguousarray(vec12.reshape(12, 8, 128).transpose(2, 0, 1))
    mod_bT = np.ascontiguousarray(np.asarray(inp['mod_b'], f32).reshape(2, 72, 128).transpose(2, 0, 1))
    ropeT, pm, rc, nmask = _consts()
    rpb_pad = np.pad(np.asarray(inp['na_rpb'], f32)[0].reshape(-1), (64, 64))
    hvec = np.zeros((128, 8), f32)
    hvec[:, 0] = np.tile(np.asarray(inp['gqa_q_norm'], f32)[0], 2)
    hvec[:, 1] = np.tile(np.asarray(inp['gqa_k_norm'], f32)[0], 2)
    hvec[:, 2:6] = np.asarray(inp['ret_gn'], f32)[0].reshape(4, 128).T
    hvec[0:64, 6] = 1.0
    hvec[64:128, 7] = 1.0
    dec_in = np.ascontiguousarray(np.stack([np.asarray(inp['ret_decay_fwd'], f32)[0], np.asarray(inp['ret_decay_bwd'], f32)[0]], 0))
    shared = dict(mod_w=np.asarray(inp['mod_w'], f32), mod_bT=mod_bT, vecs=vecs,
                  ffn_w_in=np.asarray(inp['ffn_w_in'], f32), ffn_w_out=np.asarray(inp['ffn_w_out'], f32),
                  identf=np.eye(128, dtype=f32), ab_w_in=np.asarray(inp['ab_w_in'], f32)[0], hvec=hvec, ropeT=ropeT,
                  pmat=pm, rc=rc, dec_in=dec_in,
                  ab_w_out=np.asarray(inp['ab_w_out'], f32)[0], na_w_qkv=np.asarray(inp['na_w_qkv'], f32)[0],
                  na_w_out=np.asarray(inp['na_w_out'], f32)[0], rpb_pad=rpb_pad, nmask=nmask,
                  identb=np.eye(128, dtype=f32))
    in_maps = []
    for i in range(8):
        x = np.concatenate([xs[i], xp[4 * i:4 * i + 4].reshape(1024, D)], 0)
        condT = np.ascontiguousarray(np.stack([c[i], c_ctx], 1))
        in_maps.append(dict(x=np.ascontiguousarray(x), condT=condT,
                            st_f=np.ascontiguousarray(np.asarray(inp['state_ret_fwd'], f32)[i, 0]),
                            st_b=np.ascontiguousarray(np.asarray(inp['state_ret_bwd'], f32)[i, 0]),
                            cgk=np.ascontiguousarray(np.asarray(inp['cache_gqa_k'], f32)[i, 0].reshape(256, 128)),
                            cgv=np.ascontiguousarray(np.asarray(inp['cache_gqa_v'], f32)[i, 0].reshape(256, 128)),
                            cnk=np.ascontiguousarray(np.asarray(inp['cache_na_k'], f32)[i, 0].reshape(256, 1024)),
                            cnv=np.ascontiguousarray(np.asarray(inp['cache_na_v'], f32)[i, 0].reshape(256, 1024)),
                            **shared))
    if _NC is None:
        _NC = build_program()
    res = run_bass_kernel_spmd(_NC, in_maps, core_ids=list(range(8)))
    R = res.results
    ys = np.stack([r["y"][:4096] for r in R], 0)
    yp = np.concatenate([r["y"][4096:].reshape(4, 256, D) for r in R], 0)
    gk = np.concatenate([r["o_gk"].reshape(4, 1, 256, 2, 64) for r in R], 0)
    gv = np.concatenate([r["o_gv"].reshape(4, 1, 256, 2, 64) for r in R], 0)
    rf = np.concatenate([r["o_rf"].reshape(4, 1, 8, 64, 64) for r in R], 0)
    rb = np.concatenate([r["o_rb"].reshape(4, 1, 8, 64, 64) for r in R], 0)
    nk = np.concatenate([r["o_nk"].reshape(4, 1, 256, 16, 64) for r in R], 0)
    nv = np.concatenate([r["o_nv"].reshape(4, 1, 256, 16, 64) for r in R], 0)
    return yp, ys, gk, gv, rf, rb, nk, nv
```

```python
import numpy as np
from contextlib import ExitStack
import concourse.bass as bass
import concourse.mybir as mybir
from concourse.bass_utils import run_bass_kernel_spmd
import ml_dtypes

F32, BF16 = mybir.dt.float32, mybir.dt.bfloat16
AF = mybir.ActivationFunctionType
ALU = mybir.AluOpType
AX = mybir.AxisListType

D = 1024
DFF = 2816
NJ = DFF // 128
TS = 1024
SUB = 512
NS = TS // SUB
NT = 5
NTOK = NT * TS
EPS = 1e-6


class P:
    def __init__(self, nc, es):
        self.nc = nc
        self.E = {'pe': nc.tensor, 'act': nc.scalar, 'dve': nc.vector, 'pool': nc.gpsimd, 'sp': nc.sync}
        self.sems = {}
        self.cnt = {}
        for k in self.E:
            self.sems[k] = es.enter_context(nc.semaphore("s_" + k))
            self.cnt[k] = 0
        self.NDS = 12
        for q in ('pool', 'sp'):
            for i in range(self.NDS):
                key = ('d', q, i)
                self.sems[key] = es.enter_context(nc.semaphore(f"d_{q}{i}"))
                self.cnt[key] = 0
        self.dq = {'pool': 0, 'sp': 0}
        self.waited = {}
        self.W = {}
        self.R = {}
        self.bank_i = 0
        self.banks = []

    def _deps(self, e, r, w, is_dma):
        deps = {}

        def add(sk, v, raw):
            if sk == e and not is_dma and (e == 'pe' or not raw):
                return
            if deps.get(sk, 0) < v:
                deps[sk] = v
        for k in r:
            for sk, v in self.W.get(k, {}).items():
                add(sk, v, True)
        for k in w:
            for sk, v in self.W.get(k, {}).items():
                add(sk, v, False)
            for sk, v in self.R.get(k, {}).items():
                add(sk, v, False)
        return deps

    def _wait(self, e, deps):
        for sk, v in deps.items():
            if self.waited.get((e, sk), 0) >= v:
                continue
            self.E[e].wait_ge(self.sems[sk], v)
            self.waited[(e, sk)] = v

    def _reg(self, ev, r, w):
        sk, v = ev
        for k in r:
            self.R.setdefault(k, {})[sk] = v
        for k in w:
            self.W[k] = {sk: v}
            self.R[k] = {}

    def op(self, e, fn, r=(), w=()):
        self._wait(e, self._deps(e, r, w, False))
        ins = fn(self.E[e])
        ins.then_inc(self.sems[e], 1)
        self.cnt[e] += 1
        self._reg((e, self.cnt[e]), r, w)

    def mmg(self, fns, r=(), w=()):
        self._wait('pe', self._deps('pe', r, w, False))
        pe = self.E['pe']
        for f in fns[:-1]:
            f(pe)
        ins = fns[-1](pe)
        ins.then_inc(self.sems['pe'], 1)
        self.cnt['pe'] += 1
        self._reg(('pe', self.cnt['pe']), r, w)

    def dma(self, q, out, in_, r=(), w=(), **kw):
        self._wait(q, self._deps(q, r, w, True))
        i = self.dq[q] % self.NDS
        self.dq[q] += 1
        key = ('d', q, i)
        self.E[q].dma_start(out=out, in_=in_, **kw).then_inc(self.sems[key], 16)
        self.cnt[key] += 16
        self._reg((key, self.cnt[key]), r, w)

    def barrier(self):
        for e in self.E:
            for sk, v in self.cnt.items():
                if sk == e or v == 0:
                    continue
                if self.waited.get((e, sk), 0) >= v:
                    continue
                self.E[e].wait_ge(self.sems[sk], v)
                self.waited[(e, sk)] = v
        self.W = {}
        self.R = {}

    def bank(self):
        i = self.bank_i % 8
        self.bank_i += 1
        return self.banks[i], ('ps', i)


def build_program():
    nc = bass.Bass("TRN2", target_bir_lowering=False)
    dt = nc.dram_tensor
    x_in = dt("x", [NTOK, D], F32, kind="ExternalInput").ap()
    condT = dt("condT", [D, 2], F32, kind="ExternalInput").ap()
    mod_w = dt("mod_w", [2, D, 9 * D], F32, kind="ExternalInput").ap()
    mod_bT = dt("mod_bT", [128, 2, 72], F32, kind="ExternalInput").ap()
    vecs = dt("vecs", [128, 12, 8], F32, kind="ExternalInput").ap()
    ffn_w_in = dt("ffn_w_in", [2, 2, D, 2 * DFF], F32, kind="ExternalInput").ap()
    ffn_w_out = dt("ffn_w_out", [2, 2, DFF, D], F32, kind="ExternalInput").ap()
    identf = dt("identf", [128, 128], F32, kind="ExternalInput").ap()
    y_out = dt("y", [NTOK, D], F32, kind="ExternalOutput").ap()
    ab_w_in = dt("ab_w_in", [D, 2816], F32, kind="ExternalInput").ap()
    hvec = dt("hvec", [128, 8], F32, kind="ExternalInput").ap()
    ropeT = dt("ropeT", [128, 2, 4096], F32, kind="ExternalInput").ap()
    pmat = dt("pmat", [128, 128], F32, kind="ExternalInput").ap()
    rc = dt("rc", [128, 6 * 128 + 66], F32, kind="ExternalInput").ap()
    dec_in = dt("dec_in", [2, 8], F32, kind="ExternalInput").ap()
    st_f_in = dt("st_f", [8, 64, 64], F32, kind="ExternalInput").ap()
    st_b_in = dt("st_b", [8, 64, 64], F32, kind="ExternalInput").ap()
    o_gk = dt("o_gk", [1024, 128], F32, kind="ExternalOutput").ap()
    o_gv = dt("o_gv", [1024, 128], F32, kind="ExternalOutput").ap()
    o_rf = dt("o_rf", [4, 8, 64, 64], F32, kind="ExternalOutput").ap()
    o_rb = dt("o_rb", [4, 8, 64, 64], F32, kind="ExternalOutput").ap()
    ab_w_out = dt("ab_w_out", [D, D], F32, kind="ExternalInput").ap()
    na_w_qkv = dt("na_w_qkv", [D, 3 * D], F32, kind="ExternalInput").ap()
    na_w_out = dt("na_w_out", [D, D], F32, kind="ExternalInput").ap()
    cgk = dt("cgk", [256, 128], F32, kind="ExternalInput").ap()
    cgv = dt("cgv", [256, 128], F32, kind="ExternalInput").ap()
    cnk = dt("cnk", [256, 1024], F32, kind="ExternalInput").ap()
    cnv = dt("cnv", [256, 1024], F32, kind="ExternalInput").ap()
    o_nk = dt("o_nk", [1024, 1024], F32, kind="ExternalOutput").ap()
    o_nv = dt("o_nv", [1024, 1024], F32, kind="ExternalOutput").ap()
    rpb_pad = dt("rpb_pad", [64 + 16 * 15 * 31 + 64], F32, kind="ExternalInput").ap()
    nmask = dt("nmask", [128, 2, 64], F32, kind="ExternalInput").ap()
    identb = dt("identb", [128, 128], F32, kind="ExternalInput").ap()
    FM1 = dt("FM1", [16 * 128, NTOK + 256], BF16).ap()
    TM1 = dt("TM1", [NTOK, 16, 128], BF16).ap()
    XS = dt("XS", [D, NTOK], F32).ap()
    FM0 = dt("FM0", [17 * 128, NTOK + 256], BF16).ap()
    TM0 = dt("TM0", [NTOK, 1152], BF16).ap()
    YC = dt("YC", [D, NTOK], BF16).ap()

    with ExitStack() as es:
        p = P(nc, es)
        sb = lambda name, shape, dtype: es.enter_context(nc.sbuf_tensor(name, shape, dtype))
        psum_all = es.enter_context(nc.psum_tensor("psum_all", [128, 8, SUB], F32))
        for i in range(8):
            p.banks.append(psum_all[:, i, :])
        ident_f = sb("ident_f", [128, 128], F32)
        onesD = sb("onesD", [128, 128], BF16)
        eps_t = sb("eps_t", [128, 1], F32)
        scal = sb("scal", [128, 2 * 3 * 2 * 3, 8], F32)
        vec_sb = sb("vec_sb", [128, 12, 8], F32)
        p.dma('sp', ident_f[:], identf, w=['ident_f'])
        p.dma('sp', vec_sb[:], vecs, w=['vec_sb'])
        p.op('dve', lambda v: v.memset(onesD[:], 1.0 / D), w=['onesD'])
        p.op('dve', lambda v: v.memset(eps_t[:], EPS), w=['eps_t'])

        bones = sb("bones", [128, 128], BF16)
        pm_b = sb("pm_b", [128, 128], BF16)
        hv = sb("hv", [128, 8], F32)
        p.op('dve', lambda v: v.memset(bones[:], 0.0), w=['bones'])
        p.op('dve', lambda v: v.memset(bones[0:64, 0:64], 1.0 / 64), w=['bones'])
        p.op('dve', lambda v: v.memset(bones[64:128, 64:128], 1.0 / 64), w=['bones'])
        p.dma('pool', pm_b[:], pmat, w=['pm_b'])
        p.dma('sp', hv[:], hvec, w=['hv'])

        def sc(l, s, c, which):
            return scal[:, ((l * 3 + s) * 2 + c) * 3 + which, :]

        import os
        DBG = int(os.environ.get("KDEBUG", "9"))
        cond_f = sb("cond_f", [128, 8, 2], F32)
        cond_b = sb("cond_b", [128, 8, 2], BF16)
        modb = sb("modb", [128, 2, 72], F32)
        modsb = sb("modsb", [128, 72, 2], F32)
        p.dma('sp', cond_f[:], condT.rearrange("(k p) c -> p k c", p=128), w=['cond_f'])
        p.dma('sp', modb[:], mod_bT, w=['modb'])
        p.op('act', lambda a: a.activation(out=cond_b[:], in_=cond_f[:], func=AF.Silu), r=['cond_f'], w=['cond_b'])

        def mod_layer(l, mwbufs, ncol, bk, bkey, tag):
            jper = ncol // 128
            for g in range(72 // jper):
                buf = mwbufs[g % 2]
                p.dma('pool', buf[:, :, 0:ncol], mod_w[l, :, g * ncol:(g + 1) * ncol].rearrange("(k p) n -> p k n", p=128),
                      w=[(tag, g % 2)])
                for jj in range(jper):
                    j = g * jper + jj
                    p.mmg([(lambda t, k=k: t.matmul(bk[:, 2 * j:2 * j + 2], lhsT=buf[:, k, jj * 128:(jj + 1) * 128],
                                                    rhs=cond_b[:, k, :], start=(k == 0), stop=(k == 7))) for k in range(8)],
                          r=[(tag, g % 2), 'cond_b'], w=[bkey])
                yield
            for c in range(2):
                p.op('dve', lambda v, c=c: v.tensor_tensor(
                    out=modsb[:, :, c], in0=bk[:, 0:144].rearrange("p (j c) -> p j c", c=2)[:, :, c],
                    in1=modb[:, l, :], op=ALU.add), r=[bkey, 'modb'], w=[('modsb', c)])
            for s in range(3):
                wgt = 1.0 if s == 1 else 0.5
                for c in range(2):
                    shift = modsb[:, (3 * s) * 8:(3 * s) * 8 + 8, c]
                    scale = modsb[:, (3 * s + 1) * 8:(3 * s + 1) * 8 + 8, c]
                    gate = modsb[:, (3 * s + 2) * 8:(3 * s + 2) * 8 + 8, c]
                    gpre = vec_sb[:, l * 3 + s, :]
                    gpost = vec_sb[:, 6 + l * 3 + s, :]
                    p.op('dve', lambda v: v.scalar_tensor_tensor(
                        out=sc(l, s, c, 0), in0=scale, scalar=1.0, in1=gpre, op0=ALU.add, op1=ALU.mult),
                        r=[('modsb', c), 'vec_sb'], w=['scal'])
                    p.op('dve', lambda v: v.tensor_copy(out=sc(l, s, c, 1), in_=shift), r=[('modsb', c)], w=['scal'])
                    p.op('dve', lambda v: v.scalar_tensor_tensor(
                        out=sc(l, s, c, 2), in0=gate, scalar=wgt, in1=gpost, op0=ALU.mult, op1=ALU.mult),
                        r=[('modsb', c), 'vec_sb'], w=['scal'])
            yield

        with ExitStack() as es0:
            mw = [es0.enter_context(nc.sbuf_tensor(f"mw{i}", [128, 8, 1152], BF16)) for i in range(2)]
            bk0, bk0k = p.bank()
            for _ in mod_layer(0, mw, 1152, bk0, bk0k, 'mw'):
                pass
            p.barrier()

        def tile_scope(which):
            with ExitStack() as esT:
                sbT = lambda name, shape, dtype: esT.enter_context(nc.sbuf_tensor(name + '_' + which, shape, dtype))
                xT = sbT("xT", [128, 8, TS], F32)
                hy = sbT("hy", [128, 8, TS], F32)
                def hTv(k, u):
                    return hy[:, k, u * SUB:u * SUB + SUB // 2].bitcast(BF16)
                aT = sbT("aT", [128, NJ, TS], BF16)
                sq = sbT("sq", [128, 8, SUB], BF16)
                rstd = [sbT(f"rstd{i}", [128, SUB], F32) for i in range(2)]
                tmpf = [sbT(f"tmpf{i}", [128, SUB], F32) for i in range(2)]
                sil = [sbT(f"sil{i}", [128, SUB], F32) for i in range(2)]
                NWB = 3
                wbuf = [sbT(f"wbuf{i}", [128, 8, 2, 256], BF16) for i in range(NWB)]
                wobuf = [sbT(f"wobuf{i}", [128, NJ, 256], BF16) for i in range(2)]
                cnts = {'wb': 0, 'wo': 0, 'rs': 0, 'tf': 0, 'sl': 0}

                def rot(name, n):
                    i = cnts[name] % n
                    cnts[name] += 1
                    return i

                def rstd_of(src_fn, src_keys, nchunks, lhsT_ones, lkey):
                    for k in range(nchunks):
                        p.op('act', lambda a, k=k: a.activation(out=sq[:, k, :], in_=src_fn(k), func=AF.Square),
                             r=[src_keys[k]], w=[('sq', k)])
                    bk, bkey = p.bank()
                    p.mmg([(lambda t, k=k: t.matmul(bk[:], lhsT=lhsT_ones, rhs=sq[:, k, :], start=(k == 0),
                                                    stop=(k == nchunks - 1))) for k in range(nchunks)],
                          r=[('sq', k) for k in range(nchunks)] + [lkey], w=[bkey])
                    ri = rot('rs', 2)
                    p.op('act', lambda a: a.activation(out=rstd[ri][:], in_=bk[:], func=AF.Ln, bias=eps_t[:], scale=1.0),
                         r=[bkey, 'eps_t'], w=[('rstd', ri)])
                    p.op('act', lambda a: a.activation(out=rstd[ri][:], in_=rstd[ri][:], func=AF.Exp, scale=-0.5),
                         r=[('rstd', ri)], w=[('rstd', ri)])
                    return ri

                HPEND = {}

                def ensure_h(u):
                    if u in HPEND:
                        HPEND.pop(u)()

                PRE = {'done': None, 'next': None}

                def prenorm(l, s, c):
                    if PRE['done'] == (l, s, c):
                        PRE['done'] = None
                        return
                    for u in range(NS):
                        ensure_h(u)
                        HPEND[u] = (lambda u=u: prenorm_u(l, s, c, u))

                def early_pre0():
                    if PRE['next'] is not None:
                        l, s, c = PRE['next']
                        PRE['next'] = None
                        prenorm(l, s, c)
                        PRE['done'] = (l, s, c)
                        ensure_h(0)

                def prenorm_u(l, s, c, u):
                    if True:
                        cols = slice(u * SUB, (u + 1) * SUB)
                        ri = rstd_of(lambda k: xT[:, k, cols], [('xT', k, u) for k in range(8)], 8, onesD[:], 'onesD')
                        for k in range(8):
                            ti = rot('tf', 2)
                            p.op('dve', lambda v, k=k, ti=ti: v.scalar_tensor_tensor(
                                out=tmpf[ti][:], in0=xT[:, k, cols], scalar=sc(l, s, c, 0)[:, k:k + 1], in1=rstd[ri][:],
                                op0=ALU.mult, op1=ALU.mult), r=[('xT', k, u), ('rstd', ri), 'scal'], w=[('tmpf', ti)])
                            p.op('act', lambda a, k=k, ti=ti: a.activation(
                                out=hTv(k, u), in_=tmpf[ti][:], func=AF.Identity, bias=sc(l, s, c, 1)[:, k:k + 1], scale=1.0),
                                r=[('tmpf', ti), 'scal'], w=[('hy', u)])

                def postres(l, s, c, u):
                    cols = slice(u * SUB, (u + 1) * SUB)
                    bk, bkey = p.bank()
                    p.mmg([(lambda t, k=k: t.matmul(bk[:], lhsT=onesD[:], rhs=sq[:, k, :], start=(k == 0), stop=(k == 7)))
                           for k in range(8)], r=[('sq', k) for k in range(8)] + ['onesD'], w=[bkey])
                    ri = rot('rs', 2)
                    p.op('act', lambda a: a.activation(out=rstd[ri][:], in_=bk[:], func=AF.Ln, bias=eps_t[:], scale=1.0),
                         r=[bkey, 'eps_t'], w=[('rstd', ri)])
                    p.op('act', lambda a: a.activation(out=rstd[ri][:], in_=rstd[ri][:], func=AF.Exp, scale=-0.5),
                         r=[('rstd', ri)], w=[('rstd', ri)])
                    for k in range(8):
                        ti = rot('tf', 2)
                        p.op('dve', lambda v, k=k, ti=ti: v.tensor_tensor(out=tmpf[ti][:], in0=hy[:, k, cols], in1=rstd[ri][:],
                                                                          op=ALU.mult),
                             r=[('hy', u), ('rstd', ri)], w=[('tmpf', ti)])
                        p.op('dve', lambda v, k=k, ti=ti: v.scalar_tensor_tensor(
                            out=xT[:, k, cols], in0=tmpf[ti][:], scalar=sc(l, s, c, 2)[:, k:k + 1], in1=xT[:, k, cols],
                            op0=ALU.mult, op1=ALU.add), r=[('tmpf', ti), ('xT', k, u), 'scal'], w=[('xT', k, u)])

                def evac_y(bk, bkey, m, u):
                    cols = slice(u * SUB, (u + 1) * SUB)
                    p.op('act', lambda a: a.activation(out=sq[:, m, :], in_=bk[:], func=AF.Square), r=[bkey], w=[('sq', m)])
                    p.op('dve', lambda v: v.tensor_copy(out=hy[:, m, cols], in_=bk[:]), r=[bkey], w=[('hy', u)])

                def ffn(l, i, s, c, after_mm1=None):
                    prenorm(l, s, c)
                    w_in = ffn_w_in[l, i]
                    w_out = ffn_w_out[l, i]
                    wmap = {}

                    def load_in(jp):
                        wi = rot('wb', NWB)
                        wb = wbuf[wi]
                        for half in range(2):
                            p.dma('pool', wb[:, :, half, :],
                                  w_in[:, half * DFF + jp * 256: half * DFF + (jp + 1) * 256].rearrange("(k p) n -> p k n", p=128),
                                  w=[('wbuf', wi)])
                        wmap[jp] = wi

                    def mm1(jp, u):
                        wi = wmap[jp]
                        wb = wbuf[wi]
                        ensure_h(u)
                        for jj in range(2):
                            j = jp * 2 + jj
                            cols = slice(u * SUB, (u + 1) * SUB)
                            bg, bgk = p.bank()
                            bu, buk = p.bank()
                            p.mmg([(lambda t, k=k: t.matmul(bg[:], lhsT=wb[:, k, 0, jj * 128:(jj + 1) * 128], rhs=hTv(k, u),
                                                            start=(k == 0), stop=(k == 7))) for k in range(8)],
                                  r=[('wbuf', wi), ('hy', u)], w=[bgk])
                            p.mmg([(lambda t, k=k: t.matmul(bu[:], lhsT=wb[:, k, 1, jj * 128:(jj + 1) * 128], rhs=hTv(k, u),
                                                            start=(k == 0), stop=(k == 7))) for k in range(8)],
                                  r=[('wbuf', wi), ('hy', u)], w=[buk])
                            si = rot('sl', 2)
                            p.op('act', lambda a, si=si, bg=bg: a.activation(out=sil[si][:], in_=bg[:], func=AF.Silu),
                                 r=[bgk], w=[('sil', si)])
                            p.op('dve', lambda v, si=si, bu=bu, j=j, cols=cols: v.tensor_tensor(
                                out=aT[:, j, cols], in0=bu[:], in1=sil[si][:], op=ALU.mult),
                                r=[buk, ('sil', si)], w=[('aT', j, u)])
                    load_in(0)
                    load_in(1)
                    mm1(0, 0)
                    mm1(1, 0)
                    load_in(2)
                    mm1(0, 1)
                    mm1(1, 1)
                    for jp in range(2, NJ // 2):
                        if jp + 1 < NJ // 2:
                            load_in(jp + 1)
                        mm1(jp, 0)
                        mm1(jp, 1)
                    omap = {}

                    def load_out(mp):
                        wi = rot('wo', 2)
                        p.dma('pool', wobuf[wi][:], w_out[:, mp * 256:(mp + 1) * 256].rearrange("(j p) n -> p j n", p=128),
                              w=[('wobuf', wi)])
                        omap[mp] = wi

                    def mm2(mp, u):
                        wi = omap[mp]
                        wo = wobuf[wi]
                        for mm in range(2):
                            m = mp * 2 + mm
                            cols = slice(u * SUB, (u + 1) * SUB)
                            bk, bkey = p.bank()
                            p.mmg([(lambda t, j=j: t.matmul(bk[:], lhsT=wo[:, j, mm * 128:(mm + 1) * 128], rhs=aT[:, j, cols],
                                                            start=(j == 0), stop=(j == NJ - 1))) for j in range(NJ)],
                                  r=[('wobuf', wi)] + [('aT', j, u) for j in range(NJ)], w=[bkey])
                            evac2(bk, bkey, m, u)
                    load_out(0)
                    load_out(1)
                    mm2(0, 0)
                    mm2(0, 1)
                    load_out(2)
                    mm2(1, 0)
                    mm2(1, 1)
                    load_out(3)
                    mm2(2, 0)
                    mm2(3, 0)
                    finish(l, s, c, 0)
                    mm2(2, 1)
                    early_pre0()
                    mm2(3, 1)
                    if after_mm1 is not None:
                        after_mm1()
                    finish(l, s, c, 1)

                sq2 = [sq, sbT("sq_b", [128, 8, SUB], BF16)]

                def evac2(bk, bkey, m, u):
                    cols = slice(u * SUB, (u + 1) * SUB)
                    p.op('dve', lambda v: v.tensor_copy(out=hy[:, m, cols], in_=bk[:]), r=[bkey], w=[('hy', u), ('yT', u), 'hy2'])
                    p.op('act', lambda a: a.activation(out=sq2[u][:, m, :], in_=hy[:, m, cols], func=AF.Square),
                         r=[('hy', u)], w=[('sq', u, m)])

                def finish(l, s, c, u):
                    cols = slice(u * SUB, (u + 1) * SUB)
                    bk, bkey = p.bank()
                    p.mmg([(lambda t, k=k: t.matmul(bk[:], lhsT=onesD[:], rhs=sq2[u][:, k, :], start=(k == 0), stop=(k == 7)))
                           for k in range(8)], r=[('sq', u, k) for k in range(8)] + ['onesD'], w=[bkey])
                    ri = rot('rs', 2)
                    p.op('act', lambda a: a.activation(out=rstd[ri][:], in_=bk[:], func=AF.Ln, bias=eps_t[:], scale=1.0),
                         r=[bkey, 'eps_t'], w=[('rstd', ri)])
                    p.op('act', lambda a: a.activation(out=rstd[ri][:], in_=rstd[ri][:], func=AF.Exp, scale=-0.5),
                         r=[('rstd', ri)], w=[('rstd', ri)])
                    for k in range(8):
                        ti = rot('tf', 2)
                        p.op('dve', lambda v, k=k, ti=ti: v.tensor_tensor(out=tmpf[ti][:], in0=hy[:, k, cols], in1=rstd[ri][:],
                                                                          op=ALU.mult),
                             r=[('hy', u), ('yT', u), ('rstd', ri)], w=[('tmpf', ti)])
                        p.op('dve', lambda v, k=k, ti=ti: v.scalar_tensor_tensor(
                            out=xT[:, k, cols], in0=tmpf[ti][:], scalar=sc(l, s, c, 2)[:, k:k + 1], in1=xT[:, k, cols],
                            op0=ALU.mult, op1=ALU.add), r=[('tmpf', ti), ('xT', k, u), 'scal'], w=[('xT', k, u)])

                def load_x_tokmajor(t):
                    p.dma('sp', hy[:], x_in[t * TS:(t + 1) * TS, :].rearrange("(g p) d -> p g d", p=128),
                          w=[('hy', 0), ('hy', 1), 'hy2'])
                    for k in range(8):
                        for u in range(NS):
                            bk, bkey = p.bank()
                            p.mmg([(lambda tt, gi=gi: tt.transpose(bk[:, gi * 128:(gi + 1) * 128],
                                                                   hy[:, u * 4 + gi, k * 128:(k + 1) * 128], ident_f[:]))
                                   for gi in range(4)], r=[('hy', 0), ('hy', 1), 'ident_f'], w=[bkey])
                            eng = 'act' if (k + u) % 2 == 0 else 'dve'
                            if eng == 'act':
                                p.op('act', lambda a, k=k, u=u: a.copy(out=xT[:, k, u * SUB:(u + 1) * SUB], in_=bk[:]),
                                     r=[bkey], w=[('xT', k, u)])
                            else:
                                p.op('dve', lambda v, k=k, u=u: v.tensor_copy(out=xT[:, k, u * SUB:(u + 1) * SUB], in_=bk[:]),
                                     r=[bkey], w=[('xT', k, u)])

                def store_y_tokmajor(t):
                    for g in range(8):
                        u = g // 4
                        for half in range(2):
                            bk, bkey = p.bank()
                            p.mmg([(lambda tt, kk=kk: tt.transpose(bk[:, kk * 128:(kk + 1) * 128],
                                                                   xT[:, half * 4 + kk, g * 128:(g + 1) * 128], ident_f[:]))
                                   for kk in range(4)], r=[('xT', half * 4 + kk, u) for kk in range(4)] + ['ident_f'], w=[bkey])
                            eng = 'act' if half == 0 else 'dve'
                            if eng == 'act':
                                p.op('act', lambda a, g=g, half=half: a.copy(out=hy[:, g, half * 512:(half + 1) * 512], in_=bk[:]),
                                     r=[bkey], w=[('hy', u), 'hy2'])
                            else:
                                p.op('dve', lambda v, g=g, half=half: v.tensor_copy(out=hy[:, g, half * 512:(half + 1) * 512],
                                                                                    in_=bk[:]), r=[bkey], w=[('hy', u), 'hy2'])
                    if t + 1 < NT:
                        load_x_scratch(t + 1)
                    p.dma('sp', y_out[t * TS:(t + 1) * TS, :].rearrange("(g p) d -> p g d", p=128), hy[:],
                          r=[('hy', 0), ('hy', 1)])

                stgb = [sbT(f"stgb{i}", [128, TS], BF16) for i in range(2)]
                sqn = [sbT(f"sqn{i}", [128, SUB], BF16) for i in range(2)]
                qnb = [sbT(f"qnb{i}", [128, SUB], BF16) for i in range(2)]
                rope_sb = [sbT(f"rope_sb{i}", [128, 2, SUB], F32) for i in range(2)]
                ostg = sq2[1][:, 0:4, :].bitcast(F32).rearrange("p a (b c) -> p (a b) c", b=2)
                ostg2 = sq2[1][:, 4:8, :].bitcast(F32).rearrange("p a (b c) -> p (a b) c", b=2)
                OSTG_KEYS = [('sq', 1, m) for m in range(8)]
                tstg = aT[:, 0:9, :].rearrange("p a b -> p (a b)").rearrange("p (g n) -> p g n", g=8)
                TSTG_KEYS = [('aT', j, u) for j in range(9) for u in range(NS)]
                cnts.update({'sg': 0, 'sn': 0, 'qb': 0})

                def wflat(wb):
                    return wb[:].rearrange("p k a b -> p k (a b)")

                def proj_fm(W, col0, nch, t, evac):
                    wi = rot('wb', NWB)
                    wb = wflat(wbuf[wi])
                    p.dma('pool', wb[:, :, 0:nch * 128], W[:, col0:col0 + nch * 128].rearrange("(k p) n -> p k n", p=128),
                          w=[('wbuf', wi)])
                    if 1 in HPEND:
                        ensure_h(0)
                    for ci in range(nch):
                        for u in range(NS):
                            ensure_h(u)
                            cols = slice(u * SUB, (u + 1) * SUB)
                            bk, bkey = p.bank()
                            p.mmg([(lambda tt, k=k: tt.matmul(bk[:], lhsT=wb[:, k, ci * 128:(ci + 1) * 128], rhs=hTv(k, u),
                                                              start=(k == 0), stop=(k == 7))) for k in range(8)],
                                  r=[('wbuf', wi), ('hy', u)], w=[bkey])
                            evac(ci, u, bk, bkey)

                def proj_tm(W, col0, ncol, evac):
                    wi = rot('wb', NWB)
                    wb = wflat(wbuf[wi])
                    p.dma('pool', wb[:, :, 0:ncol], W[:, col0:col0 + ncol].rearrange("(k p) n -> p k n", p=128),
                          w=[('wbuf', wi)])
                    for g in range(8):
                        u = g // 4
                        ensure_h(u)
                        bk, bkey = p.bank()
                        p.mmg([(lambda tt, k=k: tt.matmul(bk[:, 0:ncol], lhsT=hTv(k, u)[:, (g % 4) * 128:(g % 4 + 1) * 128], rhs=wb[:, k, 0:ncol],
                                                          start=(k == 0), stop=(k == 7))) for k in range(8)],
                              r=[('wbuf', wi), ('hy', u)], w=[bkey])
                        evac(g, bk, bkey)

                def fm_store(FM, chunk, t, si):
                    p.dma('sp', FM[chunk * 128:(chunk + 1) * 128, t * TS:(t + 1) * TS], stgb[si][:], r=[('stgb', si)])

                def simple_evac(FM, chunk0, t, kind):
                    st = {}

                    def ev(ci, u, bk, bkey):
                        if u == 0:
                            st['si'] = rot('sg', 2)
                        si = st['si']
                        cols = slice(u * SUB, (u + 1) * SUB)
                        if kind == 'copy':
                            p.op('act', lambda a: a.copy(out=stgb[si][:, cols], in_=bk[:]), r=[bkey], w=[('stgb', si)])
                        elif kind == 'scale':
                            p.op('act', lambda a: a.mul(out=stgb[si][:, cols], in_=bk[:], mul=0.125), r=[bkey], w=[('stgb', si)])
                        elif kind == 'silu':
                            p.op('act', lambda a: a.activation(out=stgb[si][:, cols], in_=bk[:], func=AF.Silu), r=[bkey],
                                 w=[('stgb', si)])
                        if u == NS - 1:
                            fm_store(FM, chunk0 + ci, t, si)
                    return ev

                def normrope_evac(FM, chunk0, t, c, wcol, outk):
                    st = {}

                    def ev(ci, u, bk, bkey):
                        if u == 0:
                            st['si'] = rot('sg', 2)
                        si = st['si']
                        cols = slice(u * SUB, (u + 1) * SUB)
                        t0 = rot('tf', 2)
                        p.op('act', lambda a: a.copy(out=tmpf[t0][:], in_=bk[:]), r=[bkey], w=[('tmpf', t0)])
                        sn = rot('sn', 2)
                        p.op('act', lambda a: a.activation(out=sqn[sn][:], in_=tmpf[t0][:], func=AF.Square),
                             r=[('tmpf', t0)], w=[('sqn', sn)])
                        b2, b2k = p.bank()
                        p.mmg([lambda tt: tt.matmul(b2[:], lhsT=bones[:], rhs=sqn[sn][:], start=True, stop=True)],
                              r=[('sqn', sn), 'bones'], w=[b2k])
                        ri = rot('rs', 2)
                        p.op('act', lambda a: a.activation(out=rstd[ri][:], in_=b2[:], func=AF.Ln, bias=eps_t[:], scale=1.0),
                             r=[b2k, 'eps_t'], w=[('rstd', ri)])
                        p.op('act', lambda a: a.activation(out=rstd[ri][:], in_=rstd[ri][:], func=AF.Exp, scale=-0.5),
                             r=[('rstd', ri)], w=[('rstd', ri)])
                        p.op('dve', lambda v: v.scalar_tensor_tensor(out=tmpf[t0][:], in0=tmpf[t0][:], scalar=hv[:, wcol:wcol + 1],
                                                                     in1=rstd[ri][:], op0=ALU.mult, op1=ALU.mult),
                             r=[('tmpf', t0), ('rstd', ri), 'hv'], w=[('tmpf', t0)])
                        if outk and c == 1:
                            b3, b3k = p.bank()
                            p.mmg([(lambda tt, g=g: tt.transpose(b3[:, g * 128:(g + 1) * 128], tmpf[t0][:, g * 128:(g + 1) * 128],
                                                                 ident_f[:])) for g in range(4)],
                                  r=[('tmpf', t0), 'ident_f'], w=[b3k])
                            p.op('dve', lambda v: v.tensor_copy(out=ostg[:, u * 4:(u + 1) * 4, :],
                                                                in_=b3[:].rearrange("p (g n) -> p g n", g=4)),
                                 r=[b3k], w=OSTG_KEYS)
                            if u == NS - 1:
                                p.dma('sp', o_gk.rearrange("(g p) n -> p g n", p=128), ostg, r=OSTG_KEYS)
                        if c == 0:
                            qi = rot('qb', 2)
                            p.op('act', lambda a: a.copy(out=qnb[qi][:], in_=tmpf[t0][:]), r=[('tmpf', t0)], w=[('qnb', qi)])
                            b3, b3k = p.bank()
                            p.mmg([lambda tt: tt.matmul(b3[:], lhsT=pm_b[:], rhs=qnb[qi][:], start=True, stop=True)],
                                  r=[('qnb', qi), 'pm_b'], w=[b3k])
                            t1 = rot('tf', 2)
                            p.op('dve', lambda v: v.tensor_tensor(out=tmpf[t1][:], in0=b3[:], in1=rope_sb[u][:, 1, :], op=ALU.mult),
                                 r=[b3k, ('rope', u)], w=[('tmpf', t1)])
                            p.op('dve', lambda v: v.tensor_tensor(out=tmpf[t0][:], in0=tmpf[t0][:], in1=rope_sb[u][:, 0, :],
                                                                  op=ALU.mult), r=[('tmpf', t0), ('rope', u)], w=[('tmpf', t0)])
                            p.op('dve', lambda v: v.tensor_tensor(out=stgb[si][:, cols], in0=tmpf[t0][:], in1=tmpf[t1][:],
                                                                  op=ALU.add), r=[('tmpf', t0), ('tmpf', t1)], w=[('stgb', si)])
                        else:
                            p.op('act', lambda a: a.copy(out=stgb[si][:, cols], in_=tmpf[t0][:]), r=[('tmpf', t0)],
                                 w=[('stgb', si)])
                        if u == NS - 1:
                            fm_store(FM, chunk0 + ci, t, si)
                    return ev

                def inproj_ab(t, c):
                    W = ab_w_in
                    if c == 0:
                        for u in range(NS):
                            p.dma('sp', rope_sb[u][:], ropeT[:, :, t * TS + u * SUB:t * TS + (u + 1) * SUB], w=[('rope', u)])
                    proj_fm(W, 0, 4, t, simple_evac(FM0, 0, t, 'copy'))
                    proj_fm(W, 512, 4, t, simple_evac(FM0, 4, t, 'scale'))
                    proj_fm(W, 1536, 4, t, simple_evac(FM0, 8, t, 'silu'))
                    proj_fm(W, 2048, 4, t, normrope_evac(FM0, 12, t, c, 0, False))
                    proj_fm(W, 2560, 1, t, normrope_evac(FM0, 16, t, c, 1, True))

                    def ev_k(g, bk, bkey):
                        p.op('act', lambda a: a.mul(out=tstg[:, g, 0:512], in_=bk[:], mul=0.125), r=[bkey], w=TSTG_KEYS)

                    def ev_v(g, bk, bkey):
                        p.op('dve', lambda v: v.tensor_copy(out=tstg[:, g, 512:1024], in_=bk[:]), r=[bkey], w=TSTG_KEYS)

                    def ev_gv(g, bk, bkey):
                        p.op('dve', lambda v: v.tensor_copy(out=tstg[:, g, 1024:1152], in_=bk[:, 0:128]), r=[bkey], w=TSTG_KEYS)
                        if c == 1:
                            p.op('dve', lambda v: v.tensor_copy(out=ostg2[:, g, :], in_=bk[:, 0:128]), r=[bkey], w=OSTG_KEYS)
                    proj_tm(W, 512, 512, ev_k)
                    proj_tm(W, 1024, 512, ev_v)
                    proj_tm(W, 2688, 128, ev_gv)
                    if c == 1:
                        p.dma('sp', o_gv.rearrange("(g p) n -> p g n", p=128), ostg2, r=OSTG_KEYS)
                    p.dma('sp', TM0[t * TS:(t + 1) * TS, :].rearrange("(g p) n -> p g n", p=128), tstg, r=TSTG_KEYS)

                if which == 'A':
                    for t in range(NT):
                        c = 0 if t < 4 else 1
                        load_x_tokmajor(t)
                        PRE['next'] = (0, 1, c)
                        ffn(0, 0, 0, c)
                        prenorm(0, 1, c)
                        inproj_ab(t, c)
                        p.dma('sp', XS.rearrange("(k p) n -> p k n", p=128)[:, :, t * TS:(t + 1) * TS], xT[:],
                              r=[('xT', k, u) for k in range(8) for u in range(NS)])
                    p.barrier()
                def load_x_scratch(t):
                    p.dma('sp', xT[:], XS.rearrange("(k p) n -> p k n", p=128)[:, :, t * TS:(t + 1) * TS],
                          w=[('xT', k, u) for k in range(8) for u in range(NS)])

                def load_ycat(t):
                    p.dma('sp', aT[:, 0:8, :], YC.rearrange("(k p) n -> p k n", p=128)[:, :, t * TS:(t + 1) * TS],
                          w=[('aT', j, u) for j in range(8) for u in range(NS)])

                def outproj(Wo, l, c):
                    omap = {}

                    def load_out(mp):
                        wi = rot('wo', 2)
                        p.dma('pool', wobuf[wi][:, 0:8, :], Wo[:, mp * 256:(mp + 1) * 256].rearrange("(j p) n -> p j n", p=128),
                              w=[('wobuf', wi)])
                        omap[mp] = wi

                    def mmo(mp, u):
                        wi = omap[mp]
                        wo = wobuf[wi]
                        for mm in range(2):
                            m = mp * 2 + mm
                            cols = slice(u * SUB, (u + 1) * SUB)
                            bk, bkey = p.bank()
                            p.mmg([(lambda t_, j=j: t_.matmul(bk[:], lhsT=wo[:, j, mm * 128:(mm + 1) * 128], rhs=aT[:, j, cols],
                                                              start=(j == 0), stop=(j == 7))) for j in range(8)],
                                  r=[('wobuf', wi)] + [('aT', j, u) for j in range(8)], w=[bkey])
                            evac2(bk, bkey, m, u)
                    load_out(0)
                    load_out(1)
                    mmo(0, 0)
                    mmo(0, 1)
                    load_out(2)
                    mmo(1, 0)
                    mmo(1, 1)
                    load_out(3)
                    mmo(2, 0)
                    mmo(3, 0)
                    finish(l, 1, c, 0)
                    mmo(2, 1)
                    early_pre0()
                    mmo(3, 1)
                    finish(l, 1, c, 1)

                tstg1 = aT[:, 0:16, :].rearrange("p a b -> p (a b)").rearrange("p (g h e) -> p g h e", g=8, h=16)
                TSTG1_KEYS = [('aT', j, u) for j in range(16) for u in range(NS)]
                def ostgN(gq, cb):
                    return hy[:, gq * 2 + cb, :].rearrange("p (u h w) -> p u h w", u=2, h=2)[:, :, 1, :]
                ostgN_all = bass.AP(hy, 256, [[8 * TS, 128], [512, 16], [1, 256]])

                def inproj_na(t, c):
                    W = na_w_qkv
                    proj_fm(W, 0, 4, t, simple_evac(FM1, 0, t, 'copy'))
                    proj_fm(W, 512, 4, t, simple_evac(FM1, 4, t, 'copy'))
                    proj_fm(W, 1024, 4, t, simple_evac(FM1, 8, t, 'copy'))
                    proj_fm(W, 1536, 4, t, simple_evac(FM1, 12, t, 'copy'))
                    for g in range(8):
                        p.op('dve', lambda v, g=g: v.memset(tstg1[:, g, :, 64:128], 1.0), w=TSTG1_KEYS)
                    for cb in range(2):
                        def ev_v(g, bk, bkey, cb=cb):
                            p.op('dve', lambda v: v.tensor_copy(out=tstg1[:, g, cb * 8:(cb + 1) * 8, 0:64],
                                                                in_=bk[:].rearrange("p (h e) -> p h e", h=8)), r=[bkey],
                                 w=TSTG1_KEYS)
                        proj_tm(W, 2048 + cb * 512, 512, ev_v)
                    p.dma('sp', TM1[t * TS:(t + 1) * TS, :, :].rearrange("(g p) h e -> p g (h e)", p=128),
                          tstg1.rearrange("p g h e -> p g (h e)"), r=TSTG1_KEYS)
                    if c == 1:
                        for which, o_d in ((1, o_nk), (2, o_nv)):
                            for gh in range(2):
                                for cb in range(2):
                                    def ev_o(g, bk, bkey, cb=cb, gh=gh):
                                        if g // 4 == gh:
                                            p.op('act', lambda a: a.copy(out=ostgN(g % 4, cb), in_=bk[:].rearrange("p (u w) -> p u w", u=2)),
                                                 r=[bkey], w=['hy2'])
                                    proj_tm(W, which * 1024 + cb * 512, 512, ev_o)
                                for gq in range(4):
                                    p.dma('sp', o_d[gh * 512 + gq * 128:gh * 512 + (gq + 1) * 128, :].rearrange("p (q w) -> p q w", w=256),
                                          bass.AP(hy, 256 + gq * 2048, [[8 * TS, 128], [512, 4], [1, 256]]), r=['hy2'])

                if which == 'C':
                    load_x_scratch(0)
                    for t in range(NT):
                        c = 0 if t < 4 else 1
                        load_ycat(t)
                        PRE['next'] = (0, 2, c)
                        outproj(ab_w_out, 0, c)
                        PRE['next'] = (1, 0, c)
                        ffn(0, 1, 2, c)
                        PRE['next'] = (1, 1, c)
                        ffn(1, 0, 0, c)
                        prenorm(1, 1, c)
                        ensure_h(0)
                        ensure_h(1)
                        p.dma('sp', XS.rearrange("(k p) n -> p k n", p=128)[:, :, t * TS:(t + 1) * TS], xT[:],
                              r=[('xT', k, u) for k in range(8) for u in range(NS)])
                        if t + 1 < NT:
                            load_x_scratch(t + 1)
                        inproj_na(t, c)
                    p.barrier()
                if which == 'E':
                    load_x_scratch(0)
                    load_ycat(0)
                    for t in range(NT):
                        c = 0 if t < 4 else 1
                        PRE['next'] = (1, 2, c)
                        outproj(na_w_out, 1, c)
                        ffn(1, 1, 2, c, after_mm1=(lambda t=t: load_ycat(t + 1)) if t + 1 < NT else None)
                        store_y_tokmajor(t)
                    p.barrier()
        tile_scope('A')
        with ExitStack() as esB:
            sbB = lambda name, shape, dtype: esB.enter_context(nc.sbuf_tensor(name, shape, dtype))
            rc_sb = sbB("rc_sb", [128, 6 * 128 + 66], F32)
            n1, cn, dpos, dneg = rc_sb[:, 0:128], rc_sb[:, 128:256], rc_sb[:, 256:384], rc_sb[:, 384:512]
            mge, mlt = rc_sb[:, 512:640], rc_sb[:, 640:768]
            posf, posb, c128 = rc_sb[:, 768:769], rc_sb[:, 769:770], rc_sb[:, 770:834]
            lg = sbB("lg", [128, 2, 8], F32)
            DM = sbB("DM", [128, 8, 128], F32)
            tmpd = sbB("tmpd", [128, 128], F32)
            QF = sbB("QF", [128, 4, 128], F32)
            QB = sbB("QB", [128, 4, 128], F32)
            KF = sbB("KF", [128, 8], F32)
            KB = sbB("KB", [128, 8], F32)
            CDF = sbB("CDF", [128, 4, 64], F32)
            CDB = sbB("CDB", [128, 4, 64], F32)
            eps5 = sbB("eps5", [128, 1], F32)
            p.op('dve', lambda v: v.memset(eps5[:], 1e-5), w=['eps5'])
            p.dma('sp', rc_sb[:], rc, w=['rc'])
            p.dma('sp', lg[:].rearrange("p a b -> p (a b)"), bass.AP(dec_in.tensor, 0, [[0, 128], [1, 16]]), w=['lg'])
            lgf = lg[:].rearrange("p a b -> p (a b)")
            p.op('act', lambda a: a.activation(out=lgf, in_=lgf, func=AF.Exp, scale=-1.0), r=['lg'], w=['lg'])
            p.op('dve', lambda v: v.tensor_scalar_add(out=lgf, in0=lgf, scalar1=1.0), r=['lg'], w=['lg'])
            p.op('act', lambda a: a.activation(out=lgf, in_=lgf, func=AF.Ln), r=['lg'], w=['lg'])
            p.op('dve', lambda v: v.tensor_scalar_mul(out=lgf, in0=lgf, scalar1=-1.0), r=['lg'], w=['lg'])
            for h in range(8):
                par, pr = h % 2, h // 2
                rows = slice(par * 64, par * 64 + 64)
                p.op('act', lambda a: a.activation(out=QF[rows, pr, :], in_=n1[rows, :], func=AF.Exp, scale=lg[rows, 0, h:h + 1]),
                     r=['lg', 'rc'], w=['tab'])
                p.op('act', lambda a: a.activation(out=QB[rows, pr, :], in_=cn[rows, :], func=AF.Exp, scale=lg[rows, 1, h:h + 1]),
                     r=['lg', 'rc'], w=['tab'])
                p.op('act', lambda a: a.activation(out=CDF[rows, pr, :], in_=c128[rows, :], func=AF.Exp, scale=lg[rows, 0, h:h + 1]),
                     r=['lg', 'rc'], w=['tab'])
                p.op('act', lambda a: a.activation(out=CDB[rows, pr, :], in_=c128[rows, :], func=AF.Exp, scale=lg[rows, 1, h:h + 1]),
                     r=['lg', 'rc'], w=['tab'])
                p.op('act', lambda a: a.activation(out=DM[:, h, :], in_=dpos, func=AF.Exp, scale=lg[:, 0, h:h + 1]),
                     r=['lg', 'rc'], w=[('DM', h)])
                p.op('dve', lambda v: v.tensor_tensor(out=DM[:, h, :], in0=DM[:, h, :], in1=mge, op=ALU.mult),
                     r=[('DM', h), 'rc'], w=[('DM', h)])
                p.op('act', lambda a: a.activation(out=tmpd[:], in_=dneg, func=AF.Exp, scale=lg[:, 1, h:h + 1]),
                     r=['lg', 'rc'], w=['tmpd'])
                p.op('dve', lambda v: v.tensor_tensor(out=tmpd[:], in0=tmpd[:], in1=mlt, op=ALU.mult),
                     r=['tmpd', 'rc'], w=['tmpd'])
                p.op('dve', lambda v: v.tensor_tensor(out=DM[:, h, :], in0=DM[:, h, :], in1=tmpd[:], op=ALU.add),
                     r=[('DM', h), 'tmpd'], w=[('DM', h)])
            p.op('act', lambda a: a.activation(out=KF[:], in_=lg[:, 0, :], func=AF.Exp, scale=posf), r=['lg', 'rc'], w=['tab'])
            p.op('act', lambda a: a.activation(out=KB[:], in_=lg[:, 1, :], func=AF.Exp, scale=posb), r=['lg', 'rc'], w=['tab'])

            PS = [sbB("PSf", [128, 32, 4, 64], BF16), sbB("PSb", [128, 32, 4, 64], BF16)]
            stt = [sbB("stf", [128, 4, 64], F32), sbB("stb", [128, 4, 64], F32)]
            kvin = [sbB(f"kvin{i}", [128, 1024], BF16) for i in range(3)]
            kd = [sbB(f"kd{i}", [128, 512], BF16) for i in range(2)]
            qk_sb = [sbB(f"qk_sb{i}", [128, 8, 512], BF16) for i in range(2)]
            sg_sb = [sbB(f"sg_sb{i}", [128, 4, 512], BF16) for i in range(2)]
            v_sb = [sbB(f"v_sb{i}", [128, 4, 512], BF16) for i in range(2)]
            qfd = [sbB(f"qfd{i}", [128, 4, 128], BF16) for i in range(2)]
            qbd = [sbB(f"qbd{i}", [128, 4, 128], BF16) for i in range(2)]
            qm = [sbB(f"qm{i}", [128, 2, 4, 128], BF16) for i in range(2)]
            ptb = [sbB(f"ptb{i}", [128, 512], BF16) for i in range(4)]
            gnf = [sbB(f"gnf{i}", [128, 512], F32) for i in range(2)]
            gnm = [sbB(f"gnm{i}", [128, 512], F32) for i in range(2)]
            gnb = [sbB(f"gnb{i}", [128, 512], BF16) for i in range(2)]
            gns = [sbB(f"gns{i}", [128, 512], BF16) for i in range(2)]
            gnr = [sbB(f"gnr{i}", [128, 512], F32) for i in range(2)]
            ystg = [sbB(f"ystg{i}", [128, 4, 512], BF16) for i in range(2)]
            cnts = {'kv': 0, 'kd': 0, 'blk': 0, 'qd': 0, 'pt': 0, 'gn': 0, 'sbk': 0, 'rb': 0}

            def rot(name, n):
                i = cnts[name] % n
                cnts[name] += 1
                return i
            st_in = [st_f_in, st_b_in]
            seqs = [(0, 32, None)] + [(4096 + 256 * b, 2, b) for b in range(4)]
            DBGB = int(os.environ.get("KDEBUGB", "9"))
            o_st = [o_rf, o_rb]
            KT = [KF, KB]
            CD = [CDF, CDB]

            def sbank():
                i = cnts['sbk'] % 4
                cnts['sbk'] += 1
                return p.banks[i], ('ps', i)

            def ret_states(T0, NCH, bidx, banks=None):
                for d in range(2):
                    st = stt[d]
                    if bidx is None:
                        for par in range(2):
                            p.dma('sp', st[par * 64:(par + 1) * 64, :, :],
                                  st_in[d].rearrange("(pr par) dk dv -> par dk pr dv", par=2)[par], w=[('st', d)])
                    else:
                        p.op('dve', lambda v: v.memset(st[:], 0.0), w=[('st', d)])
                    order = range(NCH) if d == 0 else range(NCH - 1, -1, -1)
                    for cc in order:
                        ki = rot('kv', 3)
                        p.dma('sp', kvin[ki][:], TM0[T0 + cc * 128:T0 + (cc + 1) * 128, 0:1024], w=[('kvin', ki)])
                        p.op('act', lambda a: a.copy(out=PS[d][:, cc, :, :], in_=st[:]), r=[('st', d)], w=[('PS', d, cc)])
                        di = rot('kd', 2)
                        p.op('dve', lambda v: v.tensor_tensor(
                            out=kd[di][:].rearrange("p (h e) -> p h e", h=8),
                            in0=kvin[ki][:, 0:512].rearrange("p (h e) -> p h e", h=8),
                            in1=KT[d][:].unsqueeze(2).to_broadcast([128, 8, 64]), op=ALU.mult),
                            r=[('kvin', ki), 'tab'], w=[('kd', di)])
                        if banks is None:
                            bk, bkey = sbank()
                        else:
                            bi_ = banks[rot('rb', 2) % len(banks)]
                            bk, bkey = p.banks[bi_], ('ps', bi_)
                        p.mmg([(lambda tt, h=h: tt.matmul(
                            bk[(h % 2) * 64:(h % 2) * 64 + 64, (h // 2) * 64:(h // 2) * 64 + 64],
                            lhsT=kd[di][:, h * 64:(h + 1) * 64], rhs=kvin[ki][:, 512 + h * 64:512 + (h + 1) * 64],
                            start=True, stop=True)) for h in range(8)], r=[('kd', di), ('kvin', ki)], w=[bkey])
                        p.op('dve', lambda v: v.tensor_tensor(out=st[:], in0=st[:], in1=CD[d][:], op=ALU.mult),
                             r=[('st', d), 'tab'], w=[('st', d)])
                        p.op('dve', lambda v: v.tensor_tensor(
                            out=st[:].rearrange("p a b -> p (a b)"), in0=bk[:, 0:256],
                            in1=st[:].rearrange("p a b -> p (a b)"), op=ALU.add), r=[bkey, ('st', d)], w=[('st', d)])
                        yield
                    if bidx is not None:
                        for par in range(2):
                            p.dma('sp', o_st[d][bidx].rearrange("(pr par) dk dv -> par dk pr dv", par=2)[par],
                                  st[par * 64:(par + 1) * 64, :, :], r=[('st', d)])

            def ret_out(T0, NCH):
                CPB = min(4, NCH)
                rfifo = []
                for blk in range(NCH // CPB):
                    N = CPB * 128
                    bi = rot('blk', 2)
                    c0 = T0 + blk * N
                    p.dma('sp', qk_sb[bi][:, :, 0:N], FM0[0:1024, c0:c0 + N].rearrange("(k p) n -> p k n", p=128),
                          w=[('qk', bi)])
                    p.dma('sp', sg_sb[bi][:, :, 0:N], FM0[1024:1536, c0:c0 + N].rearrange("(k p) n -> p k n", p=128),
                          w=[('sgs', bi)])
                    p.dma('sp', v_sb[bi][:, 0:CPB, :], TM0[c0:c0 + N, 512:1024].rearrange("(ci p) n -> p ci n", p=128),
                          w=[('vs', bi)])
                    OB = [(p.banks[4 + pr], ('ps', 4 + pr)) for pr in range(4)]
                    for ci in range(CPB):
                        cc = blk * CPB + ci
                        ccols = slice(ci * 128, (ci + 1) * 128)
                        qi = rot('qd', 2)
                        p.op('dve', lambda v: v.tensor_tensor(out=qfd[qi][:], in0=qk_sb[bi][:, 0:4, ccols], in1=QF[:], op=ALU.mult),
                             r=[('qk', bi), 'tab'], w=[('qfd', qi)])
                        p.op('dve', lambda v: v.tensor_tensor(out=qbd[qi][:], in0=qk_sb[bi][:, 0:4, ccols], in1=QB[:], op=ALU.mult),
                             r=[('qk', bi), 'tab'], w=[('qbd', qi)])
                        for par in range(2):
                            p.op('dve', lambda v, par=par: v.tensor_scalar_mul(
                                out=qm[qi][:, par, :, :], in0=qk_sb[bi][:, 0:4, ccols], scalar1=hv[:, 6 + par:7 + par]),
                                r=[('qk', bi), 'hv'], w=[('qm', qi)])
                        pts = []
                        for hb in range(2):
                            bk, bkey = sbank()
                            p.mmg([(lambda tt, hh=hh: tt.matmul(
                                bk[:, hh * 128:(hh + 1) * 128],
                                lhsT=qk_sb[bi][:, 4 + (hb * 4 + hh) // 2, ccols],
                                rhs=qm[qi][:, (hb * 4 + hh) % 2, (hb * 4 + hh) // 2, :],
                                start=True, stop=True)) for hh in range(4)], r=[('qk', bi), ('qm', qi)], w=[bkey])
                            pi = rot('pt', 4)
                            p.op('dve', lambda v, pi=pi, bk=bk: v.tensor_tensor(
                                out=ptb[pi][:], in0=bk[:], in1=DM[:, hb * 4:(hb + 1) * 4, :].rearrange("p a b -> p (a b)"),
                                op=ALU.mult), r=[bkey] + [('DM', hb * 4 + i) for i in range(4)], w=[('ptb', pi)])
                            pts.append(pi)
                        def stB(bi=bi, ci=ci, cc=cc, ccols=ccols, qi=qi, pts=pts, OB=OB):
                            for h in range(8 if DBGB >= 4 else 0):
                                par, pr = h % 2, h // 2
                                rows = slice(par * 64, par * 64 + 64)
                                ob, obk = OB[pr]
                                pi = pts[h // 4]
                                hh = h % 4
                                p.mmg([
                                    lambda tt: tt.matmul(ob[rows, ccols], lhsT=v_sb[bi][:, ci, h * 64:(h + 1) * 64],
                                                         rhs=ptb[pi][:, hh * 128:(hh + 1) * 128], start=True, stop=False),
                                    lambda tt: tt.matmul(ob[rows, ccols], lhsT=PS[0][rows, cc, pr, :], rhs=qfd[qi][rows, pr, :],
                                                         start=False, stop=False),
                                    lambda tt: tt.matmul(ob[rows, ccols], lhsT=PS[1][rows, cc, pr, :], rhs=qbd[qi][rows, pr, :],
                                                         start=False, stop=True)],
                                    r=[('vs', bi), ('ptb', pi), ('PS', 0, cc), ('PS', 1, cc), ('qfd', qi), ('qbd', qi)], w=[obk])
                        rfifo.append(stB)
                        if ci == CPB - 1:
                            def stGN(bi=bi, N=N, c0=c0, OB=OB):
                                yi = rot('gn', 2)
                                for pr in range(4 if DBGB >= 5 else 0):
                                    ob, obk = OB[pr]
                                    gi = pr % 2
                                    p.op('act', lambda a: a.copy(out=gnf[gi][:, 0:N], in_=ob[:, 0:N]), r=[obk], w=[('gnf', gi)])
                                    p.op('act', lambda a: a.copy(out=gnb[gi][:, 0:N], in_=gnf[gi][:, 0:N]), r=[('gnf', gi)], w=[('gnb', gi)])
                                    p.op('act', lambda a: a.activation(out=gns[gi][:, 0:N], in_=gnf[gi][:, 0:N], func=AF.Square),
                                         r=[('gnf', gi)], w=[('gns', gi)])
                                    bm, bmk = sbank()
                                    p.mmg([lambda tt: tt.matmul(bm[:, 0:N], lhsT=bones[:], rhs=gnb[gi][:, 0:N], start=True, stop=True)],
                                          r=[('gnb', gi), 'bones'], w=[bmk])
                                    bq, bqk = sbank()
                                    p.mmg([lambda tt: tt.matmul(bq[:, 0:N], lhsT=bones[:], rhs=gns[gi][:, 0:N], start=True, stop=True)],
                                          r=[('gns', gi), 'bones'], w=[bqk])
                                    p.op('act', lambda a: a.copy(out=gnm[gi][:, 0:N], in_=bm[:, 0:N]), r=[bmk], w=[('gnm', gi)])
                                    p.op('dve', lambda v: v.tensor_tensor(out=gnr[gi][:, 0:N], in0=gnm[gi][:, 0:N], in1=gnm[gi][:, 0:N],
                                                                          op=ALU.mult), r=[('gnm', gi)], w=[('gnr', gi)])
                                    p.op('dve', lambda v: v.tensor_tensor(out=gnr[gi][:, 0:N], in0=bq[:, 0:N], in1=gnr[gi][:, 0:N],
                                                                          op=ALU.subtract), r=[bqk, ('gnr', gi)], w=[('gnr', gi)])
                                    p.op('dve', lambda v: v.tensor_scalar_max(out=gnr[gi][:, 0:N], in0=gnr[gi][:, 0:N], scalar1=0.0),
                                         r=[('gnr', gi)], w=[('gnr', gi)])
                                    p.op('act', lambda a: a.activation(out=gnr[gi][:, 0:N], in_=gnr[gi][:, 0:N], func=AF.Ln, bias=eps5[:],
                                                                       scale=1.0), r=[('gnr', gi), 'eps5'], w=[('gnr', gi)])
                                    p.op('act', lambda a: a.activation(out=gnr[gi][:, 0:N], in_=gnr[gi][:, 0:N], func=AF.Exp, scale=-0.5),
                                         r=[('gnr', gi)], w=[('gnr', gi)])
                                    p.op('dve', lambda v: v.tensor_tensor(out=gnf[gi][:, 0:N], in0=gnf[gi][:, 0:N], in1=gnm[gi][:, 0:N],
                                                                          op=ALU.subtract), r=[('gnf', gi), ('gnm', gi)], w=[('gnf', gi)])
                                    p.op('dve', lambda v: v.tensor_tensor(out=gnf[gi][:, 0:N], in0=gnf[gi][:, 0:N], in1=gnr[gi][:, 0:N],
                                                                          op=ALU.mult), r=[('gnf', gi), ('gnr', gi)], w=[('gnf', gi)])
                                    p.op('dve', lambda v: v.scalar_tensor_tensor(
                                        out=ystg[yi][:, pr, 0:N], in0=gnf[gi][:, 0:N], scalar=hv[:, 2 + pr:3 + pr], in1=sg_sb[bi][:, pr, 0:N],
                                        op0=ALU.mult, op1=ALU.mult), r=[('gnf', gi), ('sgs', bi), 'hv'], w=[('ystg', yi)])
                                if DBGB >= 5:
                                    p.dma('sp', YC[0:512, c0:c0 + N].rearrange("(k p) n -> p k n", p=128), ystg[yi][:, :, 0:N],
                                          r=[('ystg', yi)])
                            rfifo.append(stGN)
                        while len(rfifo) > (2 if ci == CPB - 1 else 1):
                            rfifo.pop(0)()
                while rfifo:
                    rfifo.pop(0)()

            ptq = [sbB(f"ptq{i}", [128, 2, 512], BF16) for i in range(4)]
            q_sb = [sbB(f"q_sb{i}", [128, 4, 512], BF16) for i in range(2)]
            qmk = [sbB(f"qmk{i}", [128, 2, 4, 512], BF16) for i in range(1)]
            rec = [sbB(f"rec{i}", [128, 512], F32) for i in range(2)]
            yst = [sbB(f"yst{i}", [128, 4, 512], BF16) for i in range(2)]
            cnts.update({'aq': 0, 'ap': 0, 'ar': 0, 'ay': 0, 'as': 0, 'ao': 0})

            def attention(FMq, qchunk0, nheads, kT_fn, vaug_fn, kv_keys, NKT, T0, L, QN, ychunk0, npairs=3, bg=None):
                fifo = []
                LAG = 3

                def drain(n):
                    while len(fifo) > n:
                        fifo.pop(0)()
                for qt in range(L // QN):
                    c0 = T0 + qt * QN
                    for hg in range(nheads // 8):
                        qi = rot('aq', 2)
                        p.dma('sp', q_sb[qi][:, :, 0:QN],
                              FMq[(qchunk0 + hg * 4) * 128:(qchunk0 + hg * 4 + 4) * 128, c0:c0 + QN].rearrange("(k p) n -> p k n", p=128),
                              w=[('q_sb', qi)])
                        for par in range(2):
                            p.op('dve', lambda v, par=par: v.tensor_scalar_mul(
                                out=qmk[qi % len(qmk)][:, par, :, 0:QN], in0=q_sb[qi][:, :, 0:QN], scalar1=hv[:, 6 + par:7 + par]),
                                r=[('q_sb', qi), 'hv'], w=[('qmk', qi % len(qmk))])
                        yi = rot('ay', 2)
                        for hh in range(8):
                            h = hg * 8 + hh
                            if bg is not None:
                                next(bg, None)
                            ai = 6 + rot('ao', 2)
                            acc, acck = p.banks[ai], ('ps', ai)
                            for kp in range(NKT // 2):
                                b = rot('as', npairs)
                                bkeys = [('ps', 2 * b), ('ps', 2 * b + 1)]
                                p.mmg([(lambda tt, e=e: tt.matmul(p.banks[2 * b + e][:, 0:QN], lhsT=kT_fn(h, 2 * kp + e),
                                                                  rhs=qmk[qi % len(qmk)][:, h % 2, hh // 2, 0:QN], start=True, stop=True))
                                       for e in range(2)], r=[('qmk', qi % len(qmk))] + kv_keys, w=bkeys)
                                pi = rot('ap', 4)
                                p.op('act', lambda a: a.activation(out=ptq[pi][:, :, 0:QN], in_=psum_all[:, 2 * b:2 * b + 2, 0:QN],
                                                                   func=AF.Exp, scale=0.125), r=bkeys, w=[('ptq', pi)])

                                def pv(h=h, kp=kp, pi=pi, acc=acc, acck=acck):
                                    p.mmg([(lambda tt, e=e: tt.matmul(acc[:, 0:QN], lhsT=vaug_fn(h, 2 * kp + e), rhs=ptq[pi][:, e, 0:QN],
                                                                      start=(kp == 0 and e == 0), stop=(kp == NKT // 2 - 1 and e == 1)))
                                           for e in range(2)], r=[('ptq', pi)] + kv_keys, w=[acck])
                                fifo.append(pv)
                                drain(LAG)

                            def norm(h=h, hh=hh, acc=acc, acck=acck, yi=yi):
                                ri = rot('ar', 2)
                                p.op('dve', lambda v: v.reciprocal(out=rec[ri][0:64, 0:QN], in_=acc[64:128, 0:QN]), r=[acck],
                                     w=[('rec', ri)])
                                p.op('dve', lambda v: v.tensor_tensor(
                                    out=yst[yi][(h % 2) * 64:(h % 2) * 64 + 64, hh // 2, 0:QN], in0=acc[0:64, 0:QN],
                                    in1=rec[ri][0:64, 0:QN], op=ALU.mult), r=[acck, ('rec', ri)], w=[('yst', yi)])
                            fifo.append(norm)

                        def store(hg=hg, c0=c0, yi=yi):
                            p.dma('sp', YC[(ychunk0 + hg * 4) * 128:(ychunk0 + hg * 4 + 4) * 128, c0:c0 + QN].rearrange(
                                "(k p) n -> p k n", p=128), yst[yi][:, :, 0:QN], r=[('yst', yi)])
                        fifo.append(store)
                drain(0)

            kT2 = sbB("kT2", [128, 2, 4352], BF16)
            vaug = sbB("vaug", [128, 34, 2, 128], BF16)
            ck_f = sbB("ck_f", [128, 2, 128], F32)
            ck_b = sbB("ck_b", [128, 256], BF16)
            p.dma('sp', ck_f[:], cgk.rearrange("(g p) n -> p g n", p=128), w=['ck_f'])
            bk, bkey = sbank()
            p.mmg([(lambda tt, g=g: tt.transpose(bk[:, g * 128:(g + 1) * 128], ck_f[:, g, :], ident_f[:])) for g in range(2)],
                  r=['ck_f', 'ident_f'], w=[bkey])
            p.op('act', lambda a: a.copy(out=ck_b[:], in_=bk[:, 0:256]), r=[bkey], w=['ck_b'])
            p.dma('sp', FM0[16 * 128:17 * 128, NTOK:NTOK + 256], ck_b[:], r=['ck_b'], w=['FM0c'])

            def gqa(T0, L, sample, bg):
                NK = L + (256 if sample else 0)
                NKT = NK // 128
                for kvh in range(2):
                    for half in range(2):
                        src = FM0[16 * 128 + kvh * 64:16 * 128 + kvh * 64 + 64, :]
                        p.dma('sp', kT2[half * 64:half * 64 + 64, kvh, 0:L], src[:, T0:T0 + L], w=['kT2'])
                        if sample:
                            p.dma('sp', kT2[half * 64:half * 64 + 64, kvh, L:L + 256], src[:, NTOK:NTOK + 256],
                                  r=['FM0c'], w=['kT2'])
                p.op('dve', lambda v: v.memset(vaug[:], 1.0), w=['vaug'])
                for kvh in range(2):
                    p.dma('sp', vaug[:, 0:L // 128, kvh, 0:64],
                          TM0[T0:T0 + L, 1024 + kvh * 64:1088 + kvh * 64].rearrange("(kt p) e -> p kt e", p=128), w=['vaug'])
                    if sample:
                        p.dma('pool', vaug[:, L // 128:L // 128 + 2, kvh, 0:64],
                              cgv[:, kvh * 64:(kvh + 1) * 64].rearrange("(kt p) e -> p kt e", p=128), w=['vaug'])
                attention(FM0, 12, 8, lambda h, kt: kT2[:, h // 4, kt * 128:(kt + 1) * 128],
                          lambda h, kt: vaug[:, kt, h // 4, :], ['kT2', 'vaug'], NKT, T0, L, 512 if sample else 256, 4,
                          npairs=(2 if bg is not None else 3), bg=bg)

            mwB = [sbB(f"mwB{i}", [128, 8, 256], BF16) for i in range(2)]

            def background():
                g1 = ret_states(0, 32, None, banks=[4])
                g2 = mod_layer(1, mwB, 256, p.banks[5], ('ps', 5), 'mwB')
                d1 = d2 = False
                while not (d1 and d2):
                    if not d1:
                        try:
                            next(g1)
                        except StopIteration:
                            d1 = True
                    if not d2:
                        try:
                            next(g2)
                        except StopIteration:
                            d2 = True
                    yield
            bg = background()
            gqa(0, 4096, True, bg)
            for _ in bg:
                pass
            for (T0, NCH, bidx) in seqs[1:]:
                gqa(T0, NCH * 128, False, None)
            ret_out(0, 32)
            for (T0, NCH, bidx) in seqs[1:]:
                for _ in ret_states(T0, NCH, bidx):
                    pass
                ret_out(T0, NCH)
            p.barrier()
        tile_scope('C')
        with ExitStack() as esD:
            sbD = lambda name, shape, dtype: esD.enter_context(nc.sbuf_tensor(name + '_D', shape, dtype))
            cnts = {'aq': 0, 'ap': 0, 'ar': 0, 'ay': 0, 'as': 0, 'ao': 0}

            def rot(name, n):
                i = cnts[name] % n
                cnts[name] += 1
                return i
            ptq = [sbD(f"ptq{i}", [128, 2, 512], BF16) for i in range(4)]
            q_sb = [sbD(f"q_sb{i}", [128, 4, 512], BF16) for i in range(2)]
            qmk = [sbD(f"qmk{i}", [128, 2, 4, 512], BF16) for i in range(2)]
            rec = [sbD(f"rec{i}", [128, 512], F32) for i in range(2)]
            yst = [sbD(f"yst{i}", [128, 4, 512], BF16) for i in range(2)]
            knT = sbD("knT", [128, 8, 256], BF16)
            vaugN = sbD("vaugN", [128, 2, 16, 128], BF16)
            def attention(FMq, qchunk0, nheads, kT_fn, vaug_fn, kv_keys, NKT, T0, L, QN, ychunk0, npairs=3, bg=None):
                fifo = []
                LAG = 3

                def drain(n):
                    while len(fifo) > n:
                        fifo.pop(0)()
                for qt in range(L // QN):
                    c0 = T0 + qt * QN
                    for hg in range(nheads // 8):
                        qi = rot('aq', 2)
                        p.dma('sp', q_sb[qi][:, :, 0:QN],
                              FMq[(qchunk0 + hg * 4) * 128:(qchunk0 + hg * 4 + 4) * 128, c0:c0 + QN].rearrange("(k p) n -> p k n", p=128),
                              w=[('q_sb', qi)])
                        for par in range(2):
                            p.op('dve', lambda v, par=par: v.tensor_scalar_mul(
                                out=qmk[qi % len(qmk)][:, par, :, 0:QN], in0=q_sb[qi][:, :, 0:QN], scalar1=hv[:, 6 + par:7 + par]),
                                r=[('q_sb', qi), 'hv'], w=[('qmk', qi % len(qmk))])
                        yi = rot('ay', 2)
                        for hh in range(8):
                            h = hg * 8 + hh
                            if bg is not None:
                                next(bg, None)
                            ai = 6 + rot('ao', 2)
                            acc, acck = p.banks[ai], ('ps', ai)
                            for kp in range(NKT // 2):
                                b = rot('as', npairs)
                                bkeys = [('ps', 2 * b), ('ps', 2 * b + 1)]
                                p.mmg([(lambda tt, e=e: tt.matmul(p.banks[2 * b + e][:, 0:QN], lhsT=kT_fn(h, 2 * kp + e),
                                                                  rhs=qmk[qi % len(qmk)][:, h % 2, hh // 2, 0:QN], start=True, stop=True))
                                       for e in range(2)], r=[('qmk', qi % len(qmk))] + kv_keys, w=bkeys)
                                pi = rot('ap', 4)
                                p.op('act', lambda a: a.activation(out=ptq[pi][:, :, 0:QN], in_=psum_all[:, 2 * b:2 * b + 2, 0:QN],
                                                                   func=AF.Exp, scale=0.125), r=bkeys, w=[('ptq', pi)])

                                def pv(h=h, kp=kp, pi=pi, acc=acc, acck=acck):
                                    p.mmg([(lambda tt, e=e: tt.matmul(acc[:, 0:QN], lhsT=vaug_fn(h, 2 * kp + e), rhs=ptq[pi][:, e, 0:QN],
                                                                      start=(kp == 0 and e == 0), stop=(kp == NKT // 2 - 1 and e == 1)))
                                           for e in range(2)], r=[('ptq', pi)] + kv_keys, w=[acck])
                                fifo.append(pv)
                                drain(LAG)

                            def norm(h=h, hh=hh, acc=acc, acck=acck, yi=yi):
                                ri = rot('ar', 2)
                                p.op('dve', lambda v: v.reciprocal(out=rec[ri][0:64, 0:QN], in_=acc[64:128, 0:QN]), r=[acck],
                                     w=[('rec', ri)])
                                p.op('dve', lambda v: v.tensor_tensor(
                                    out=yst[yi][(h % 2) * 64:(h % 2) * 64 + 64, hh // 2, 0:QN], in0=acc[0:64, 0:QN],
                                    in1=rec[ri][0:64, 0:QN], op=ALU.mult), r=[acck, ('rec', ri)], w=[('yst', yi)])
                            fifo.append(norm)

                        def store(hg=hg, c0=c0, yi=yi):
                            p.dma('sp', YC[(ychunk0 + hg * 4) * 128:(ychunk0 + hg * 4 + 4) * 128, c0:c0 + QN].rearrange(
                                "(k p) n -> p k n", p=128), yst[yi][:, :, 0:QN], r=[('yst', yi)])
                        fifo.append(store)
                drain(0)


            def na_prompt(T0):
                p.dma('sp', knT[:], FM1[8 * 128:16 * 128, T0:T0 + 256].rearrange("(k p) n -> p k n", p=128), w=['knT'])
                p.dma('sp', vaugN[:].rearrange("p kt h e -> p kt (h e)"),
                      TM1[T0:T0 + 256, :, :].rearrange("(kt p) h e -> p kt (h e)", p=128), w=['vaugN'])
                attention(FM1, 0, 16, lambda h, kt: knT[:, h // 2, kt * 128:(kt + 1) * 128],
                          lambda h, kt: vaugN[:, kt, h, :], ['knT', 'vaugN'], 2, T0, 256, 256, 0)
            ident_b = sbD("ident_b", [128, 128], BF16)
            p.dma('pool', ident_b[:], identb, w=['ident_b'])
            nm = sbD("nm", [128, 2, 64], F32)
            p.dma('sp', nm[:], nmask, w=['nm'])
            BT = sbD("BT", [128, 16, 14, 64], BF16)
            graw = [sbD(f"graw{i}", [128, 14, 64], F32) for i in range(2)]
            gtmp = [sbD(f"gtmp{i}", [128, 14, 64], F32) for i in range(2)]
            def build_bt(h):
                gi = h % 2
                base = 64 + h * 465 - 48
                for half in range(2):
                    p.dma('sp', graw[gi][half * 64:(half + 1) * 64, :, :],
                          bass.AP(rpb_pad.tensor, base + half * 31, [[1, 64], [31, 14], [1, 64]]), w=[('graw', gi)])
                grev = bass.AP(graw[gi], 63, [[14 * 64, 128], [64, 14], [-1, 64]])
                p.op('dve', lambda v: v.tensor_tensor(out=gtmp[gi][:], in0=grev,
                                                      in1=nm[:, 0:1, :].to_broadcast([128, 14, 64]), op=ALU.mult),
                     r=[('graw', gi), 'nm'], w=[('gtmp', gi)])
                p.op('dve', lambda v: v.scalar_tensor_tensor(out=BT[:, h, :, :], in0=gtmp[gi][:], scalar=8.0,
                                                             in1=nm[:, 1:2, :].to_broadcast([128, 14, 64]),
                                                             op0=ALU.mult, op1=ALU.add),
                     r=[('gtmp', gi), 'nm'], w=['BT'])
            ctxK = sbD("ctxK", [128, 8, 256], BF16)
            ctxV = sbD("ctxV", [128, 2, 16, 128], BF16)
            ckf = sbD("ckf", [128, 2, 1024], F32)
            p.dma('sp', ckf[:], cnk.rearrange("(g p) n -> p g n", p=128), w=['ckf'])
            for k in range(8):
                si = rot('as', 4)
                bk, bkey = p.banks[si], ('ps', si)
                p.mmg([(lambda tt, g=g: tt.transpose(bk[:, g * 128:(g + 1) * 128], ckf[:, g, k * 128:(k + 1) * 128], ident_f[:]))
                       for g in range(2)], r=['ckf', 'ident_f'], w=[bkey])
                p.op('act', lambda a: a.copy(out=ctxK[:, k, :], in_=bk[:, 0:256]), r=[bkey], w=['ctxK'])
            p.op('dve', lambda v: v.memset(ctxV[:], 1.0), w=['ctxV'])
            for kt in range(2):
                p.dma('pool', ctxV[:, kt, :, 0:64], cnv[kt * 128:(kt + 1) * 128, :].rearrange("p (h e) -> p h e", h=16),
                      w=['ctxV'])
            for b in range(4):
                na_prompt(4096 + 256 * b)
                for h in range(4 * b, 4 * b + 4):
                    build_bt(h)
            kwin = [sbD(f"kwin{i}", [128, 8, 512], BF16) for i in range(2)]
            vwin = [sbD(f"vwin{i}", [128, 4, 16, 128], BF16) for i in range(2)]
            qrow = [sbD(f"qrow{i}", [128, 8, 64], BF16) for i in range(2)]
            qrm = [sbD(f"qrm{i}", [128, 2, 8, 64], BF16) for i in range(2)]
            ptn = [sbD(f"ptn{i}", [128, 768], BF16) for i in range(4)]
            recN = [sbD(f"recN{i}", [128, 512], F32) for i in range(2)]
            yblk = [sbD(f"yblk{i}", [128, 8, 512], BF16) for i in range(2)]
            cnts.update({'pn': 0})
            nfifo = []

            def ndrain(n):
                while len(nfifo) > n:
                    nfifo.pop(0)()
            for r in range(64):
                r0 = min(max(r - 4, 0), 56)
                dl = r - r0
                wi = r % 2
                p.dma('sp', kwin[wi][:], FM1[8 * 128:16 * 128, r0 * 64:r0 * 64 + 512].rearrange("(k p) n -> p k n", p=128),
                      w=[('kwin', wi)])
                p.dma('sp', vwin[wi][:].rearrange("p kt h e -> p kt (h e)"),
                      TM1[r0 * 64:r0 * 64 + 512, :, :].rearrange("(kt p) h e -> p kt (h e)", p=128), w=[('vwin', wi)])
                p.dma('sp', qrow[wi][:], FM1[0:8 * 128, r * 64:(r + 1) * 64].rearrange("(k p) n -> p k n", p=128),
                      w=[('qrow', wi)])
                for par in range(2):
                    p.op('dve', lambda v, par=par: v.tensor_scalar_mul(out=qrm[wi][:, par, :, :], in0=qrow[wi][:],
                                                                        scalar1=hv[:, 6 + par:7 + par]),
                         r=[('qrow', wi), 'hv'], w=[('qrm', wi)])
                accs = [(p.banks[4 + 2 * wi + par], ('ps', 4 + 2 * wi + par)) for par in range(2)]
                for m in range(8):
                    si = rot('as', 2)
                    sb1, sb1k = p.banks[2 * si], ('ps', 2 * si)
                    sb2, sb2k = p.banks[2 * si + 1], ('ps', 2 * si + 1)
                    qpair = qrm[wi][:, :, m, :]
                    fns = []
                    for i in range(4):
                        j = 2 * i - dl + 7
                        fns.append(lambda tt, i=i: tt.matmul(sb1[:, i * 128:(i + 1) * 128], lhsT=kwin[wi][:, m, i * 128:(i + 1) * 128],
                                                             rhs=qpair, start=True, stop=False))
                        fns.append(lambda tt, i=i, j=j: tt.matmul(sb1[:, i * 128:(i + 1) * 128], lhsT=ident_b[:],
                                                                  rhs=BT[:, 2 * m:2 * m + 2, j, :], start=False, stop=True))
                    p.mmg(fns, r=[('kwin', wi), ('qrm', wi), 'BT', 'ident_b'], w=[sb1k])
                    p.mmg([(lambda tt, kt=kt: tt.matmul(sb2[:, kt * 128:(kt + 1) * 128], lhsT=ctxK[:, m, kt * 128:(kt + 1) * 128],
                                                        rhs=qpair, start=True, stop=True)) for kt in range(2)],
                          r=[('qrm', wi), 'ctxK'], w=[sb2k])
                    pi = rot('pn', 4)
                    p.op('act', lambda a: a.activation(out=ptn[pi][:, 0:512], in_=sb1[:], func=AF.Exp, scale=0.125),
                         r=[sb1k], w=[('ptn', pi)])
                    p.op('act', lambda a: a.activation(out=ptn[pi][:, 512:768], in_=sb2[:, 0:256], func=AF.Exp, scale=0.125),
                         r=[sb2k], w=[('ptn', pi)])

                    def pv(m=m, pi=pi, wi=wi, accs=accs):
                        for par in range(2):
                            h = 2 * m + par
                            acc, acck = accs[par]
                            fns = []
                            for i in range(4):
                                fns.append(lambda tt, i=i: tt.matmul(acc[:, m * 64:(m + 1) * 64], lhsT=vwin[wi][:, i, h, :],
                                                                     rhs=ptn[pi][:, i * 128 + par * 64:i * 128 + par * 64 + 64],
                                                                     start=(i == 0), stop=False))
                            for kt in range(2):
                                fns.append(lambda tt, kt=kt: tt.matmul(acc[:, m * 64:(m + 1) * 64], lhsT=ctxV[:, kt, h, :],
                                                                       rhs=ptn[pi][:, 512 + kt * 128 + par * 64:512 + kt * 128 + par * 64 + 64],
                                                                       start=False, stop=(kt == 1)))
                            p.mmg(fns, r=[('vwin', wi), ('ptn', pi), 'ctxV'], w=[acck])
                    nfifo.append(pv)
                    ndrain(2)

                def rownorm(r=r, accs=accs):
                    yi = (r // 8) % 2
                    for par in range(2):
                        acc, acck = accs[par]
                        ri = rot('ar', 2)
                        p.op('dve', lambda v: v.reciprocal(out=recN[ri][0:64, :], in_=acc[64:128, :]), r=[acck], w=[('recN', ri)])
                        p.op('dve', lambda v: v.tensor_tensor(
                            out=yblk[yi][par * 64:(par + 1) * 64, :, (r % 8) * 64:(r % 8 + 1) * 64],
                            in0=acc[0:64, :].rearrange("p (m q) -> p m q", m=8),
                            in1=recN[ri][0:64, :].rearrange("p (m q) -> p m q", m=8),
                            op=ALU.mult), r=[acck, ('recN', ri)], w=[('yblk', yi)])
                    if r % 8 == 7:
                        c0 = (r - 7) * 64
                        p.dma('sp', YC[:, c0:c0 + 512].rearrange("(k p) n -> p k n", p=128), yblk[yi][:], r=[('yblk', yi)])
                nfifo.append(rownorm)
            ndrain(0)
            p.barrier()
        tile_scope('E')
    return nc


_NC = None


def _consts():
    f32 = np.float32
    t = np.arange(4096)
    rows, cols = t // 64, t % 64
    inv = (10000.0 ** (-np.arange(16, dtype=np.float64) / 16))
    cosT = np.zeros((128, 4096), f32)
    sinT = np.zeros((128, 4096), f32)
    for pp in range(128):
        i = pp % 64
        pos = rows if i < 32 else cols
        ang = (pos.astype(np.float32)[:, None] * inv.astype(np.float32)[None, :])[:, i % 16].astype(np.float32)
        cosT[pp] = np.cos(ang)
        sinT[pp] = np.sin(ang)
    ropeT = np.ascontiguousarray(np.stack([cosT, sinT], 1))
    pm = np.zeros((128, 128), f32)
    for base in range(0, 128, 32):
        for j in range(16):
            pm[base + 16 + j, base + j] = -1.0
            pm[base + j, base + 16 + j] = 1.0
    m = np.arange(128)[:, None].astype(f32)
    n = np.arange(128)[None, :].astype(f32)
    rc = np.concatenate([np.broadcast_to(n + 1, (128, 128)), np.broadcast_to(128 - n, (128, 128)),
                         np.maximum(n - m, 0), np.maximum(m - n, 0), (n >= m).astype(f32), (m > n).astype(f32),
                         127 - m, m, np.full((128, 64), 128.0, f32)], 1).astype(f32)
    cc = np.arange(64)
    c0 = np.clip(cc - 8, 0, 48)
    kc = np.arange(64)[:, None]
    m01 = ((kc >= c0[None, :]) & (kc < c0[None, :] + 16)).astype(f32)
    m01 = np.concatenate([m01, m01], 0)
    nmask = np.ascontiguousarray(np.stack([m01, (m01 - 1.0) * 240000.0], 1).astype(f32))
    return ropeT, pm, np.ascontiguousarray(rc), nmask


def kernel(**inp):
    global _NC
    f32 = np.float32
    xs, xp = np.asarray(inp['x_sample'], f32), np.asarray(inp['x_prompt'], f32)
    c, c_ctx = np.asarray(inp['c'], f32), np.asarray(inp['c_ctx'], f32)
    vec12 = np.concatenate([np.asarray(inp['norm_pre'], f32).reshape(6, D), np.asarray(inp['norm_post'], f32).reshape(6, D)], 0)
    vecs = np.ascontiguousarray(vec12.reshape(12, 8, 128).transpose(2, 0, 1))
    mod_bT = np.ascontiguousarray(np.asarray(inp['mod_b'], f32).reshape(2, 72, 128).transpose(2, 0, 1))
    ropeT, pm, rc, nmask = _consts()
    rpb_pad = np.pad(np.asarray(inp['na_rpb'], f32)[0].reshape(-1), (64, 64))
    hvec = np.zeros((128, 8), f32)
    hvec[:, 0] = np.tile(np.asarray(inp['gqa_q_norm'], f32)[0], 2)
    hvec[:, 1] = np.tile(np.asarray(inp['gqa_k_norm'], f32)[0], 2)
    hvec[:, 2:6] = np.asarray(inp['ret_gn'], f32)[0].reshape(4, 128).T
    hvec[0:64, 6] = 1.0
    hvec[64:128, 7] = 1.0
    dec_in = np.ascontiguousarray(np.stack([np.asarray(inp['ret_decay_fwd'], f32)[0], np.asarray(inp['ret_decay_bwd'], f32)[0]], 0))
    shared = dict(mod_w=np.asarray(inp['mod_w'], f32), mod_bT=mod_bT, vecs=vecs,
                  ffn_w_in=np.asarray(inp['ffn_w_in'], f32), ffn_w_out=np.asarray(inp['ffn_w_out'], f32),
                  identf=np.eye(128, dtype=f32), ab_w_in=np.asarray(inp['ab_w_in'], f32)[0], hvec=hvec, ropeT=ropeT,
                  pmat=pm, rc=rc, dec_in=dec_in,
                  ab_w_out=np.asarray(inp['ab_w_out'], f32)[0], na_w_qkv=np.asarray(inp['na_w_qkv'], f32)[0],
                  na_w_out=np.asarray(inp['na_w_out'], f32)[0], rpb_pad=rpb_pad, nmask=nmask,
                  identb=np.eye(128, dtype=f32))
    in_maps = []
    for i in range(8):
        x = np.concatenate([xs[i], xp[4 * i:4 * i + 4].reshape(1024, D)], 0)
        condT = np.ascontiguousarray(np.stack([c[i], c_ctx], 1))
        in_maps.append(dict(x=np.ascontiguousarray(x), condT=condT,
                            st_f=np.ascontiguousarray(np.asarray(inp['state_ret_fwd'], f32)[i, 0]),
                            st_b=np.ascontiguousarray(np.asarray(inp['state_ret_bwd'], f32)[i, 0]),
                            cgk=np.ascontiguousarray(np.asarray(inp['cache_gqa_k'], f32)[i, 0].reshape(256, 128)),
                            cgv=np.ascontiguousarray(np.asarray(inp['cache_gqa_v'], f32)[i, 0].reshape(256, 128)),
                            cnk=np.ascontiguousarray(np.asarray(inp['cache_na_k'], f32)[i, 0].reshape(256, 1024)),
                            cnv=np.ascontiguousarray(np.asarray(inp['cache_na_v'], f32)[i, 0].reshape(256, 1024)),
                            **shared))
    if _NC is None:
        _NC = build_program()
    res = run_bass_kernel_spmd(_NC, in_maps, core_ids=list(range(8)))
    R = res.results
    ys = np.stack([r["y"][:4096] for r in R], 0)
    yp = np.concatenate([r["y"][4096:].reshape(4, 256, D) for r in R], 0)
    gk = np.concatenate([r["o_gk"].reshape(4, 1, 256, 2, 64) for r in R], 0)
    gv = np.concatenate([r["o_gv"].reshape(4, 1, 256, 2, 64) for r in R], 0)
    rf = np.concatenate([r["o_rf"].reshape(4, 1, 8, 64, 64) for r in R], 0)
    rb = np.concatenate([r["o_rb"].reshape(4, 1, 8, 64, 64) for r in R], 0)
    nk = np.concatenate([r["o_nk"].reshape(4, 1, 256, 16, 64) for r in R], 0)
    nv = np.concatenate([r["o_nv"].reshape(4, 1, 256, 16, 64) for r in R], 0)
    return yp, ys, gk, gv, rf, rb, nk, nv
```

```python
import numpy as np
from contextlib import ExitStack
import concourse.bass as bass
import concourse.mybir as mybir
from concourse.bass_utils import run_bass_kernel_spmd
import ml_dtypes

F32, BF16 = mybir.dt.float32, mybir.dt.bfloat16
AF = mybir.ActivationFunctionType
ALU = mybir.AluOpType
AX = mybir.AxisListType

D = 1024
DFF = 2816
NJ = DFF // 128
TS = 1024
SUB = 512
NS = TS // SUB
NT = 5
NTOK = NT * TS
EPS = 1e-6


class P:
    def __init__(self, nc, es):
        self.nc = nc
        self.E = {'pe': nc.tensor, 'act': nc.scalar, 'dve': nc.vector, 'pool': nc.gpsimd, 'sp': nc.sync}
        self.sems = {}
        self.cnt = {}
        for k in self.E:
            self.sems[k] = es.enter_context(nc.semaphore("s_" + k))
            self.cnt[k] = 0
        self.NDS = 12
        for q in ('pool', 'sp'):
            for i in range(self.NDS):
                key = ('d', q, i)
                self.sems[key] = es.enter_context(nc.semaphore(f"d_{q}{i}"))
                self.cnt[key] = 0
        self.dq = {'pool': 0, 'sp': 0}
        self.waited = {}
        self.W = {}
        self.R = {}
        self.bank_i = 0
        self.banks = []

    def _deps(self, e, r, w, is_dma):
        deps = {}

        def add(sk, v, raw):
            if sk == e and not is_dma and (e == 'pe' or not raw):
                return
            if deps.get(sk, 0) < v:
                deps[sk] = v
        for k in r:
            for sk, v in self.W.get(k, {}).items():
                add(sk, v, True)
        for k in w:
            for sk, v in self.W.get(k, {}).items():
                add(sk, v, False)
            for sk, v in self.R.get(k, {}).items():
                add(sk, v, False)
        return deps

    def _wait(self, e, deps):
        for sk, v in deps.items():
            if self.waited.get((e, sk), 0) >= v:
                continue
            self.E[e].wait_ge(self.sems[sk], v)
            self.waited[(e, sk)] = v

    def _reg(self, ev, r, w):
        sk, v = ev
        for k in r:
            self.R.setdefault(k, {})[sk] = v
        for k in w:
            self.W[k] = {sk: v}
            self.R[k] = {}

    def op(self, e, fn, r=(), w=()):
        self._wait(e, self._deps(e, r, w, False))
        ins = fn(self.E[e])
        ins.then_inc(self.sems[e], 1)
        self.cnt[e] += 1
        self._reg((e, self.cnt[e]), r, w)

    def mmg(self, fns, r=(), w=()):
        self._wait('pe', self._deps('pe', r, w, False))
        pe = self.E['pe']
        for f in fns[:-1]:
            f(pe)
        ins = fns[-1](pe)
        ins.then_inc(self.sems['pe'], 1)
        self.cnt['pe'] += 1
        self._reg(('pe', self.cnt['pe']), r, w)

    def dma(self, q, out, in_, r=(), w=(), **kw):
        self._wait(q, self._deps(q, r, w, True))
        i = self.dq[q] % self.NDS
        self.dq[q] += 1
        key = ('d', q, i)
        self.E[q].dma_start(out=out, in_=in_, **kw).then_inc(self.sems[key], 16)
        self.cnt[key] += 16
        self._reg((key, self.cnt[key]), r, w)

    def barrier(self):
        for e in self.E:
            for sk, v in self.cnt.items():
                if sk == e or v == 0:
                    continue
                if self.waited.get((e, sk), 0) >= v:
                    continue
                self.E[e].wait_ge(self.sems[sk], v)
                self.waited[(e, sk)] = v
        self.W = {}
        self.R = {}

    def bank(self):
        i = self.bank_i % 8
        self.bank_i += 1
        return self.banks[i], ('ps', i)


def build_program():
    nc = bass.Bass("TRN2", target_bir_lowering=False)
    dt = nc.dram_tensor
    x_in = dt("x", [NTOK, D], F32, kind="ExternalInput").ap()
    condT = dt("condT", [D, 2], F32, kind="ExternalInput").ap()
    mod_w = dt("mod_w", [2, D, 9 * D], F32, kind="ExternalInput").ap()
    mod_bT = dt("mod_bT", [128, 2, 72], F32, kind="ExternalInput").ap()
    vecs = dt("vecs", [128, 12, 8], F32, kind="ExternalInput").ap()
    ffn_w_in = dt("ffn_w_in", [2, 2, D, 2 * DFF], F32, kind="ExternalInput").ap()
    ffn_w_out = dt("ffn_w_out", [2, 2, DFF, D], F32, kind="ExternalInput").ap()
    identf = dt("identf", [128, 128], F32, kind="ExternalInput").ap()
    y_out = dt("y", [NTOK, D], F32, kind="ExternalOutput").ap()
    ab_w_in = dt("ab_w_in", [D, 2816], F32, kind="ExternalInput").ap()
    hvec = dt("hvec", [128, 8], F32, kind="ExternalInput").ap()
    ropeT = dt("ropeT", [128, 2, 4096], F32, kind="ExternalInput").ap()
    pmat = dt("pmat", [128, 128], F32, kind="ExternalInput").ap()
    rc = dt("rc", [128, 6 * 128 + 66], F32, kind="ExternalInput").ap()
    dec_in = dt("dec_in", [2, 8], F32, kind="ExternalInput").ap()
    st_f_in = dt("st_f", [8, 64, 64], F32, kind="ExternalInput").ap()
    st_b_in = dt("st_b", [8, 64, 64], F32, kind="ExternalInput").ap()
    o_gk = dt("o_gk", [1024, 128], F32, kind="ExternalOutput").ap()
    o_gv = dt("o_gv", [1024, 128], F32, kind="ExternalOutput").ap()
    o_rf = dt("o_rf", [4, 8, 64, 64], F32, kind="ExternalOutput").ap()
    o_rb = dt("o_rb", [4, 8, 64, 64], F32, kind="ExternalOutput").ap()
    ab_w_out = dt("ab_w_out", [D, D], F32, kind="ExternalInput").ap()
    na_w_qkv = dt("na_w_qkv", [D, 3 * D], F32, kind="ExternalInput").ap()
    na_w_out = dt("na_w_out", [D, D], F32, kind="ExternalInput").ap()
    cgk = dt("cgk", [256, 128], F32, kind="ExternalInput").ap()
    cgv = dt("cgv", [256, 128], F32, kind="ExternalInput").ap()
    cnk = dt("cnk", [256, 1024], F32, kind="ExternalInput").ap()
    cnv = dt("cnv", [256, 1024], F32, kind="ExternalInput").ap()
    o_nk = dt("o_nk", [1024, 1024], F32, kind="ExternalOutput").ap()
    o_nv = dt("o_nv", [1024, 1024], F32, kind="ExternalOutput").ap()
    rpb_pad = dt("rpb_pad", [64 + 16 * 15 * 31 + 64], F32, kind="ExternalInput").ap()
    nmask = dt("nmask", [128, 2, 64], F32, kind="ExternalInput").ap()
    identb = dt("identb", [128, 128], F32, kind="ExternalInput").ap()
    FM1 = dt("FM1", [16 * 128, NTOK + 256], BF16).ap()
    TM1 = dt("TM1", [NTOK, 16, 128], BF16).ap()
    XS = dt("XS", [D, NTOK], F32).ap()
    FM0 = dt("FM0", [17 * 128, NTOK + 256], BF16).ap()
    TM0 = dt("TM0", [NTOK, 1152], BF16).ap()
    YC = dt("YC", [D, NTOK], BF16).ap()

    with ExitStack() as es:
        p = P(nc, es)
        sb = lambda name, shape, dtype: es.enter_context(nc.sbuf_tensor(name, shape, dtype))
        psum_all = es.enter_context(nc.psum_tensor("psum_all", [128, 8, SUB], F32))
        for i in range(8):
            p.banks.append(psum_all[:, i, :])
        ident_f = sb("ident_f", [128, 128], F32)
        onesD = sb("onesD", [128, 128], BF16)
        eps_t = sb("eps_t", [128, 1], F32)
        scal = sb("scal", [128, 2 * 3 * 2 * 3, 8], F32)
        vec_sb = sb("vec_sb", [128, 12, 8], F32)
        p.dma('sp', ident_f[:], identf, w=['ident_f'])
        p.dma('sp', vec_sb[:], vecs, w=['vec_sb'])
        p.op('dve', lambda v: v.memset(onesD[:], 1.0 / D), w=['onesD'])
        p.op('dve', lambda v: v.memset(eps_t[:], EPS), w=['eps_t'])

        bones = sb("bones", [128, 128], BF16)
        pm_b = sb("pm_b", [128, 128], BF16)
        hv = sb("hv", [128, 8], F32)
        p.op('dve', lambda v: v.memset(bones[:], 0.0), w=['bones'])
        p.op('dve', lambda v: v.memset(bones[0:64, 0:64], 1.0 / 64), w=['bones'])
        p.op('dve', lambda v: v.memset(bones[64:128, 64:128], 1.0 / 64), w=['bones'])
        p.dma('pool', pm_b[:], pmat, w=['pm_b'])
        p.dma('sp', hv[:], hvec, w=['hv'])

        def sc(l, s, c, which):
            return scal[:, ((l * 3 + s) * 2 + c) * 3 + which, :]

        import os
        DBG = int(os.environ.get("KDEBUG", "9"))
        cond_f = sb("cond_f", [128, 8, 2], F32)
        cond_b = sb("cond_b", [128, 8, 2], BF16)
        modb = sb("modb", [128, 2, 72], F32)
        modsb = sb("modsb", [128, 72, 2], F32)
        p.dma('sp', cond_f[:], condT.rearrange("(k p) c -> p k c", p=128), w=['cond_f'])
        p.dma('sp', modb[:], mod_bT, w=['modb'])
        p.op('act', lambda a: a.activation(out=cond_b[:], in_=cond_f[:], func=AF.Silu), r=['cond_f'], w=['cond_b'])

        def mod_layer(l, mwbufs, ncol, bk, bkey, tag):
            jper = ncol // 128
            for g in range(72 // jper):
                buf = mwbufs[g % 2]
                p.dma('pool', buf[:, :, 0:ncol], mod_w[l, :, g * ncol:(g + 1) * ncol].rearrange("(k p) n -> p k n", p=128),
                      w=[(tag, g % 2)])
                for jj in range(jper):
                    j = g * jper + jj
                    p.mmg([(lambda t, k=k: t.matmul(bk[:, 2 * j:2 * j + 2], lhsT=buf[:, k, jj * 128:(jj + 1) * 128],
                                                    rhs=cond_b[:, k, :], start=(k == 0), stop=(k == 7))) for k in range(8)],
                          r=[(tag, g % 2), 'cond_b'], w=[bkey])
                yield
            for c in range(2):
                p.op('dve', lambda v, c=c: v.tensor_tensor(
                    out=modsb[:, :, c], in0=bk[:, 0:144].rearrange("p (j c) -> p j c", c=2)[:, :, c],
                    in1=modb[:, l, :], op=ALU.add), r=[bkey, 'modb'], w=[('modsb', c)])
            for s in range(3):
                wgt = 1.0 if s == 1 else 0.5
                for c in range(2):
                    shift = modsb[:, (3 * s) * 8:(3 * s) * 8 + 8, c]
                    scale = modsb[:, (3 * s + 1) * 8:(3 * s + 1) * 8 + 8, c]
                    gate = modsb[:, (3 * s + 2) * 8:(3 * s + 2) * 8 + 8, c]
                    gpre = vec_sb[:, l * 3 + s, :]
                    gpost = vec_sb[:, 6 + l * 3 + s, :]
                    p.op('dve', lambda v: v.scalar_tensor_tensor(
                        out=sc(l, s, c, 0), in0=scale, scalar=1.0, in1=gpre, op0=ALU.add, op1=ALU.mult),
                        r=[('modsb', c), 'vec_sb'], w=['scal'])
                    p.op('dve', lambda v: v.tensor_copy(out=sc(l, s, c, 1), in_=shift), r=[('modsb', c)], w=['scal'])
                    p.op('dve', lambda v: v.scalar_tensor_tensor(
                        out=sc(l, s, c, 2), in0=gate, scalar=wgt, in1=gpost, op0=ALU.mult, op1=ALU.mult),
                        r=[('modsb', c), 'vec_sb'], w=['scal'])
            yield

        with ExitStack() as es0:
            mw = [es0.enter_context(nc.sbuf_tensor(f"mw{i}", [128, 8, 1152], BF16)) for i in range(2)]
            bk0, bk0k = p.bank()
            for _ in mod_layer(0, mw, 1152, bk0, bk0k, 'mw'):
                pass
            p.barrier()

        def tile_scope(which):
            with ExitStack() as esT:
                sbT = lambda name, shape, dtype: esT.enter_context(nc.sbuf_tensor(name + '_' + which, shape, dtype))
                xT = sbT("xT", [128, 8, TS], F32)
                hy = sbT("hy", [128, 8, TS], F32)
                def hTv(k, u):
                    return hy[:, k, u * SUB:u * SUB + SUB // 2].bitcast(BF16)
                aT = sbT("aT", [128, NJ, TS], BF16)
                sq = sbT("sq", [128, 8, SUB], BF16)
                rstd = [sbT(f"rstd{i}", [128, SUB], F32) for i in range(2)]
                tmpf = [sbT(f"tmpf{i}", [128, SUB], F32) for i in range(2)]
                sil = [sbT(f"sil{i}", [128, SUB], F32) for i in range(2)]
                NWB = 3
                wbuf = [sbT(f"wbuf{i}", [128, 8, 2, 256], BF16) for i in range(NWB)]
                wobuf = [sbT(f"wobuf{i}", [128, NJ, 256], BF16) for i in range(2)]
                cnts = {'wb': 0, 'wo': 0, 'rs': 0, 'tf': 0, 'sl': 0}

                def rot(name, n):
                    i = cnts[name] % n
                    cnts[name] += 1
                    return i

                def rstd_of(src_fn, src_keys, nchunks, lhsT_ones, lkey):
                    for k in range(nchunks):
                        p.op('act', lambda a, k=k: a.activation(out=sq[:, k, :], in_=src_fn(k), func=AF.Square),
                             r=[src_keys[k]], w=[('sq', k)])
                    bk, bkey = p.bank()
                    p.mmg([(lambda t, k=k: t.matmul(bk[:], lhsT=lhsT_ones, rhs=sq[:, k, :], start=(k == 0),
                                                    stop=(k == nchunks - 1))) for k in range(nchunks)],
                          r=[('sq', k) for k in range(nchunks)] + [lkey], w=[bkey])
                    ri = rot('rs', 2)
                    p.op('act', lambda a: a.activation(out=rstd[ri][:], in_=bk[:], func=AF.Ln, bias=eps_t[:], scale=1.0),
                         r=[bkey, 'eps_t'], w=[('rstd', ri)])
                    p.op('act', lambda a: a.activation(out=rstd[ri][:], in_=rstd[ri][:], func=AF.Exp, scale=-0.5),
                         r=[('rstd', ri)], w=[('rstd', ri)])
                    return ri

                HPEND = {}

                def ensure_h(u):
                    if u in HPEND:
                        HPEND.pop(u)()

                PRE = {'done': None, 'next': None}

                def prenorm(l, s, c):
                    if PRE['done'] == (l, s, c):
                        PRE['done'] = None
                        return
                    for u in range(NS):
                        ensure_h(u)
                        HPEND[u] = (lambda u=u: prenorm_u(l, s, c, u))

                def early_pre0():
                    if PRE['next'] is not None:
                        l, s, c = PRE['next']
                        PRE['next'] = None
                        prenorm(l, s, c)
                        PRE['done'] = (l, s, c)
                        ensure_h(0)

                def prenorm_u(l, s, c, u):
                    if True:
                        cols = slice(u * SUB, (u + 1) * SUB)
                        ri = rstd_of(lambda k: xT[:, k, cols], [('xT', k, u) for k in range(8)], 8, onesD[:], 'onesD')
                        for k in range(8):
                            ti = rot('tf', 2)
                            p.op('dve', lambda v, k=k, ti=ti: v.scalar_tensor_tensor(
                                out=tmpf[ti][:], in0=xT[:, k, cols], scalar=sc(l, s, c, 0)[:, k:k + 1], in1=rstd[ri][:],
                                op0=ALU.mult, op1=ALU.mult), r=[('xT', k, u), ('rstd', ri), 'scal'], w=[('tmpf', ti)])
                            p.op('act', lambda a, k=k, ti=ti: a.activation(
                                out=hTv(k, u), in_=tmpf[ti][:], func=AF.Identity, bias=sc(l, s, c, 1)[:, k:k + 1], scale=1.0),
                                r=[('tmpf', ti), 'scal'], w=[('hy', u)])

                def postres(l, s, c, u):
                    cols = slice(u * SUB, (u + 1) * SUB)
                    bk, bkey = p.bank()
                    p.mmg([(lambda t, k=k: t.matmul(bk[:], lhsT=onesD[:], rhs=sq[:, k, :], start=(k == 0), stop=(k == 7)))
                           for k in range(8)], r=[('sq', k) for k in range(8)] + ['onesD'], w=[bkey])
                    ri = rot('rs', 2)
                    p.op('act', lambda a: a.activation(out=rstd[ri][:], in_=bk[:], func=AF.Ln, bias=eps_t[:], scale=1.0),
                         r=[bkey, 'eps_t'], w=[('rstd', ri)])
                    p.op('act', lambda a: a.activation(out=rstd[ri][:], in_=rstd[ri][:], func=AF.Exp, scale=-0.5),
                         r=[('rstd', ri)], w=[('rstd', ri)])
                    for k in range(8):
                        ti = rot('tf', 2)
                        p.op('dve', lambda v, k=k, ti=ti: v.tensor_tensor(out=tmpf[ti][:], in0=hy[:, k, cols], in1=rstd[ri][:],
                                                                          op=ALU.mult),
                             r=[('hy', u), ('rstd', ri)], w=[('tmpf', ti)])
                        p.op('dve', lambda v, k=k, ti=ti: v.scalar_tensor_tensor(
                            out=xT[:, k, cols], in0=tmpf[ti][:], scalar=sc(l, s, c, 2)[:, k:k + 1], in1=xT[:, k, cols],
                            op0=ALU.mult, op1=ALU.add), r=[('tmpf', ti), ('xT', k, u), 'scal'], w=[('xT', k, u)])

                def evac_y(bk, bkey, m, u):
                    cols = slice(u * SUB, (u + 1) * SUB)
                    p.op('act', lambda a: a.activation(out=sq[:, m, :], in_=bk[:], func=AF.Square), r=[bkey], w=[('sq', m)])
                    p.op('dve', lambda v: v.tensor_copy(out=hy[:, m, cols], in_=bk[:]), r=[bkey], w=[('hy', u)])

                def ffn(l, i, s, c, after_mm1=None):
                    prenorm(l, s, c)
                    w_in = ffn_w_in[l, i]
                    w_out = ffn_w_out[l, i]
                    wmap = {}

                    def load_in(jp):
                        wi = rot('wb', NWB)
                        wb = wbuf[wi]
                        for half in range(2):
                            p.dma('pool', wb[:, :, half, :],
                                  w_in[:, half * DFF + jp * 256: half * DFF + (jp + 1) * 256].rearrange("(k p) n -> p k n", p=128),
                                  w=[('wbuf', wi)])
                        wmap[jp] = wi

                    def mm1(jp, u):
                        wi = wmap[jp]
                        wb = wbuf[wi]
                        ensure_h(u)
                        for jj in range(2):
                            j = jp * 2 + jj
                            cols = slice(u * SUB, (u + 1) * SUB)
                            bg, bgk = p.bank()
                            bu, buk = p.bank()
                            p.mmg([(lambda t, k=k: t.matmul(bg[:], lhsT=wb[:, k, 0, jj * 128:(jj + 1) * 128], rhs=hTv(k, u),
                                                            start=(k == 0), stop=(k == 7))) for k in range(8)],
                                  r=[('wbuf', wi), ('hy', u)], w=[bgk])
                            p.mmg([(lambda t, k=k: t.matmul(bu[:], lhsT=wb[:, k, 1, jj * 128:(jj + 1) * 128], rhs=hTv(k, u),
                                                            start=(k == 0), stop=(k == 7))) for k in range(8)],
                                  r=[('wbuf', wi), ('hy', u)], w=[buk])
                            si = rot('sl', 2)
                            p.op('act', lambda a, si=si, bg=bg: a.activation(out=sil[si][:], in_=bg[:], func=AF.Silu),
                                 r=[bgk], w=[('sil', si)])
                            p.op('dve', lambda v, si=si, bu=bu, j=j, cols=cols: v.tensor_tensor(
                                out=aT[:, j, cols], in0=bu[:], in1=sil[si][:], op=ALU.mult),
                                r=[buk, ('sil', si)], w=[('aT', j, u)])
                    load_in(0)
                    load_in(1)
                    mm1(0, 0)
                    mm1(1, 0)
                    load_in(2)
                    mm1(0, 1)
                    mm1(1, 1)
                    for jp in range(2, NJ // 2):
                        if jp + 1 < NJ // 2:
                            load_in(jp + 1)
                        mm1(jp, 0)
                        mm1(jp, 1)
                    omap = {}

                    def load_out(mp):
                        wi = rot('wo', 2)
                        p.dma('pool', wobuf[wi][:], w_out[:, mp * 256:(mp + 1) * 256].rearrange("(j p) n -> p j n", p=128),
                              w=[('wobuf', wi)])
                        omap[mp] = wi

                    def mm2(mp, u):
                        wi = omap[mp]
                        wo = wobuf[wi]
                        for mm in range(2):
                            m = mp * 2 + mm
                            cols = slice(u * SUB, (u + 1) * SUB)
                            bk, bkey = p.bank()
                            p.mmg([(lambda t, j=j: t.matmul(bk[:], lhsT=wo[:, j, mm * 128:(mm + 1) * 128], rhs=aT[:, j, cols],
                                                            start=(j == 0), stop=(j == NJ - 1))) for j in range(NJ)],
                                  r=[('wobuf', wi)] + [('aT', j, u) for j in range(NJ)], w=[bkey])
                            evac2(bk, bkey, m, u)
                    load_out(0)
                    load_out(1)
                    mm2(0, 0)
                    mm2(0, 1)
                    load_out(2)
                    mm2(1, 0)
                    mm2(1, 1)
                    load_out(3)
                    mm2(2, 0)
                    mm2(3, 0)
                    finish(l, s, c, 0)
                    mm2(2, 1)
                    early_pre0()
                    mm2(3, 1)
                    if after_mm1 is not None:
                        after_mm1()
                    finish(l, s, c, 1)

                sq2 = [sq, sbT("sq_b", [128, 8, SUB], BF16)]

                def evac2(bk, bkey, m, u):
                    cols = slice(u * SUB, (u + 1) * SUB)
                    p.op('dve', lambda v: v.tensor_copy(out=hy[:, m, cols], in_=bk[:]), r=[bkey], w=[('hy', u), ('yT', u), 'hy2'])
                    p.op('act', lambda a: a.activation(out=sq2[u][:, m, :], in_=hy[:, m, cols], func=AF.Square),
                         r=[('hy', u)], w=[('sq', u, m)])

                def finish(l, s, c, u):
                    cols = slice(u * SUB, (u + 1) * SUB)
                    bk, bkey = p.bank()
                    p.mmg([(lambda t, k=k: t.matmul(bk[:], lhsT=onesD[:], rhs=sq2[u][:, k, :], start=(k == 0), stop=(k == 7)))
                           for k in range(8)], r=[('sq', u, k) for k in range(8)] + ['onesD'], w=[bkey])
                    ri = rot('rs', 2)
                    p.op('act', lambda a: a.activation(out=rstd[ri][:], in_=bk[:], func=AF.Ln, bias=eps_t[:], scale=1.0),
                         r=[bkey, 'eps_t'], w=[('rstd', ri)])
                    p.op('act', lambda a: a.activation(out=rstd[ri][:], in_=rstd[ri][:], func=AF.Exp, scale=-0.5),
                         r=[('rstd', ri)], w=[('rstd', ri)])
                    for k in range(8):
                        ti = rot('tf', 2)
                        p.op('dve', lambda v, k=k, ti=ti: v.tensor_tensor(out=tmpf[ti][:], in0=hy[:, k, cols], in1=rstd[ri][:],
                                                                          op=ALU.mult),
                             r=[('hy', u), ('yT', u), ('rstd', ri)], w=[('tmpf', ti)])
                        p.op('dve', lambda v, k=k, ti=ti: v.scalar_tensor_tensor(
                            out=xT[:, k, cols], in0=tmpf[ti][:], scalar=sc(l, s, c, 2)[:, k:k + 1], in1=xT[:, k, cols],
                            op0=ALU.mult, op1=ALU.add), r=[('tmpf', ti), ('xT', k, u), 'scal'], w=[('xT', k, u)])

                def load_x_tokmajor(t):
                    for u in range(NS):
                        p.dma('sp', hy[:, u * 4:(u + 1) * 4, :],
                              x_in[t * TS + u * SUB:t * TS + (u + 1) * SUB, :].rearrange("(g p) d -> p g d", p=128),
                              w=[('hy', 0), ('hy', 1), 'hy2', ('xin', u)])
                    for u in range(NS):
                        for k in range(8):
                            bk, bkey = p.bank()
                            p.mmg([(lambda tt, gi=gi: tt.transpose(bk[:, gi * 128:(gi + 1) * 128],
                                                                   hy[:, u * 4 + gi, k * 128:(k + 1) * 128], ident_f[:]))
                                   for gi in range(4)], r=[('xin', u), 'ident_f'], w=[bkey])
                            eng = 'act' if (k + u) % 2 == 0 else 'dve'
                            if eng == 'act':
                                p.op('act', lambda a, k=k, u=u: a.copy(out=xT[:, k, u * SUB:(u + 1) * SUB], in_=bk[:]),
                                     r=[bkey], w=[('xT', k, u)])
                            else:
                                p.op('dve', lambda v, k=k, u=u: v.tensor_copy(out=xT[:, k, u * SUB:(u + 1) * SUB], in_=bk[:]),
                                     r=[bkey], w=[('xT', k, u)])

                def store_y_tokmajor(t):
                    for g in range(8):
                        u = g // 4
                        for half in range(2):
                            bk, bkey = p.bank()
                            p.mmg([(lambda tt, kk=kk: tt.transpose(bk[:, kk * 128:(kk + 1) * 128],
                                                                   xT[:, half * 4 + kk, g * 128:(g + 1) * 128], ident_f[:]))
                                   for kk in range(4)], r=[('xT', half * 4 + kk, u) for kk in range(4)] + ['ident_f'], w=[bkey])
                            eng = 'act' if half == 0 else 'dve'
                            if eng == 'act':
                                p.op('act', lambda a, g=g, half=half: a.copy(out=hy[:, g, half * 512:(half + 1) * 512], in_=bk[:]),
                                     r=[bkey], w=[('hy', u), 'hy2'])
                            else:
                                p.op('dve', lambda v, g=g, half=half: v.tensor_copy(out=hy[:, g, half * 512:(half + 1) * 512],
                                                                                    in_=bk[:]), r=[bkey], w=[('hy', u), 'hy2'])
                    if t + 1 < NT:
                        load_x_scratch(t + 1)
                    p.dma('sp', y_out[t * TS:(t + 1) * TS, :].rearrange("(g p) d -> p g d", p=128), hy[:],
                          r=[('hy', 0), ('hy', 1)])

                stgb = [sbT(f"stgb{i}", [128, TS], BF16) for i in range(2)]
                sqn = [sbT(f"sqn{i}", [128, SUB], BF16) for i in range(2)]
                qnb = [sbT(f"qnb{i}", [128, SUB], BF16) for i in range(2)]
                rope_sb = [sbT(f"rope_sb{i}", [128, 2, SUB], F32) for i in range(2)]
                ostg = sq2[1][:, 0:4, :].bitcast(F32).rearrange("p a (b c) -> p (a b) c", b=2)
                ostg2 = sq2[1][:, 4:8, :].bitcast(F32).rearrange("p a (b c) -> p (a b) c", b=2)
                OSTG_KEYS = [('sq', 1, m) for m in range(8)]
                tstg = aT[:, 0:9, :].rearrange("p a b -> p (a b)").rearrange("p (g n) -> p g n", g=8)
                TSTG_KEYS = [('aT', j, u) for j in range(9) for u in range(NS)]
                cnts.update({'sg': 0, 'sn': 0, 'qb': 0})

                def wflat(wb):
                    return wb[:].rearrange("p k a b -> p k (a b)")

                def proj_fm(W, col0, nch, t, evac):
                    wi = rot('wb', NWB)
                    wb = wflat(wbuf[wi])
                    p.dma('pool', wb[:, :, 0:nch * 128], W[:, col0:col0 + nch * 128].rearrange("(k p) n -> p k n", p=128),
                          w=[('wbuf', wi)])
                    if 1 in HPEND:
                        ensure_h(0)
                    for ci in range(nch):
                        for u in range(NS):
                            ensure_h(u)
                            cols = slice(u * SUB, (u + 1) * SUB)
                            bk, bkey = p.bank()
                            p.mmg([(lambda tt, k=k: tt.matmul(bk[:], lhsT=wb[:, k, ci * 128:(ci + 1) * 128], rhs=hTv(k, u),
                                                              start=(k == 0), stop=(k == 7))) for k in range(8)],
                                  r=[('wbuf', wi), ('hy', u)], w=[bkey])
                            evac(ci, u, bk, bkey)

                def proj_tm(W, col0, ncol, evac):
                    wi = rot('wb', NWB)
                    wb = wflat(wbuf[wi])
                    p.dma('pool', wb[:, :, 0:ncol], W[:, col0:col0 + ncol].rearrange("(k p) n -> p k n", p=128),
                          w=[('wbuf', wi)])
                    for g in range(8):
                        u = g // 4
                        ensure_h(u)
                        bk, bkey = p.bank()
                        p.mmg([(lambda tt, k=k: tt.matmul(bk[:, 0:ncol], lhsT=hTv(k, u)[:, (g % 4) * 128:(g % 4 + 1) * 128], rhs=wb[:, k, 0:ncol],
                                                          start=(k == 0), stop=(k == 7))) for k in range(8)],
                              r=[('wbuf', wi), ('hy', u)], w=[bkey])
                        evac(g, bk, bkey)

                def fm_store(FM, chunk, t, si):
                    p.dma('sp', FM[chunk * 128:(chunk + 1) * 128, t * TS:(t + 1) * TS], stgb[si][:], r=[('stgb', si)])

                def simple_evac(FM, chunk0, t, kind):
                    st = {}

                    def ev(ci, u, bk, bkey):
                        if u == 0:
                            st['si'] = rot('sg', 2)
                        si = st['si']
                        cols = slice(u * SUB, (u + 1) * SUB)
                        if kind == 'copy':
                            p.op('act', lambda a: a.copy(out=stgb[si][:, cols], in_=bk[:]), r=[bkey], w=[('stgb', si)])
                        elif kind == 'scale':
                            p.op('act', lambda a: a.mul(out=stgb[si][:, cols], in_=bk[:], mul=0.125), r=[bkey], w=[('stgb', si)])
                        elif kind == 'silu':
                            p.op('act', lambda a: a.activation(out=stgb[si][:, cols], in_=bk[:], func=AF.Silu), r=[bkey],
                                 w=[('stgb', si)])
                        if u == NS - 1:
                            fm_store(FM, chunk0 + ci, t, si)
                    return ev

                def normrope_evac(FM, chunk0, t, c, wcol, outk):
                    st = {}

                    def ev(ci, u, bk, bkey):
                        if u == 0:
                            st['si'] = rot('sg', 2)
                        si = st['si']
                        cols = slice(u * SUB, (u + 1) * SUB)
                        t0 = rot('tf', 2)
                        p.op('act', lambda a: a.copy(out=tmpf[t0][:], in_=bk[:]), r=[bkey], w=[('tmpf', t0)])
                        sn = rot('sn', 2)
                        p.op('act', lambda a: a.activation(out=sqn[sn][:], in_=tmpf[t0][:], func=AF.Square),
                             r=[('tmpf', t0)], w=[('sqn', sn)])
                        b2, b2k = p.bank()
                        p.mmg([lambda tt: tt.matmul(b2[:], lhsT=bones[:], rhs=sqn[sn][:], start=True, stop=True)],
                              r=[('sqn', sn), 'bones'], w=[b2k])
                        ri = rot('rs', 2)
                        p.op('act', lambda a: a.activation(out=rstd[ri][:], in_=b2[:], func=AF.Ln, bias=eps_t[:], scale=1.0),
                             r=[b2k, 'eps_t'], w=[('rstd', ri)])
                        p.op('act', lambda a: a.activation(out=rstd[ri][:], in_=rstd[ri][:], func=AF.Exp, scale=-0.5),
                             r=[('rstd', ri)], w=[('rstd', ri)])
                        p.op('dve', lambda v: v.scalar_tensor_tensor(out=tmpf[t0][:], in0=tmpf[t0][:], scalar=hv[:, wcol:wcol + 1],
                                                                     in1=rstd[ri][:], op0=ALU.mult, op1=ALU.mult),
                             r=[('tmpf', t0), ('rstd', ri), 'hv'], w=[('tmpf', t0)])
                        if outk and c == 1:
                            b3, b3k = p.bank()
                            p.mmg([(lambda tt, g=g: tt.transpose(b3[:, g * 128:(g + 1) * 128], tmpf[t0][:, g * 128:(g + 1) * 128],
                                                                 ident_f[:])) for g in range(4)],
                                  r=[('tmpf', t0), 'ident_f'], w=[b3k])
                            p.op('dve', lambda v: v.tensor_copy(out=ostg[:, u * 4:(u + 1) * 4, :],
                                                                in_=b3[:].rearrange("p (g n) -> p g n", g=4)),
                                 r=[b3k], w=OSTG_KEYS)
                            if u == NS - 1:
                                p.dma('sp', o_gk.rearrange("(g p) n -> p g n", p=128), ostg, r=OSTG_KEYS)
                        if c == 0:
                            qi = rot('qb', 2)
                            p.op('act', lambda a: a.copy(out=qnb[qi][:], in_=tmpf[t0][:]), r=[('tmpf', t0)], w=[('qnb', qi)])
                            b3, b3k = p.bank()
                            p.mmg([lambda tt: tt.matmul(b3[:], lhsT=pm_b[:], rhs=qnb[qi][:], start=True, stop=True)],
                                  r=[('qnb', qi), 'pm_b'], w=[b3k])
                            t1 = rot('tf', 2)
                            p.op('dve', lambda v: v.tensor_tensor(out=tmpf[t1][:], in0=b3[:], in1=rope_sb[u][:, 1, :], op=ALU.mult),
                                 r=[b3k, ('rope', u)], w=[('tmpf', t1)])
                            p.op('dve', lambda v: v.tensor_tensor(out=tmpf[t0][:], in0=tmpf[t0][:], in1=rope_sb[u][:, 0, :],
                                                                  op=ALU.mult), r=[('tmpf', t0), ('rope', u)], w=[('tmpf', t0)])
                            p.op('dve', lambda v: v.tensor_tensor(out=stgb[si][:, cols], in0=tmpf[t0][:], in1=tmpf[t1][:],
                                                                  op=ALU.add), r=[('tmpf', t0), ('tmpf', t1)], w=[('stgb', si)])
                        else:
                            p.op('act', lambda a: a.copy(out=stgb[si][:, cols], in_=tmpf[t0][:]), r=[('tmpf', t0)],
                                 w=[('stgb', si)])
                        if u == NS - 1:
                            fm_store(FM, chunk0 + ci, t, si)
                    return ev

                def inproj_ab(t, c):
                    W = ab_w_in
                    if c == 0:
                        for u in range(NS):
                            p.dma('sp', rope_sb[u][:], ropeT[:, :, t * TS + u * SUB:t * TS + (u + 1) * SUB], w=[('rope', u)])
                    proj_fm(W, 0, 4, t, simple_evac(FM0, 0, t, 'copy'))
                    proj_fm(W, 512, 4, t, simple_evac(FM0, 4, t, 'scale'))
                    proj_fm(W, 1536, 4, t, simple_evac(FM0, 8, t, 'silu'))
                    proj_fm(W, 2048, 4, t, normrope_evac(FM0, 12, t, c, 0, False))
                    proj_fm(W, 2560, 1, t, normrope_evac(FM0, 16, t, c, 1, True))

                    def ev_k(g, bk, bkey):
                        p.op('act', lambda a: a.mul(out=tstg[:, g, 0:512], in_=bk[:], mul=0.125), r=[bkey], w=TSTG_KEYS)

                    def ev_v(g, bk, bkey):
                        p.op('dve', lambda v: v.tensor_copy(out=tstg[:, g, 512:1024], in_=bk[:]), r=[bkey], w=TSTG_KEYS)

                    def ev_gv(g, bk, bkey):
                        p.op('dve', lambda v: v.tensor_copy(out=tstg[:, g, 1024:1152], in_=bk[:, 0:128]), r=[bkey], w=TSTG_KEYS)
                        if c == 1:
                            p.op('dve', lambda v: v.tensor_copy(out=ostg2[:, g, :], in_=bk[:, 0:128]), r=[bkey], w=OSTG_KEYS)
                    proj_tm(W, 512, 512, ev_k)
                    proj_tm(W, 1024, 512, ev_v)
                    proj_tm(W, 2688, 128, ev_gv)
                    if c == 1:
                        p.dma('sp', o_gv.rearrange("(g p) n -> p g n", p=128), ostg2, r=OSTG_KEYS)
                    p.dma('sp', TM0[t * TS:(t + 1) * TS, :].rearrange("(g p) n -> p g n", p=128), tstg, r=TSTG_KEYS)

                if which == 'A':
                    for t in range(NT):
                        c = 0 if t < 4 else 1
                        load_x_tokmajor(t)
                        PRE['next'] = (0, 1, c)
                        ffn(0, 0, 0, c)
                        prenorm(0, 1, c)
                        inproj_ab(t, c)
                        p.dma('sp', XS.rearrange("(k p) n -> p k n", p=128)[:, :, t * TS:(t + 1) * TS], xT[:],
                              r=[('xT', k, u) for k in range(8) for u in range(NS)])
                    p.barrier()
                def load_x_scratch(t):
                    p.dma('sp', xT[:], XS.rearrange("(k p) n -> p k n", p=128)[:, :, t * TS:(t + 1) * TS],
                          w=[('xT', k, u) for k in range(8) for u in range(NS)])

                def load_ycat(t):
                    p.dma('sp', aT[:, 0:8, :], YC.rearrange("(k p) n -> p k n", p=128)[:, :, t * TS:(t + 1) * TS],
                          w=[('aT', j, u) for j in range(8) for u in range(NS)])

                def outproj(Wo, l, c):
                    omap = {}

                    def load_out(mp):
                        wi = rot('wo', 2)
                        p.dma('pool', wobuf[wi][:, 0:8, :], Wo[:, mp * 256:(mp + 1) * 256].rearrange("(j p) n -> p j n", p=128),
                              w=[('wobuf', wi)])
                        omap[mp] = wi

                    def mmo(mp, u):
                        wi = omap[mp]
                        wo = wobuf[wi]
                        for mm in range(2):
                            m = mp * 2 + mm
                            cols = slice(u * SUB, (u + 1) * SUB)
                            bk, bkey = p.bank()
                            p.mmg([(lambda t_, j=j: t_.matmul(bk[:], lhsT=wo[:, j, mm * 128:(mm + 1) * 128], rhs=aT[:, j, cols],
                                                              start=(j == 0), stop=(j == 7))) for j in range(8)],
                                  r=[('wobuf', wi)] + [('aT', j, u) for j in range(8)], w=[bkey])
                            evac2(bk, bkey, m, u)
                    load_out(0)
                    load_out(1)
                    mmo(0, 0)
                    mmo(0, 1)
                    load_out(2)
                    mmo(1, 0)
                    mmo(1, 1)
                    load_out(3)
                    mmo(2, 0)
                    mmo(3, 0)
                    finish(l, 1, c, 0)
                    mmo(2, 1)
                    early_pre0()
                    mmo(3, 1)
                    finish(l, 1, c, 1)

                tstg1 = aT[:, 0:16, :].rearrange("p a b -> p (a b)").rearrange("p (g h e) -> p g h e", g=8, h=16)
                TSTG1_KEYS = [('aT', j, u) for j in range(16) for u in range(NS)]
                def ostgN(gq, cb):
                    return hy[:, gq * 2 + cb, :].rearrange("p (u h w) -> p u h w", u=2, h=2)[:, :, 1, :]
                ostgN_all = bass.AP(hy, 256, [[8 * TS, 128], [512, 16], [1, 256]])

                def inproj_na(t, c):
                    W = na_w_qkv
                    proj_fm(W, 0, 4, t, simple_evac(FM1, 0, t, 'copy'))
                    proj_fm(W, 512, 4, t, simple_evac(FM1, 4, t, 'copy'))
                    proj_fm(W, 1024, 4, t, simple_evac(FM1, 8, t, 'copy'))
                    proj_fm(W, 1536, 4, t, simple_evac(FM1, 12, t, 'copy'))
                    for g in range(8):
                        p.op('dve', lambda v, g=g: v.memset(tstg1[:, g, :, 64:128], 1.0), w=TSTG1_KEYS)
                    for cb in range(2):
                        def ev_v(g, bk, bkey, cb=cb):
                            p.op('dve', lambda v: v.tensor_copy(out=tstg1[:, g, cb * 8:(cb + 1) * 8, 0:64],
                                                                in_=bk[:].rearrange("p (h e) -> p h e", h=8)), r=[bkey],
                                 w=TSTG1_KEYS)
                        proj_tm(W, 2048 + cb * 512, 512, ev_v)
                    p.dma('sp', TM1[t * TS:(t + 1) * TS, :, :].rearrange("(g p) h e -> p g (h e)", p=128),
                          tstg1.rearrange("p g h e -> p g (h e)"), r=TSTG1_KEYS)
                    if c == 1:
                        for which, o_d in ((1, o_nk), (2, o_nv)):
                            for gh in range(2):
                                for cb in range(2):
                                    def ev_o(g, bk, bkey, cb=cb, gh=gh):
                                        if g // 4 == gh:
                                            p.op('act', lambda a: a.copy(out=ostgN(g % 4, cb), in_=bk[:].rearrange("p (u w) -> p u w", u=2)),
                                                 r=[bkey], w=['hy2'])
                                    proj_tm(W, which * 1024 + cb * 512, 512, ev_o)
                                for gq in range(4):
                                    p.dma('sp', o_d[gh * 512 + gq * 128:gh * 512 + (gq + 1) * 128, :].rearrange("p (q w) -> p q w", w=256),
                                          bass.AP(hy, 256 + gq * 2048, [[8 * TS, 128], [512, 4], [1, 256]]), r=['hy2'])

                if which == 'C':
                    load_x_scratch(0)
                    for t in range(NT):
                        c = 0 if t < 4 else 1
                        load_ycat(t)
                        PRE['next'] = (0, 2, c)
                        outproj(ab_w_out, 0, c)
                        PRE['next'] = (1, 0, c)
                        ffn(0, 1, 2, c)
                        PRE['next'] = (1, 1, c)
                        ffn(1, 0, 0, c)
                        prenorm(1, 1, c)
                        ensure_h(0)
                        ensure_h(1)
                        p.dma('sp', XS.rearrange("(k p) n -> p k n", p=128)[:, :, t * TS:(t + 1) * TS], xT[:],
                              r=[('xT', k, u) for k in range(8) for u in range(NS)])
                        if t + 1 < NT:
                            load_x_scratch(t + 1)
                        inproj_na(t, c)
                    p.barrier()
                if which == 'E':
                    load_x_scratch(0)
                    load_ycat(0)
                    for t in range(NT):
                        c = 0 if t < 4 else 1
                        PRE['next'] = (1, 2, c)
                        outproj(na_w_out, 1, c)
                        ffn(1, 1, 2, c, after_mm1=(lambda t=t: load_ycat(t + 1)) if t + 1 < NT else None)
                        store_y_tokmajor(t)
                    p.barrier()
        tile_scope('A')
        with ExitStack() as esB:
            sbB = lambda name, shape, dtype: esB.enter_context(nc.sbuf_tensor(name, shape, dtype))
            rc_sb = sbB("rc_sb", [128, 6 * 128 + 66], F32)
            n1, cn, dpos, dneg = rc_sb[:, 0:128], rc_sb[:, 128:256], rc_sb[:, 256:384], rc_sb[:, 384:512]
            mge, mlt = rc_sb[:, 512:640], rc_sb[:, 640:768]
            posf, posb, c128 = rc_sb[:, 768:769], rc_sb[:, 769:770], rc_sb[:, 770:834]
            lg = sbB("lg", [128, 2, 8], F32)
            DM = sbB("DM", [128, 8, 128], F32)
            tmpd = sbB("tmpd", [128, 128], F32)
            QF = sbB("QF", [128, 4, 128], F32)
            QB = sbB("QB", [128, 4, 128], F32)
            KF = sbB("KF", [128, 8], F32)
            KB = sbB("KB", [128, 8], F32)
            CDF = sbB("CDF", [128, 4, 64], F32)
            CDB = sbB("CDB", [128, 4, 64], F32)
            eps5 = sbB("eps5", [128, 1], F32)
            p.op('dve', lambda v: v.memset(eps5[:], 1e-5), w=['eps5'])
            p.dma('sp', rc_sb[:], rc, w=['rc'])
            p.dma('sp', lg[:].rearrange("p a b -> p (a b)"), bass.AP(dec_in.tensor, 0, [[0, 128], [1, 16]]), w=['lg'])
            lgf = lg[:].rearrange("p a b -> p (a b)")
            p.op('act', lambda a: a.activation(out=lgf, in_=lgf, func=AF.Exp, scale=-1.0), r=['lg'], w=['lg'])
            p.op('dve', lambda v: v.tensor_scalar_add(out=lgf, in0=lgf, scalar1=1.0), r=['lg'], w=['lg'])
            p.op('act', lambda a: a.activation(out=lgf, in_=lgf, func=AF.Ln), r=['lg'], w=['lg'])
            p.op('dve', lambda v: v.tensor_scalar_mul(out=lgf, in0=lgf, scalar1=-1.0), r=['lg'], w=['lg'])
            for h in range(8):
                par, pr = h % 2, h // 2
                rows = slice(par * 64, par * 64 + 64)
                p.op('act', lambda a: a.activation(out=QF[rows, pr, :], in_=n1[rows, :], func=AF.Exp, scale=lg[rows, 0, h:h + 1]),
                     r=['lg', 'rc'], w=['tab'])
                p.op('act', lambda a: a.activation(out=QB[rows, pr, :], in_=cn[rows, :], func=AF.Exp, scale=lg[rows, 1, h:h + 1]),
                     r=['lg', 'rc'], w=['tab'])
                p.op('act', lambda a: a.activation(out=CDF[rows, pr, :], in_=c128[rows, :], func=AF.Exp, scale=lg[rows, 0, h:h + 1]),
                     r=['lg', 'rc'], w=['tab'])
                p.op('act', lambda a: a.activation(out=CDB[rows, pr, :], in_=c128[rows, :], func=AF.Exp, scale=lg[rows, 1, h:h + 1]),
                     r=['lg', 'rc'], w=['tab'])
                p.op('act', lambda a: a.activation(out=DM[:, h, :], in_=dpos, func=AF.Exp, scale=lg[:, 0, h:h + 1]),
                     r=['lg', 'rc'], w=[('DM', h)])
                p.op('dve', lambda v: v.tensor_tensor(out=DM[:, h, :], in0=DM[:, h, :], in1=mge, op=ALU.mult),
                     r=[('DM', h), 'rc'], w=[('DM', h)])
                p.op('act', lambda a: a.activation(out=tmpd[:], in_=dneg, func=AF.Exp, scale=lg[:, 1, h:h + 1]),
                     r=['lg', 'rc'], w=['tmpd'])
                p.op('dve', lambda v: v.tensor_tensor(out=tmpd[:], in0=tmpd[:], in1=mlt, op=ALU.mult),
                     r=['tmpd', 'rc'], w=['tmpd'])
                p.op('dve', lambda v: v.tensor_tensor(out=DM[:, h, :], in0=DM[:, h, :], in1=tmpd[:], op=ALU.add),
                     r=[('DM', h), 'tmpd'], w=[('DM', h)])
            p.op('act', lambda a: a.activation(out=KF[:], in_=lg[:, 0, :], func=AF.Exp, scale=posf), r=['lg', 'rc'], w=['tab'])
            p.op('act', lambda a: a.activation(out=KB[:], in_=lg[:, 1, :], func=AF.Exp, scale=posb), r=['lg', 'rc'], w=['tab'])

            PS = [sbB("PSf", [128, 32, 4, 64], BF16), sbB("PSb", [128, 32, 4, 64], BF16)]
            stt = [sbB("stf", [128, 4, 64], F32), sbB("stb", [128, 4, 64], F32)]
            kvin = [sbB(f"kvin{i}", [128, 1024], BF16) for i in range(3)]
            kd = [sbB(f"kd{i}", [128, 512], BF16) for i in range(2)]
            qk_sb = [sbB(f"qk_sb{i}", [128, 8, 512], BF16) for i in range(2)]
            sg_sb = [sbB(f"sg_sb{i}", [128, 4, 512], BF16) for i in range(2)]
            v_sb = [sbB(f"v_sb{i}", [128, 4, 512], BF16) for i in range(2)]
            qfd = [sbB(f"qfd{i}", [128, 4, 128], BF16) for i in range(2)]
            qbd = [sbB(f"qbd{i}", [128, 4, 128], BF16) for i in range(2)]
            qm = [sbB(f"qm{i}", [128, 2, 4, 128], BF16) for i in range(2)]
            ptb = [sbB(f"ptb{i}", [128, 512], BF16) for i in range(4)]
            gnf = [sbB(f"gnf{i}", [128, 512], F32) for i in range(2)]
            gnm = [sbB(f"gnm{i}", [128, 512], F32) for i in range(2)]
            gnb = [sbB(f"gnb{i}", [128, 512], BF16) for i in range(2)]
            gns = [sbB(f"gns{i}", [128, 512], BF16) for i in range(2)]
            gnr = [sbB(f"gnr{i}", [128, 512], F32) for i in range(2)]
            ystg = [sbB(f"ystg{i}", [128, 4, 512], BF16) for i in range(2)]
            cnts = {'kv': 0, 'kd': 0, 'blk': 0, 'qd': 0, 'pt': 0, 'gn': 0, 'sbk': 0, 'rb': 0}

            def rot(name, n):
                i = cnts[name] % n
                cnts[name] += 1
                return i
            st_in = [st_f_in, st_b_in]
            seqs = [(0, 32, None)] + [(4096 + 256 * b, 2, b) for b in range(4)]
            DBGB = int(os.environ.get("KDEBUGB", "9"))
            o_st = [o_rf, o_rb]
            KT = [KF, KB]
            CD = [CDF, CDB]

            def sbank():
                i = cnts['sbk'] % 4
                cnts['sbk'] += 1
                return p.banks[i], ('ps', i)

            def ret_states(T0, NCH, bidx, banks=None):
                for d in range(2):
                    st = stt[d]
                    if bidx is None:
                        for par in range(2):
                            p.dma('sp', st[par * 64:(par + 1) * 64, :, :],
                                  st_in[d].rearrange("(pr par) dk dv -> par dk pr dv", par=2)[par], w=[('st', d)])
                    else:
                        p.op('dve', lambda v: v.memset(st[:], 0.0), w=[('st', d)])
                    order = range(NCH) if d == 0 else range(NCH - 1, -1, -1)
                    for cc in order:
                        ki = rot('kv', 3)
                        p.dma('sp', kvin[ki][:], TM0[T0 + cc * 128:T0 + (cc + 1) * 128, 0:1024], w=[('kvin', ki)])
                        p.op('act', lambda a: a.copy(out=PS[d][:, cc, :, :], in_=st[:]), r=[('st', d)], w=[('PS', d, cc)])
                        di = rot('kd', 2)
                        p.op('dve', lambda v: v.tensor_tensor(
                            out=kd[di][:].rearrange("p (h e) -> p h e", h=8),
                            in0=kvin[ki][:, 0:512].rearrange("p (h e) -> p h e", h=8),
                            in1=KT[d][:].unsqueeze(2).to_broadcast([128, 8, 64]), op=ALU.mult),
                            r=[('kvin', ki), 'tab'], w=[('kd', di)])
                        if banks is None:
                            bk, bkey = sbank()
                        else:
                            bi_ = banks[rot('rb', 2) % len(banks)]
                            bk, bkey = p.banks[bi_], ('ps', bi_)
                        p.mmg([(lambda tt, h=h: tt.matmul(
                            bk[(h % 2) * 64:(h % 2) * 64 + 64, (h // 2) * 64:(h // 2) * 64 + 64],
                            lhsT=kd[di][:, h * 64:(h + 1) * 64], rhs=kvin[ki][:, 512 + h * 64:512 + (h + 1) * 64],
                            start=True, stop=True)) for h in range(8)], r=[('kd', di), ('kvin', ki)], w=[bkey])
                        p.op('dve', lambda v: v.tensor_tensor(out=st[:], in0=st[:], in1=CD[d][:], op=ALU.mult),
                             r=[('st', d), 'tab'], w=[('st', d)])
                        p.op('dve', lambda v: v.tensor_tensor(
                            out=st[:].rearrange("p a b -> p (a b)"), in0=bk[:, 0:256],
                            in1=st[:].rearrange("p a b -> p (a b)"), op=ALU.add), r=[bkey, ('st', d)], w=[('st', d)])
                        yield
                    if bidx is not None:
                        for par in range(2):
                            p.dma('sp', o_st[d][bidx].rearrange("(pr par) dk dv -> par dk pr dv", par=2)[par],
                                  st[par * 64:(par + 1) * 64, :, :], r=[('st', d)])

            def ret_out(T0, NCH):
                CPB = min(4, NCH)
                rfifo = []
                for blk in range(NCH // CPB):
                    N = CPB * 128
                    bi = rot('blk', 2)
                    c0 = T0 + blk * N
                    p.dma('sp', qk_sb[bi][:, :, 0:N], FM0[0:1024, c0:c0 + N].rearrange("(k p) n -> p k n", p=128),
                          w=[('qk', bi)])
                    p.dma('sp', sg_sb[bi][:, :, 0:N], FM0[1024:1536, c0:c0 + N].rearrange("(k p) n -> p k n", p=128),
                          w=[('sgs', bi)])
                    p.dma('sp', v_sb[bi][:, 0:CPB, :], TM0[c0:c0 + N, 512:1024].rearrange("(ci p) n -> p ci n", p=128),
                          w=[('vs', bi)])
                    OB = [(p.banks[4 + pr], ('ps', 4 + pr)) for pr in range(4)]
                    for ci in range(CPB):
                        cc = blk * CPB + ci
                        ccols = slice(ci * 128, (ci + 1) * 128)
                        qi = rot('qd', 2)
                        p.op('dve', lambda v: v.tensor_tensor(out=qfd[qi][:], in0=qk_sb[bi][:, 0:4, ccols], in1=QF[:], op=ALU.mult),
                             r=[('qk', bi), 'tab'], w=[('qfd', qi)])
                        p.op('dve', lambda v: v.tensor_tensor(out=qbd[qi][:], in0=qk_sb[bi][:, 0:4, ccols], in1=QB[:], op=ALU.mult),
                             r=[('qk', bi), 'tab'], w=[('qbd', qi)])
                        for par in range(2):
                            p.op('dve', lambda v, par=par: v.tensor_scalar_mul(
                                out=qm[qi][:, par, :, :], in0=qk_sb[bi][:, 0:4, ccols], scalar1=hv[:, 6 + par:7 + par]),
                                r=[('qk', bi), 'hv'], w=[('qm', qi)])
                        pts = []
                        for hb in range(2):
                            bk, bkey = sbank()
                            p.mmg([(lambda tt, hh=hh: tt.matmul(
                                bk[:, hh * 128:(hh + 1) * 128],
                                lhsT=qk_sb[bi][:, 4 + (hb * 4 + hh) // 2, ccols],
                                rhs=qm[qi][:, (hb * 4 + hh) % 2, (hb * 4 + hh) // 2, :],
                                start=True, stop=True)) for hh in range(4)], r=[('qk', bi), ('qm', qi)], w=[bkey])
                            pi = rot('pt', 4)
                            p.op('dve', lambda v, pi=pi, bk=bk: v.tensor_tensor(
                                out=ptb[pi][:], in0=bk[:], in1=DM[:, hb * 4:(hb + 1) * 4, :].rearrange("p a b -> p (a b)"),
                                op=ALU.mult), r=[bkey] + [('DM', hb * 4 + i) for i in range(4)], w=[('ptb', pi)])
                            pts.append(pi)
                        def stB(bi=bi, ci=ci, cc=cc, ccols=ccols, qi=qi, pts=pts, OB=OB):
                            for h in range(8 if DBGB >= 4 else 0):
                                par, pr = h % 2, h // 2
                                rows = slice(par * 64, par * 64 + 64)
                                ob, obk = OB[pr]
                                pi = pts[h // 4]
                                hh = h % 4
                                p.mmg([
                                    lambda tt: tt.matmul(ob[rows, ccols], lhsT=v_sb[bi][:, ci, h * 64:(h + 1) * 64],
                                                         rhs=ptb[pi][:, hh * 128:(hh + 1) * 128], start=True, stop=False),
                                    lambda tt: tt.matmul(ob[rows, ccols], lhsT=PS[0][rows, cc, pr, :], rhs=qfd[qi][rows, pr, :],
                                                         start=False, stop=False),
                                    lambda tt: tt.matmul(ob[rows, ccols], lhsT=PS[1][rows, cc, pr, :], rhs=qbd[qi][rows, pr, :],
                                                         start=False, stop=True)],
                                    r=[('vs', bi), ('ptb', pi), ('PS', 0, cc), ('PS', 1, cc), ('qfd', qi), ('qbd', qi)], w=[obk])
                        rfifo.append(stB)
                        if ci == CPB - 1:
                            def stGN(bi=bi, N=N, c0=c0, OB=OB):
                                yi = rot('gn', 2)
                                for pr in range(4 if DBGB >= 5 else 0):
                                    ob, obk = OB[pr]
                                    gi = pr % 2
                                    p.op('act', lambda a: a.copy(out=gnf[gi][:, 0:N], in_=ob[:, 0:N]), r=[obk], w=[('gnf', gi)])
                                    p.op('act', lambda a: a.copy(out=gnb[gi][:, 0:N], in_=gnf[gi][:, 0:N]), r=[('gnf', gi)], w=[('gnb', gi)])
                                    p.op('act', lambda a: a.activation(out=gns[gi][:, 0:N], in_=gnf[gi][:, 0:N], func=AF.Square),
                                         r=[('gnf', gi)], w=[('gns', gi)])
                                    bm, bmk = sbank()
                                    p.mmg([lambda tt: tt.matmul(bm[:, 0:N], lhsT=bones[:], rhs=gnb[gi][:, 0:N], start=True, stop=True)],
                                          r=[('gnb', gi), 'bones'], w=[bmk])
                                    bq, bqk = sbank()
                                    p.mmg([lambda tt: tt.matmul(bq[:, 0:N], lhsT=bones[:], rhs=gns[gi][:, 0:N], start=True, stop=True)],
                                          r=[('gns', gi), 'bones'], w=[bqk])
                                    p.op('act', lambda a: a.copy(out=gnm[gi][:, 0:N], in_=bm[:, 0:N]), r=[bmk], w=[('gnm', gi)])
                                    p.op('dve', lambda v: v.tensor_tensor(out=gnr[gi][:, 0:N], in0=gnm[gi][:, 0:N], in1=gnm[gi][:, 0:N],
                                                                          op=ALU.mult), r=[('gnm', gi)], w=[('gnr', gi)])
                                    p.op('dve', lambda v: v.tensor_tensor(out=gnr[gi][:, 0:N], in0=bq[:, 0:N], in1=gnr[gi][:, 0:N],
                                                                          op=ALU.subtract), r=[bqk, ('gnr', gi)], w=[('gnr', gi)])
                                    p.op('dve', lambda v: v.tensor_scalar_max(out=gnr[gi][:, 0:N], in0=gnr[gi][:, 0:N], scalar1=0.0),
                                         r=[('gnr', gi)], w=[('gnr', gi)])
                                    p.op('act', lambda a: a.activation(out=gnr[gi][:, 0:N], in_=gnr[gi][:, 0:N], func=AF.Ln, bias=eps5[:],
                                                                       scale=1.0), r=[('gnr', gi), 'eps5'], w=[('gnr', gi)])
                                    p.op('act', lambda a: a.activation(out=gnr[gi][:, 0:N], in_=gnr[gi][:, 0:N], func=AF.Exp, scale=-0.5),
                                         r=[('gnr', gi)], w=[('gnr', gi)])
                                    p.op('dve', lambda v: v.tensor_tensor(out=gnf[gi][:, 0:N], in0=gnf[gi][:, 0:N], in1=gnm[gi][:, 0:N],
                                                                          op=ALU.subtract), r=[('gnf', gi), ('gnm', gi)], w=[('gnf', gi)])
                                    p.op('dve', lambda v: v.tensor_tensor(out=gnf[gi][:, 0:N], in0=gnf[gi][:, 0:N], in1=gnr[gi][:, 0:N],
                                                                          op=ALU.mult), r=[('gnf', gi), ('gnr', gi)], w=[('gnf', gi)])
                                    p.op('dve', lambda v: v.scalar_tensor_tensor(
                                        out=ystg[yi][:, pr, 0:N], in0=gnf[gi][:, 0:N], scalar=hv[:, 2 + pr:3 + pr], in1=sg_sb[bi][:, pr, 0:N],
                                        op0=ALU.mult, op1=ALU.mult), r=[('gnf', gi), ('sgs', bi), 'hv'], w=[('ystg', yi)])
                                if DBGB >= 5:
                                    p.dma('sp', YC[0:512, c0:c0 + N].rearrange("(k p) n -> p k n", p=128), ystg[yi][:, :, 0:N],
                                          r=[('ystg', yi)])
                            rfifo.append(stGN)
                        while len(rfifo) > (2 if ci == CPB - 1 else 1):
                            rfifo.pop(0)()
                while rfifo:
                    rfifo.pop(0)()

            ptq = [sbB(f"ptq{i}", [128, 2, 512], BF16) for i in range(4)]
            q_sb = [sbB(f"q_sb{i}", [128, 4, 512], BF16) for i in range(2)]
            qmk = [sbB(f"qmk{i}", [128, 2, 4, 512], BF16) for i in range(1)]
            rec = [sbB(f"rec{i}", [128, 512], F32) for i in range(2)]
            yst = [sbB(f"yst{i}", [128, 4, 512], BF16) for i in range(2)]
            cnts.update({'aq': 0, 'ap': 0, 'ar': 0, 'ay': 0, 'as': 0, 'ao': 0})

            def attention(FMq, qchunk0, nheads, kT_fn, vaug_fn, kv_keys, NKT, T0, L, QN, ychunk0, npairs=3, bg=None):
                fifo = []
                LAG = 3

                def drain(n):
                    while len(fifo) > n:
                        fifo.pop(0)()
                for qt in range(L // QN):
                    c0 = T0 + qt * QN
                    for hg in range(nheads // 8):
                        qi = rot('aq', 2)
                        p.dma('sp', q_sb[qi][:, :, 0:QN],
                              FMq[(qchunk0 + hg * 4) * 128:(qchunk0 + hg * 4 + 4) * 128, c0:c0 + QN].rearrange("(k p) n -> p k n", p=128),
                              w=[('q_sb', qi)])
                        for par in range(2):
                            p.op('dve', lambda v, par=par: v.tensor_scalar_mul(
                                out=qmk[qi % len(qmk)][:, par, :, 0:QN], in0=q_sb[qi][:, :, 0:QN], scalar1=hv[:, 6 + par:7 + par]),
                                r=[('q_sb', qi), 'hv'], w=[('qmk', qi % len(qmk))])
                        yi = rot('ay', 2)
                        for hh in range(8):
                            h = hg * 8 + hh
                            if bg is not None:
                                next(bg, None)
                            ai = 6 + rot('ao', 2)
                            acc, acck = p.banks[ai], ('ps', ai)
                            for kp in range(NKT // 2):
                                b = rot('as', npairs)
                                bkeys = [('ps', 2 * b), ('ps', 2 * b + 1)]
                                p.mmg([(lambda tt, e=e: tt.matmul(p.banks[2 * b + e][:, 0:QN], lhsT=kT_fn(h, 2 * kp + e),
                                                                  rhs=qmk[qi % len(qmk)][:, h % 2, hh // 2, 0:QN], start=True, stop=True))
                                       for e in range(2)], r=[('qmk', qi % len(qmk))] + kv_keys, w=bkeys)
                                pi = rot('ap', 4)
                                p.op('act', lambda a: a.activation(out=ptq[pi][:, :, 0:QN], in_=psum_all[:, 2 * b:2 * b + 2, 0:QN],
                                                                   func=AF.Exp, scale=0.125), r=bkeys, w=[('ptq', pi)])

                                def pv(h=h, kp=kp, pi=pi, acc=acc, acck=acck):
                                    p.mmg([(lambda tt, e=e: tt.matmul(acc[:, 0:QN], lhsT=vaug_fn(h, 2 * kp + e), rhs=ptq[pi][:, e, 0:QN],
                                                                      start=(kp == 0 and e == 0), stop=(kp == NKT // 2 - 1 and e == 1)))
                                           for e in range(2)], r=[('ptq', pi)] + kv_keys, w=[acck])
                                fifo.append(pv)
                                drain(LAG)

                            def norm(h=h, hh=hh, acc=acc, acck=acck, yi=yi):
                                ri = rot('ar', 2)
                                p.op('dve', lambda v: v.reciprocal(out=rec[ri][0:64, 0:QN], in_=acc[64:128, 0:QN]), r=[acck],
                                     w=[('rec', ri)])
                                p.op('dve', lambda v: v.tensor_tensor(
                                    out=yst[yi][(h % 2) * 64:(h % 2) * 64 + 64, hh // 2, 0:QN], in0=acc[0:64, 0:QN],
                                    in1=rec[ri][0:64, 0:QN], op=ALU.mult), r=[acck, ('rec', ri)], w=[('yst', yi)])
                            fifo.append(norm)

                        def store(hg=hg, c0=c0, yi=yi):
                            p.dma('sp', YC[(ychunk0 + hg * 4) * 128:(ychunk0 + hg * 4 + 4) * 128, c0:c0 + QN].rearrange(
                                "(k p) n -> p k n", p=128), yst[yi][:, :, 0:QN], r=[('yst', yi)])
                        fifo.append(store)
                drain(0)

            kT2 = sbB("kT2", [128, 2, 4352], BF16)
            vaug = sbB("vaug", [128, 34, 2, 128], BF16)
            ck_f = sbB("ck_f", [128, 2, 128], F32)
            ck_b = sbB("ck_b", [128, 256], BF16)
            p.dma('sp', ck_f[:], cgk.rearrange("(g p) n -> p g n", p=128), w=['ck_f'])
            bk, bkey = sbank()
            p.mmg([(lambda tt, g=g: tt.transpose(bk[:, g * 128:(g + 1) * 128], ck_f[:, g, :], ident_f[:])) for g in range(2)],
                  r=['ck_f', 'ident_f'], w=[bkey])
            p.op('act', lambda a: a.copy(out=ck_b[:], in_=bk[:, 0:256]), r=[bkey], w=['ck_b'])
            p.dma('sp', FM0[16 * 128:17 * 128, NTOK:NTOK + 256], ck_b[:], r=['ck_b'], w=['FM0c'])

            def gqa(T0, L, sample, bg):
                NK = L + (256 if sample else 0)
                NKT = NK // 128
                for kvh in range(2):
                    for half in range(2):
                        src = FM0[16 * 128 + kvh * 64:16 * 128 + kvh * 64 + 64, :]
                        p.dma('sp', kT2[half * 64:half * 64 + 64, kvh, 0:L], src[:, T0:T0 + L], w=['kT2'])
                        if sample:
                            p.dma('sp', kT2[half * 64:half * 64 + 64, kvh, L:L + 256], src[:, NTOK:NTOK + 256],
                                  r=['FM0c'], w=['kT2'])
                p.op('dve', lambda v: v.memset(vaug[:], 1.0), w=['vaug'])
                for kvh in range(2):
                    p.dma('sp', vaug[:, 0:L // 128, kvh, 0:64],
                          TM0[T0:T0 + L, 1024 + kvh * 64:1088 + kvh * 64].rearrange("(kt p) e -> p kt e", p=128), w=['vaug'])
                    if sample:
                        p.dma('pool', vaug[:, L // 128:L // 128 + 2, kvh, 0:64],
                              cgv[:, kvh * 64:(kvh + 1) * 64].rearrange("(kt p) e -> p kt e", p=128), w=['vaug'])
                attention(FM0, 12, 8, lambda h, kt: kT2[:, h // 4, kt * 128:(kt + 1) * 128],
                          lambda h, kt: vaug[:, kt, h // 4, :], ['kT2', 'vaug'], NKT, T0, L, 512 if sample else 256, 4,
                          npairs=(2 if bg is not None else 3), bg=bg)

            mwB = [sbB(f"mwB{i}", [128, 8, 256], BF16) for i in range(2)]

            def background():
                g1 = ret_states(0, 32, None, banks=[4])
                g2 = mod_layer(1, mwB, 256, p.banks[5], ('ps', 5), 'mwB')
                d1 = d2 = False
                while not (d1 and d2):
                    if not d1:
                        try:
                            next(g1)
                        except StopIteration:
                            d1 = True
                    if not d2:
                        try:
                            next(g2)
                        except StopIteration:
                            d2 = True
                    yield
            bg = background()
            gqa(0, 4096, True, bg)
            for _ in bg:
                pass
            for (T0, NCH, bidx) in seqs[1:]:
                gqa(T0, NCH * 128, False, None)
            ret_out(0, 32)
            for (T0, NCH, bidx) in seqs[1:]:
                for _ in ret_states(T0, NCH, bidx):
                    pass
                ret_out(T0, NCH)
            p.barrier()
        tile_scope('C')
        with ExitStack() as esD:
            sbD = lambda name, shape, dtype: esD.enter_context(nc.sbuf_tensor(name + '_D', shape, dtype))
            cnts = {'aq': 0, 'ap': 0, 'ar': 0, 'ay': 0, 'as': 0, 'ao': 0}

            def rot(name, n):
                i = cnts[name] % n
                cnts[name] += 1
                return i
            ptq = [sbD(f"ptq{i}", [128, 2, 512], BF16) for i in range(4)]
            q_sb = [sbD(f"q_sb{i}", [128, 4, 512], BF16) for i in range(2)]
            qmk = [sbD(f"qmk{i}", [128, 2, 4, 512], BF16) for i in range(2)]
            rec = [sbD(f"rec{i}", [128, 512], F32) for i in range(2)]
            yst = [sbD(f"yst{i}", [128, 4, 512], BF16) for i in range(2)]
            knT = sbD("knT", [128, 8, 256], BF16)
            vaugN = sbD("vaugN", [128, 2, 16, 128], BF16)
            def attention(FMq, qchunk0, nheads, kT_fn, vaug_fn, kv_keys, NKT, T0, L, QN, ychunk0, npairs=3, bg=None):
                fifo = []
                LAG = 3

                def drain(n):
                    while len(fifo) > n:
                        fifo.pop(0)()
                for qt in range(L // QN):
                    c0 = T0 + qt * QN
                    for hg in range(nheads // 8):
                        qi = rot('aq', 2)
                        p.dma('sp', q_sb[qi][:, :, 0:QN],
                              FMq[(qchunk0 + hg * 4) * 128:(qchunk0 + hg * 4 + 4) * 128, c0:c0 + QN].rearrange("(k p) n -> p k n", p=128),
                              w=[('q_sb', qi)])
                        for par in range(2):
                            p.op('dve', lambda v, par=par: v.tensor_scalar_mul(
                                out=qmk[qi % len(qmk)][:, par, :, 0:QN], in0=q_sb[qi][:, :, 0:QN], scalar1=hv[:, 6 + par:7 + par]),
                                r=[('q_sb', qi), 'hv'], w=[('qmk', qi % len(qmk))])
                        yi = rot('ay', 2)
                        for hh in range(8):
                            h = hg * 8 + hh
                            if bg is not None:
                                next(bg, None)
                            ai = 6 + rot('ao', 2)
                            acc, acck = p.banks[ai], ('ps', ai)
                            for kp in range(NKT // 2):
                                b = rot('as', npairs)
                                bkeys = [('ps', 2 * b), ('ps', 2 * b + 1)]
                                p.mmg([(lambda tt, e=e: tt.matmul(p.banks[2 * b + e][:, 0:QN], lhsT=kT_fn(h, 2 * kp + e),
                                                                  rhs=qmk[qi % len(qmk)][:, h % 2, hh // 2, 0:QN], start=True, stop=True))
                                       for e in range(2)], r=[('qmk', qi % len(qmk))] + kv_keys, w=bkeys)
                                pi = rot('ap', 4)
                                p.op('act', lambda a: a.activation(out=ptq[pi][:, :, 0:QN], in_=psum_all[:, 2 * b:2 * b + 2, 0:QN],
                                                                   func=AF.Exp, scale=0.125), r=bkeys, w=[('ptq', pi)])

                                def pv(h=h, kp=kp, pi=pi, acc=acc, acck=acck):
                                    p.mmg([(lambda tt, e=e: tt.matmul(acc[:, 0:QN], lhsT=vaug_fn(h, 2 * kp + e), rhs=ptq[pi][:, e, 0:QN],
                                                                      start=(kp == 0 and e == 0), stop=(kp == NKT // 2 - 1 and e == 1)))
                                           for e in range(2)], r=[('ptq', pi)] + kv_keys, w=[acck])
                                fifo.append(pv)
                                drain(LAG)

                            def norm(h=h, hh=hh, acc=acc, acck=acck, yi=yi):
                                ri = rot('ar', 2)
                                p.op('dve', lambda v: v.reciprocal(out=rec[ri][0:64, 0:QN], in_=acc[64:128, 0:QN]), r=[acck],
                                     w=[('rec', ri)])
                                p.op('dve', lambda v: v.tensor_tensor(
                                    out=yst[yi][(h % 2) * 64:(h % 2) * 64 + 64, hh // 2, 0:QN], in0=acc[0:64, 0:QN],
                                    in1=rec[ri][0:64, 0:QN], op=ALU.mult), r=[acck, ('rec', ri)], w=[('yst', yi)])
                            fifo.append(norm)

                        def store(hg=hg, c0=c0, yi=yi):
                            p.dma('sp', YC[(ychunk0 + hg * 4) * 128:(ychunk0 + hg * 4 + 4) * 128, c0:c0 + QN].rearrange(
                                "(k p) n -> p k n", p=128), yst[yi][:, :, 0:QN], r=[('yst', yi)])
                        fifo.append(store)
                drain(0)


            def na_prompt(T0):
                p.dma('sp', knT[:], FM1[8 * 128:16 * 128, T0:T0 + 256].rearrange("(k p) n -> p k n", p=128), w=['knT'])
                p.dma('sp', vaugN[:].rearrange("p kt h e -> p kt (h e)"),
                      TM1[T0:T0 + 256, :, :].rearrange("(kt p) h e -> p kt (h e)", p=128), w=['vaugN'])
                attention(FM1, 0, 16, lambda h, kt: knT[:, h // 2, kt * 128:(kt + 1) * 128],
                          lambda h, kt: vaugN[:, kt, h, :], ['knT', 'vaugN'], 2, T0, 256, 256, 0)
            ident_b = sbD("ident_b", [128, 128], BF16)
            p.dma('pool', ident_b[:], identb, w=['ident_b'])
            nm = sbD("nm", [128, 2, 64], F32)
            p.dma('sp', nm[:], nmask, w=['nm'])
            BT = sbD("BT", [128, 16, 14, 64], BF16)
            graw = [sbD(f"graw{i}", [128, 14, 64], F32) for i in range(2)]
            gtmp = [sbD(f"gtmp{i}", [128, 14, 64], F32) for i in range(2)]
            def build_bt(h):
                gi = h % 2
                base = 64 + h * 465 - 48
                for half in range(2):
                    p.dma('sp', graw[gi][half * 64:(half + 1) * 64, :, :],
                          bass.AP(rpb_pad.tensor, base + half * 31, [[1, 64], [31, 14], [1, 64]]), w=[('graw', gi)])
                grev = bass.AP(graw[gi], 63, [[14 * 64, 128], [64, 14], [-1, 64]])
                p.op('dve', lambda v: v.tensor_tensor(out=gtmp[gi][:], in0=grev,
                                                      in1=nm[:, 0:1, :].to_broadcast([128, 14, 64]), op=ALU.mult),
                     r=[('graw', gi), 'nm'], w=[('gtmp', gi)])
                p.op('dve', lambda v: v.scalar_tensor_tensor(out=BT[:, h, :, :], in0=gtmp[gi][:], scalar=8.0,
                                                             in1=nm[:, 1:2, :].to_broadcast([128, 14, 64]),
                                                             op0=ALU.mult, op1=ALU.add),
                     r=[('gtmp', gi), 'nm'], w=['BT'])
            ctxK = sbD("ctxK", [128, 8, 256], BF16)
            ctxV = sbD("ctxV", [128, 2, 16, 128], BF16)
            ckf = sbD("ckf", [128, 2, 1024], F32)
            p.dma('sp', ckf[:], cnk.rearrange("(g p) n -> p g n", p=128), w=['ckf'])
            for k in range(8):
                si = rot('as', 4)
                bk, bkey = p.banks[si], ('ps', si)
                p.mmg([(lambda tt, g=g: tt.transpose(bk[:, g * 128:(g + 1) * 128], ckf[:, g, k * 128:(k + 1) * 128], ident_f[:]))
                       for g in range(2)], r=['ckf', 'ident_f'], w=[bkey])
                p.op('act', lambda a: a.copy(out=ctxK[:, k, :], in_=bk[:, 0:256]), r=[bkey], w=['ctxK'])
            p.op('dve', lambda v: v.memset(ctxV[:], 1.0), w=['ctxV'])
            for kt in range(2):
                p.dma('pool', ctxV[:, kt, :, 0:64], cnv[kt * 128:(kt + 1) * 128, :].rearrange("p (h e) -> p h e", h=16),
                      w=['ctxV'])
            for b in range(4):
                na_prompt(4096 + 256 * b)
                for h in range(4 * b, 4 * b + 4):
                    build_bt(h)
            kwin = [sbD(f"kwin{i}", [128, 8, 512], BF16) for i in range(2)]
            vwin = [sbD(f"vwin{i}", [128, 4, 16, 128], BF16) for i in range(2)]
            qrow = [sbD(f"qrow{i}", [128, 8, 64], BF16) for i in range(2)]
            qrm = [sbD(f"qrm{i}", [128, 2, 8, 64], BF16) for i in range(2)]
            ptn = [sbD(f"ptn{i}", [128, 768], BF16) for i in range(4)]
            recN = [sbD(f"recN{i}", [128, 512], F32) for i in range(2)]
            yblk = [sbD(f"yblk{i}", [128, 8, 512], BF16) for i in range(2)]
            cnts.update({'pn': 0})
            nfifo = []

            def ndrain(n):
                while len(nfifo) > n:
                    nfifo.pop(0)()
            for r in range(64):
                r0 = min(max(r - 4, 0), 56)
                dl = r - r0
                wi = r % 2
                p.dma('sp', kwin[wi][:], FM1[8 * 128:16 * 128, r0 * 64:r0 * 64 + 512].rearrange("(k p) n -> p k n", p=128),
                      w=[('kwin', wi)])
                p.dma('sp', vwin[wi][:].rearrange("p kt h e -> p kt (h e)"),
                      TM1[r0 * 64:r0 * 64 + 512, :, :].rearrange("(kt p) h e -> p kt (h e)", p=128), w=[('vwin', wi)])
                p.dma('sp', qrow[wi][:], FM1[0:8 * 128, r * 64:(r + 1) * 64].rearrange("(k p) n -> p k n", p=128),
                      w=[('qrow', wi)])
                for par in range(2):
                    p.op('dve', lambda v, par=par: v.tensor_scalar_mul(out=qrm[wi][:, par, :, :], in0=qrow[wi][:],
                                                                        scalar1=hv[:, 6 + par:7 + par]),
                         r=[('qrow', wi), 'hv'], w=[('qrm', wi)])
                accs = [(p.banks[4 + 2 * wi + par], ('ps', 4 + 2 * wi + par)) for par in range(2)]
                for m in range(8):
                    si = rot('as', 2)
                    sb1, sb1k = p.banks[2 * si], ('ps', 2 * si)
                    sb2, sb2k = p.banks[2 * si + 1], ('ps', 2 * si + 1)
                    qpair = qrm[wi][:, :, m, :]
                    fns = []
                    for i in range(4):
                        j = 2 * i - dl + 7
                        fns.append(lambda tt, i=i: tt.matmul(sb1[:, i * 128:(i + 1) * 128], lhsT=kwin[wi][:, m, i * 128:(i + 1) * 128],
                                                             rhs=qpair, start=True, stop=False))
                        fns.append(lambda tt, i=i, j=j: tt.matmul(sb1[:, i * 128:(i + 1) * 128], lhsT=ident_b[:],
                                                                  rhs=BT[:, 2 * m:2 * m + 2, j, :], start=False, stop=True))
                    p.mmg(fns, r=[('kwin', wi), ('qrm', wi), 'BT', 'ident_b'], w=[sb1k])
                    p.mmg([(lambda tt, kt=kt: tt.matmul(sb2[:, kt * 128:(kt + 1) * 128], lhsT=ctxK[:, m, kt * 128:(kt + 1) * 128],
                                                        rhs=qpair, start=True, stop=True)) for kt in range(2)],
                          r=[('qrm', wi), 'ctxK'], w=[sb2k])
                    pi = rot('pn', 4)
                    p.op('act', lambda a: a.activation(out=ptn[pi][:, 0:512], in_=sb1[:], func=AF.Exp, scale=0.125),
                         r=[sb1k], w=[('ptn', pi)])
                    p.op('act', lambda a: a.activation(out=ptn[pi][:, 512:768], in_=sb2[:, 0:256], func=AF.Exp, scale=0.125),
                         r=[sb2k], w=[('ptn', pi)])

                    def pv(m=m, pi=pi, wi=wi, accs=accs):
                        for par in range(2):
                            h = 2 * m + par
                            acc, acck = accs[par]
                            fns = []
                            for i in range(4):
                                fns.append(lambda tt, i=i: tt.matmul(acc[:, m * 64:(m + 1) * 64], lhsT=vwin[wi][:, i, h, :],
                                                                     rhs=ptn[pi][:, i * 128 + par * 64:i * 128 + par * 64 + 64],
                                                                     start=(i == 0), stop=False))
                            for kt in range(2):
                                fns.append(lambda tt, kt=kt: tt.matmul(acc[:, m * 64:(m + 1) * 64], lhsT=ctxV[:, kt, h, :],
                                                                       rhs=ptn[pi][:, 512 + kt * 128 + par * 64:512 + kt * 128 + par * 64 + 64],
                                                                       start=False, stop=(kt == 1)))
                            p.mmg(fns, r=[('vwin', wi), ('ptn', pi), 'ctxV'], w=[acck])
                    nfifo.append(pv)
                    ndrain(2)

                def rownorm(r=r, accs=accs):
                    yi = (r // 8) % 2
                    for par in range(2):
                        acc, acck = accs[par]
                        ri = rot('ar', 2)
                        p.op('dve', lambda v: v.reciprocal(out=recN[ri][0:64, :], in_=acc[64:128, :]), r=[acck], w=[('recN', ri)])
                        p.op('dve', lambda v: v.tensor_tensor(
                            out=yblk[yi][par * 64:(par + 1) * 64, :, (r % 8) * 64:(r % 8 + 1) * 64],
                            in0=acc[0:64, :].rearrange("p (m q) -> p m q", m=8),
                            in1=recN[ri][0:64, :].rearrange("p (m q) -> p m q", m=8),
                            op=ALU.mult), r=[acck, ('recN', ri)], w=[('yblk', yi)])
                    if r % 8 == 7:
                        c0 = (r - 7) * 64
                        p.dma('sp', YC[:, c0:c0 + 512].rearrange("(k p) n -> p k n", p=128), yblk[yi][:], r=[('yblk', yi)])
                nfifo.append(rownorm)
            ndrain(0)
            p.barrier()
        tile_scope('E')
    return nc


_NC = None


def _consts():
    f32 = np.float32
    t = np.arange(4096)
    rows, cols = t // 64, t % 64
    inv = (10000.0 ** (-np.arange(16, dtype=np.float64) / 16))
    cosT = np.zeros((128, 4096), f32)
    sinT = np.zeros((128, 4096), f32)
    for pp in range(128):
        i = pp % 64
        pos = rows if i < 32 else cols
        ang = (pos.astype(np.float32)[:, None] * inv.astype(np.float32)[None, :])[:, i % 16].astype(np.float32)
        cosT[pp] = np.cos(ang)
        sinT[pp] = np.sin(ang)
    ropeT = np.ascontiguousarray(np.stack([cosT, sinT], 1))
    pm = np.zeros((128, 128), f32)
    for base in range(0, 128, 32):
        for j in range(16):
            pm[base + 16 + j, base + j] = -1.0
            pm[base + j, base + 16 + j] = 1.0
    m = np.arange(128)[:, None].astype(f32)
    n = np.arange(128)[None, :].astype(f32)
    rc = np.concatenate([np.broadcast_to(n + 1, (128, 128)), np.broadcast_to(128 - n, (128, 128)),
                         np.maximum(n - m, 0), np.maximum(m - n, 0), (n >= m).astype(f32), (m > n).astype(f32),
                         127 - m, m, np.full((128, 64), 128.0, f32)], 1).astype(f32)
    cc = np.arange(64)
    c0 = np.clip(cc - 8, 0, 48)
    kc = np.arange(64)[:, None]
    m01 = ((kc >= c0[None, :]) & (kc < c0[None, :] + 16)).astype(f32)
    m01 = np.concatenate([m01, m01], 0)
    nmask = np.ascontiguousarray(np.stack([m01, (m01 - 1.0) * 240000.0], 1).astype(f32))
    return ropeT, pm, np.ascontiguousarray(rc), nmask


def kernel(**inp):
    global _NC
    f32 = np.float32
    xs, xp = np.asarray(inp['x_sample'], f32), np.asarray(inp['x_prompt'], f32)
    c, c_ctx = np.asarray(inp['c'], f32), np.asarray(inp['c_ctx'], f32)
    vec12 = np.concatenate([np.asarray(inp['norm_pre'], f32).reshape(6, D), np.asarray(inp['norm_post'], f32).reshape(6, D)], 0)
    vecs = np.ascontiguousarray(vec12.reshape(12, 8, 128).transpose(2, 0, 1))
    mod_bT = np.ascontiguousarray(np.asarray(inp['mod_b'], f32).reshape(2, 72, 128).transpose(2, 0, 1))
    ropeT, pm, rc, nmask = _consts()
    rpb_pad = np.pad(np.asarray(inp['na_rpb'], f32)[0].reshape(-1), (64, 64))
    hvec = np.zeros((128, 8), f32)
    hvec[:, 0] = np.tile(np.asarray(inp['gqa_q_norm'], f32)[0], 2)
    hvec[:, 1] = np.tile(np.asarray(inp['gqa_k_norm'], f32)[0], 2)
    hvec[:, 2:6] = np.asarray(inp['ret_gn'], f32)[0].reshape(4, 128).T
    hvec[0:64, 6] = 1.0
    hvec[64:128, 7] = 1.0
    dec_in = np.ascontiguousarray(np.stack([np.asarray(inp['ret_decay_fwd'], f32)[0], np.asarray(inp['ret_decay_bwd'], f32)[0]], 0))
    shared = dict(mod_w=np.asarray(inp['mod_w'], f32), mod_bT=mod_bT, vecs=vecs,
                  ffn_w_in=np.asarray(inp['ffn_w_in'], f32), ffn_w_out=np.asarray(inp['ffn_w_out'], f32),
                  identf=np.eye(128, dtype=f32), ab_w_in=np.asarray(inp['ab_w_in'], f32)[0], hvec=hvec, ropeT=ropeT,
                  pmat=pm, rc=rc, dec_in=dec_in,
                  ab_w_out=np.asarray(inp['ab_w_out'], f32)[0], na_w_qkv=np.asarray(inp['na_w_qkv'], f32)[0],
                  na_w_out=np.asarray(inp['na_w_out'], f32)[0], rpb_pad=rpb_pad, nmask=nmask,
                  identb=np.eye(128, dtype=f32))
    in_maps = []
    for i in range(8):
        x = np.concatenate([xs[i], xp[4 * i:4 * i + 4].reshape(1024, D)], 0)
        condT = np.ascontiguousarray(np.stack([c[i], c_ctx], 1))
        in_maps.append(dict(x=np.ascontiguousarray(x), condT=condT,
                            st_f=np.ascontiguousarray(np.asarray(inp['state_ret_fwd'], f32)[i, 0]),
                            st_b=np.ascontiguousarray(np.asarray(inp['state_ret_bwd'], f32)[i, 0]),
                            cgk=np.ascontiguousarray(np.asarray(inp['cache_gqa_k'], f32)[i, 0].reshape(256, 128)),
                            cgv=np.ascontiguousarray(np.asarray(inp['cache_gqa_v'], f32)[i, 0].reshape(256, 128)),
                            cnk=np.ascontiguousarray(np.asarray(inp['cache_na_k'], f32)[i, 0].reshape(256, 1024)),
                            cnv=np.ascontiguousarray(np.asarray(inp['cache_na_v'], f32)[i, 0].reshape(256, 1024)),
                            **shared))
    if _NC is None:
        _NC = build_program()
    res = run_bass_kernel_spmd(_NC, in_maps, core_ids=list(range(8)))
    R = res.results
    ys = np.stack([r["y"][:4096] for r in R], 0)
    yp = np.concatenate([r["y"][4096:].reshape(4, 256, D) for r in R], 0)
    gk = np.concatenate([r["o_gk"].reshape(4, 1, 256, 2, 64) for r in R], 0)
    gv = np.concatenate([r["o_gv"].reshape(4, 1, 256, 2, 64) for r in R], 0)
    rf = np.concatenate([r["o_rf"].reshape(4, 1, 8, 64, 64) for r in R], 0)
    rb = np.concatenate([r["o_rb"].reshape(4, 1, 8, 64, 64) for r in R], 0)
    nk = np.concatenate([r["o_nk"].reshape(4, 1, 256, 16, 64) for r in R], 0)
    nv = np.concatenate([r["o_nv"].reshape(4, 1, 256, 16, 64) for r in R], 0)
    return yp, ys, gk, gv, rf, rb, nk, nv
```
